# Optimizing a Trainium2 kernel written in Bass

```python
import math
import numpy as np
import jax
import jax.numpy as jnp
from jax import lax

D_MODEL = 1024
BATCH = 8
SEQ = 4096
DEPTH = 2

GRID_W = 64
CTX_LEN = 256
HEAD_DIM = 64
D_CONV = 512
CONV_K = 31
NA_HEADS = 8
NA_MAX_ROWS = 8
NA_COLS = 16
WA_HEADS = 8
WA_KV_HEADS = 2
WA_WINDOW = 128
WA_BLOCK = 128
ROPE_BASE = 10000.0
EPS = 1e-6
D_NA = NA_HEADS * HEAD_DIM
D_WA = WA_HEADS * HEAD_DIM
D_WA_KV = WA_KV_HEADS * HEAD_DIM
SPLIT_SIZES = (2 * D_CONV, D_CONV, D_NA, D_NA, D_NA, D_NA, D_WA, D_WA_KV, D_WA_KV, D_WA, 3 * D_MODEL)
D_IN = sum(SPLIT_SIZES)
SPLIT_POINTS = tuple(int(v) for v in np.cumsum(SPLIT_SIZES)[:-1])
SPLIT_STARTS = (0,) + SPLIT_POINTS

kernel_name = 'hybrid_conv_natten_swa_prefix_dit_block'


def rms_norm(x, g):
    xf = x.astype(jnp.float32)
    y = xf * lax.rsqrt(jnp.mean(xf * xf, axis=-1, keepdims=True) + EPS)
    return (y * g.astype(jnp.float32)).astype(x.dtype)


def layer_norm(x, g, b):
    xf = x.astype(jnp.float32)
    mu = jnp.mean(xf, axis=-1, keepdims=True)
    var = jnp.mean(jnp.square(xf - mu), axis=-1, keepdims=True)
    y = (xf - mu) * lax.rsqrt(var + EPS)
    return (y * g.astype(jnp.float32) + b.astype(jnp.float32)).astype(x.dtype)


def depthwise_conv(x, w, b):
    y = lax.conv_general_dilated(
        x, w[:, None, :].astype(x.dtype), window_strides=(1,),
        padding=[(CONV_K // 2, CONV_K // 2)],
        dimension_numbers=('NWC', 'WIO', 'NWC'),
        feature_group_count=x.shape[-1])
    return y + b


def conv_branch(u_glu, u_gate, conv_w, conv_b, ln_g, ln_b, w_proj):
    a = u_glu[..., :D_CONV] * jax.nn.sigmoid(u_glu[..., D_CONV:])
    a = depthwise_conv(a, conv_w, conv_b)
    a = jax.nn.silu(layer_norm(a, ln_g, ln_b))
    return (a * jax.nn.silu(u_gate)) @ w_proj


def split_heads(t, n):
    return t.reshape(t.shape[0], t.shape[1], n, HEAD_DIM)


def rotate(x, pos):
    half = x.shape[-1] // 2
    inv = ROPE_BASE ** (-jnp.arange(half, dtype=jnp.float32) / half)
    ang = pos.astype(jnp.float32)[:, None] * inv[None, :]
    cos = jnp.cos(ang)[None, :, None, :]
    sin = jnp.sin(ang)[None, :, None, :]
    x1 = x[..., :half].astype(jnp.float32)
    x2 = x[..., half:].astype(jnp.float32)
    return jnp.concatenate([x1 * cos - x2 * sin, x1 * sin + x2 * cos], axis=-1).astype(x.dtype)


def axial_rope(t, row_pos, col_pos):
    half = HEAD_DIM // 2
    return jnp.concatenate([rotate(t[..., :half], row_pos), rotate(t[..., half:], col_pos)], axis=-1)


def sink_softmax(s, sink):
    m = jnp.maximum(jnp.max(s, axis=-1, keepdims=True), sink)
    e = jnp.exp(s - m)
    return e / (jnp.sum(e, axis=-1, keepdims=True) + jnp.exp(sink - m))


def context_attention(q, k, v, sink):
    s = jnp.einsum('bqkgd,bnkd->bkgqn', q, k).astype(jnp.float32) * HEAD_DIM ** -0.5
    if sink is None:
        p = jax.nn.softmax(s, axis=-1)
    else:
        p = sink_softmax(s, sink.astype(jnp.float32)[None, :, :, None, None])
    o = jnp.einsum('bkgqn,bnkd->bqkgd', p.astype(v.dtype), v)
    return o.reshape(o.shape[0], o.shape[1], -1)


def neighbourhood_attention(q, k, v, k_ctx, v_ctx, rpb):
    B, S = q.shape[0], q.shape[1]
    rows = S // GRID_W
    kr = min(NA_MAX_ROWS, rows)
    scale = HEAD_DIM ** -0.5
    grid = (B, rows, GRID_W, NA_HEADS, HEAD_DIM)
    qg, kg, vg = q.reshape(grid), k.reshape(grid), v.reshape(grid)
    col_start = np.clip(np.arange(GRID_W) - NA_COLS // 2, 0, GRID_W - NA_COLS)
    col_idx = col_start[:, None] + np.arange(NA_COLS)[None, :]
    col_off = col_idx - np.arange(GRID_W)[:, None] + (NA_COLS - 1)
    rpb_cols = rpb[:, :, col_off]

    def row_fn(i):
        rs = jnp.clip(i - kr // 2, 0, rows - kr)
        q_i = lax.dynamic_index_in_dim(qg, i, axis=1, keepdims=False)
        k_w = lax.dynamic_slice_in_dim(kg, rs, kr, axis=1)[:, :, col_idx]
        v_w = lax.dynamic_slice_in_dim(vg, rs, kr, axis=1)[:, :, col_idx]
        row_off = rs + jnp.arange(kr) - i + (NA_MAX_ROWS - 1)
        bias = jnp.transpose(rpb_cols[:, row_off], (0, 2, 1, 3)).astype(jnp.float32)
        s_w = jnp.einsum('bjhd,brjchd->bhjrc', q_i, k_w).astype(jnp.float32) * scale + bias[None]
        s_c = jnp.einsum('bjhd,bnhd->bhjn', q_i, k_ctx).astype(jnp.float32) * scale
        s = jnp.concatenate([s_w.reshape(B, NA_HEADS, GRID_W, kr * NA_COLS), s_c], axis=-1)
        p = jax.nn.softmax(s, axis=-1).astype(v.dtype)
        p_w = p[..., :kr * NA_COLS].reshape(B, NA_HEADS, GRID_W, kr, NA_COLS)
        o = (jnp.einsum('bhjrc,brjchd->bjhd', p_w, v_w)
             + jnp.einsum('bhjn,bnhd->bjhd', p[..., kr * NA_COLS:], v_ctx))
        return o

    out = lax.map(row_fn, jnp.arange(rows))
    return jnp.moveaxis(out, 0, 1).reshape(B, S, NA_HEADS * HEAD_DIM)


def window_attention(q, k, v, k_ctx, v_ctx, sink):
    B, S = q.shape[0], q.shape[1]
    G = WA_HEADS // WA_KV_HEADS
    nb = S // WA_BLOCK
    span = WA_BLOCK + 2 * WA_WINDOW
    scale = HEAD_DIM ** -0.5
    qb = q.reshape(B, nb, WA_BLOCK, WA_KV_HEADS, G, HEAD_DIM)
    pad = ((0, 0), (WA_WINDOW, WA_WINDOW), (0, 0), (0, 0))
    kp = jnp.pad(k, pad)
    vp = jnp.pad(v, pad)
    rel = (jnp.arange(span) - WA_WINDOW)[None, :] - jnp.arange(WA_BLOCK)[:, None]
    in_window = jnp.abs(rel) <= WA_WINDOW
    sink_b = sink.astype(jnp.float32).reshape(1, WA_KV_HEADS, G, 1, 1)

    def block_fn(n):
        start = n * WA_BLOCK
        q_n = lax.dynamic_index_in_dim(qb, n, axis=1, keepdims=False)
        k_n = lax.dynamic_slice_in_dim(kp, start, span, axis=1)
        v_n = lax.dynamic_slice_in_dim(vp, start, span, axis=1)
        kpos = start - WA_WINDOW + jnp.arange(span)
        valid = in_window & ((kpos >= 0) & (kpos < S))[None, :]
        s_w = jnp.einsum('bqkgd,bskd->bkgqs', q_n, k_n).astype(jnp.float32) * scale
        s_w = jnp.where(valid, s_w, -jnp.inf)
        s_c = jnp.einsum('bqkgd,bnkd->bkgqn', q_n, k_ctx).astype(jnp.float32) * scale
        p = sink_softmax(jnp.concatenate([s_w, s_c], axis=-1), sink_b).astype(v.dtype)
        o = (jnp.einsum('bkgqs,bskd->bqkgd', p[..., :span], v_n)
             + jnp.einsum('bkgqn,bnkd->bqkgd', p[..., span:], v_ctx))
        return o

    out = lax.map(block_fn, jnp.arange(nb))
    return jnp.moveaxis(out, 0, 1).reshape(B, S, WA_HEADS * HEAD_DIM)


def merge_branches(a_glu, a_gate, o_na, n_gate, o_wa, w_gate, merge,
                   conv_w, conv_b, cln_g, cln_b, w_proj_a, w_proj_b, w_proj_c, w_o):
    y_a = conv_branch(a_glu, a_gate, conv_w, conv_b, cln_g, cln_b, w_proj_a)
    y_b = (o_na * jax.nn.silu(n_gate)) @ w_proj_b
    y_c = (o_wa * jax.nn.silu(w_gate)) @ w_proj_c
    g_a, g_b, g_c = jnp.split(merge, 3, axis=-1)
    y = jax.nn.sigmoid(g_a) * y_a + jax.nn.sigmoid(g_b) * y_b + jax.nn.sigmoid(g_c) * y_c
    return y @ w_o


def context_kv(hc, w_in_l):
    st = SPLIT_STARTS
    nk, nv = jnp.split(hc @ w_in_l[:, st[3]:st[5]], 2, axis=-1)
    wk, wv = jnp.split(hc @ w_in_l[:, st[7]:st[9]], 2, axis=-1)
    return nk, nv, wk, wv


def setup_inputs(seed: int = 0) -> dict:
    key = jax.random.key(seed)
    ks = jax.random.split(key, 24)
    L = DEPTH

    def nrm(k, shape, s):
        return jax.random.normal(k, shape, jnp.float32) * s

    return {
        'x': nrm(ks[0], (BATCH, SEQ, D_MODEL), 1.0),
        'c': nrm(ks[1], (BATCH, D_MODEL), 1.0),
        'ctx': nrm(ks[2], (BATCH, CTX_LEN, D_MODEL), 1.0),
        'c_ctx': nrm(ks[3], (D_MODEL,), 1.0),
        'norm_g': 1.0 + nrm(ks[4], (L, D_MODEL), 0.02),
        'w_ada': nrm(ks[5], (L, D_MODEL, 3 * D_MODEL), 0.5 * D_MODEL ** -0.5),
        'b_ada': nrm(ks[6], (L, 3 * D_MODEL), 0.02),
        'w_in': nrm(ks[7], (L, D_MODEL, D_IN), D_MODEL ** -0.5),
        'conv_w': nrm(ks[8], (L, CONV_K, D_CONV), CONV_K ** -0.5),
        'conv_b': nrm(ks[9], (L, D_CONV), 0.02),
        'cln_g': 1.0 + nrm(ks[10], (L, D_CONV), 0.02),
        'cln_b': nrm(ks[11], (L, D_CONV), 0.02),
        'w_proj_a': nrm(ks[12], (L, D_CONV, D_MODEL), D_CONV ** -0.5),
        'na_q_norm': 1.0 + nrm(ks[13], (L, HEAD_DIM), 0.02),
        'na_k_norm': 1.0 + nrm(ks[14], (L, HEAD_DIM), 0.02),
        'na_rpb': nrm(ks[15], (L, NA_HEADS, 2 * NA_MAX_ROWS - 1, 2 * NA_COLS - 1), 0.1),
        'w_proj_b': nrm(ks[16], (L, D_NA, D_MODEL), D_NA ** -0.5),
        'wa_q_norm': 1.0 + nrm(ks[17], (L, HEAD_DIM), 0.02),
        'wa_k_norm': 1.0 + nrm(ks[18], (L, HEAD_DIM), 0.02),
        'wa_sink': nrm(ks[19], (L, WA_HEADS), 1.0),
        'w_proj_c': nrm(ks[20], (L, D_WA, D_MODEL), D_WA ** -0.5),
        'w_o': nrm(ks[21], (L, D_MODEL, D_MODEL), D_MODEL ** -0.5),
    }


def reference(x, c, ctx, c_ctx, norm_g, w_ada, b_ada, w_in, conv_w, conv_b, cln_g, cln_b,
              w_proj_a, na_q_norm, na_k_norm, na_rpb, w_proj_b, wa_q_norm, wa_k_norm,
              wa_sink, w_proj_c, w_o):
    B, S, _ = x.shape
    N = ctx.shape[1]
    G = WA_HEADS // WA_KV_HEADS
    t = jnp.arange(S)
    row_pos = t // GRID_W
    col_pos = t % GRID_W
    for l in range(DEPTH):
        update_ctx = l < DEPTH - 1
        mod = jax.nn.silu(c) @ w_ada[l] + b_ada[l]
        mod_c = jax.nn.silu(c_ctx) @ w_ada[l] + b_ada[l]
        shift, scale, gate = jnp.split(mod[:, None, :], 3, axis=-1)
        shift_c, scale_c, gate_c = jnp.split(mod_c[None, None, :], 3, axis=-1)
        h = rms_norm(x, norm_g[l]) * (1.0 + scale) + shift
        hc = rms_norm(ctx, norm_g[l]) * (1.0 + scale_c) + shift_c

        (a_glu, a_gate, nq, nk, nv, n_gate, wq, wk, wv, w_gate, merge) = jnp.split(
            h @ w_in[l], SPLIT_POINTS, axis=-1)
        if update_ctx:
            cp = jnp.split(hc @ w_in[l], SPLIT_POINTS, axis=-1)
            nck, ncv, wck, wcv = cp[3], cp[4], cp[7], cp[8]
        else:
            nck, ncv, wck, wcv = context_kv(hc, w_in[l])

        nk_c = rms_norm(split_heads(nck, NA_HEADS), na_k_norm[l])
        nv_c = split_heads(ncv, NA_HEADS)
        wk_c = rms_norm(split_heads(wck, WA_KV_HEADS), wa_k_norm[l])
        wv_c = split_heads(wcv, WA_KV_HEADS)

        o_na = neighbourhood_attention(
            rms_norm(split_heads(nq, NA_HEADS), na_q_norm[l]),
            rms_norm(split_heads(nk, NA_HEADS), na_k_norm[l]),
            split_heads(nv, NA_HEADS), nk_c, nv_c, na_rpb[l])
        o_wa = window_attention(
            axial_rope(rms_norm(split_heads(wq, WA_HEADS), wa_q_norm[l]), row_pos, col_pos),
            axial_rope(rms_norm(split_heads(wk, WA_KV_HEADS), wa_k_norm[l]), row_pos, col_pos),
            split_heads(wv, WA_KV_HEADS), wk_c, wv_c, wa_sink[l])
        y = merge_branches(a_glu, a_gate, o_na, n_gate, o_wa, w_gate, merge,
                           conv_w[l], conv_b[l], cln_g[l], cln_b[l],
                           w_proj_a[l], w_proj_b[l], w_proj_c[l], w_o[l])
        x_new = x + gate * y

        if update_ctx:
            (ac_glu, ac_gate, ncq, _, _, nc_gate, wcq, _, _, wc_gate, merge_c) = cp
            q_na_c = rms_norm(split_heads(ncq, NA_HEADS), na_q_norm[l])[:, :, :, None, :]
            oc_na = context_attention(q_na_c, nk_c, nv_c, None)
            q_wa_c = rms_norm(split_heads(wcq, WA_HEADS), wa_q_norm[l]).reshape(
                B, N, WA_KV_HEADS, G, HEAD_DIM)
            oc_wa = context_attention(q_wa_c, wk_c, wv_c, wa_sink[l].reshape(WA_KV_HEADS, G))
            yc = merge_branches(ac_glu, ac_gate, oc_na, nc_gate, oc_wa, wc_gate, merge_c,
                                conv_w[l], conv_b[l], cln_g[l], cln_b[l],
                                w_proj_a[l], w_proj_b[l], w_proj_c[l], w_o[l])
            ctx = ctx + gate_c * yc
        x = x_new
    return x
```

```python
import numpy as np
from contextlib import ExitStack
import concourse.bass as bass
import concourse.mybir as mybir
from concourse.bass_utils import run_bass_kernel_spmd

F32 = mybir.dt.float32
BF16 = mybir.dt.bfloat16
AF = mybir.ActivationFunctionType
ALU = mybir.AluOpType

L = 2
S_LAT = 4096
N_CTX = 256
S_ALL = S_LAT + N_CTX
D = 1024
D_IN = 7936
EPS = 1e-6
NEG = -30000.0
ENGS = ("pe", "act", "dve", "pool", "sp")
BLOCKS = [(b * 512, 512) for b in range(8)] + [(S_LAT, N_CTX)]


class Region:
    __slots__ = ("name", "lw", "reads")

    def __init__(self, name=""):
        self.name = name
        self.lw = None
        self.reads = {}


class Sched:
    NDS = 12

    def __init__(self, nc, stack):
        self.nc = nc
        self.ops = {e: [] for e in ENGS}
        self.cnt = {e: 0 for e in ENGS}
        self.semobj = {}
        for e in ENGS:
            self.semobj[("e", e)] = stack.enter_context(nc.semaphore("s_" + e))
        self.dq = {}
        for q in ("sp", "act", "pool"):
            lst = []
            for i in range(self.NDS):
                key = ("d", q, i)
                self.semobj[key] = stack.enter_context(nc.semaphore("d_%s%d" % (q, i)))
                lst.append([key, 0])
            self.dq[q] = [lst, 0]
        self.known = {e: {} for e in ENGS}
        self.out_tokens = []

    def _need(self, eng, tok, waits):
        if tok is None:
            return
        key, val = tok
        if eng == "pe" and key == ("e", "pe"):
            return
        if self.known[eng].get(key, 0) >= val:
            return
        if waits.get(key, 0) < val:
            waits[key] = val

    def _collect(self, eng, reads, writes, waits):
        for r in reads:
            self._need(eng, r.lw, waits)
        for w in writes:
            self._need(eng, w.lw, waits)
            for k, v in w.reads.items():
                self._need(eng, (k, v), waits)
        for k, v in waits.items():
            self.known[eng][k] = v
        return [(self.semobj[k], v) for k, v in waits.items()]

    def op(self, eng, fn, reads=(), writes=()):
        wl = self._collect(eng, reads, writes, {})
        self.cnt[eng] += 1
        seq = self.cnt[eng]
        key = ("e", eng)
        sem = self.semobj[key]
        for r in reads:
            r.reads[key] = seq
        for w in writes:
            w.lw = (key, seq)
            w.reads = {}

        def emit(e):
            for s, v in wl:
                e.wait_ge(s, v)
            fn(e).then_inc(sem, 1)
        self.ops[eng].append(emit)

    def dma(self, q, fn, reads=(), writes=(), is_output=False):
        lst, idx = self.dq[q]
        ent = lst[idx % self.NDS]
        self.dq[q][1] = idx + 1
        key = ent[0]
        waits = {}
        if ent[1] > 0:
            self._need(q, (key, ent[1]), waits)
        wl = self._collect(q, reads, writes, waits)
        ent[1] += 16
        val = ent[1]
        sem = self.semobj[key]
        for r in reads:
            if r.reads.get(key, 0) < val:
                r.reads[key] = val
        for w in writes:
            w.lw = (key, val)
            w.reads = {}
        if is_output:
            self.out_tokens.append((key, val))

        def emit(e):
            for s, v in wl:
                e.wait_ge(s, v)
            fn(e).then_inc(sem, 16)
        self.ops[q].append(emit)

    def barrier(self):
        toks = [(("e", x), self.cnt[x]) for x in ENGS if self.cnt[x] > 0]
        for q in self.dq:
            for key, val in self.dq[q][0]:
                if val > 0:
                    toks.append((key, val))
        for e in ENGS:
            waits = {}
            for key, val in toks:
                if key == ("e", e):
                    continue
                if self.known[e].get(key, 0) < val:
                    waits[key] = val
                    self.known[e][key] = val
            wl = [(self.semobj[k], v) for k, v in waits.items()]
            if wl:
                def emit(eo, wl=wl):
                    for s, v in wl:
                        eo.wait_ge(s, v)
                self.ops[e].append(emit)

    def finish(self):
        final = {}
        for k, v in self.out_tokens:
            final[k] = max(final.get(k, 0), v)
        wl = [(self.semobj[k], v) for k, v in final.items()]

        def emit(e):
            for s, v in wl:
                e.wait_ge(s, v)
        self.ops["sp"].append(emit)

    def emit_all(self):
        with self.nc.Block() as block:
            @block.tensor
            def _(e):
                for f in self.ops["pe"]:
                    f(e)

            @block.scalar
            def _(e):
                for f in self.ops["act"]:
                    f(e)

            @block.vector
            def _(e):
                for f in self.ops["dve"]:
                    f(e)

            @block.gpsimd
            def _(e):
                for f in self.ops["pool"]:
                    f(e)

            @block.sync
            def _(e):
                for f in self.ops["sp"]:
                    f(e)


class Arena:
    def __init__(self, t, size, name):
        self.t, self.size, self.name, self.off = t, size, name, 0

    def reset(self):
        self.off = 0

    def take(self, n, name=""):
        n16 = (n + 15) // 16 * 16
        assert self.off + n16 <= self.size, (self.name, name, self.off, n, self.size)
        ap = self.t[:, self.off:self.off + n]
        self.off += n16
        return ap


NA_TYPES = [None, 0, 2, 60, 62]


def na_chunks(qt):
    i0 = 2 * qt
    if i0 <= 2:
        return 1 + i0 // 2, [0, 1, 2, 3]
    if i0 >= 60:
        return 3 + (i0 - 60) // 2, [28, 29, 30, 31]
    return 0, [qt - 2, qt - 1, qt, qt + 1, qt + 2]


def _na_index():
    idx = np.full((5, 128, 5, 128), 15 * 31, dtype=np.int64)
    for ty in range(5):
        qt = 8 if ty == 0 else NA_TYPES[ty] // 2
        _, chunks = na_chunks(qt)
        for ci, kc in enumerate(chunks):
            for p in range(128):
                r = 2 * kc + p // 64
                c = p % 64
                for n in range(128):
                    i = 2 * qt + n // 64
                    j = n % 64
                    rs = min(max(i - 4, 0), 56)
                    cs = min(max(j - 8, 0), 48)
                    if rs <= r < rs + 8 and cs <= c < cs + 16:
                        idx[ty, p, ci, n] = (r - i + 7) * 31 + (c - j + 15)
    return idx


_NA_IDX = None


def _consts():
    cst = np.zeros((128, 5 * 128 + 384), np.float32)
    cst[:, 0:128] = np.eye(128)
    pm = np.zeros((128, 128), np.float32)
    for m in range(128):
        sub = m % 32
        if sub < 16:
            pm[m + 16, m] = 1.0
        else:
            pm[m - 16, m] = 1.0
    cst[:, 128:256] = pm
    bo = np.zeros((128, 128), np.float32)
    bo[0:64, 0:64] = 1.0 / 64
    bo[64:128, 64:128] = 1.0 / 64
    cst[:, 256:384] = bo
    cst[:, 384:512] = 1.0 / 512
    cst[:, 512:640] = 1.0
    p = np.arange(128)[:, None]
    n = np.arange(128)[None, :]
    cst[:, 640:768] = (n <= p)
    cst[:, 768:896] = 1.0
    cst[:, 896:1024] = (p <= n)
    t = np.arange(S_LAT)
    rowp = (t // 64).astype(np.float32)
    colp = (t % 64).astype(np.float32)
    inv = (10000.0 ** (-np.arange(16, dtype=np.float32) / 16)).astype(np.float32)
    rc = np.zeros((128, S_LAT), np.float32)
    rs = np.zeros((128, S_LAT), np.float32)
    for pp in range(128):
        d = pp % 64
        sub = d % 32
        pos = rowp if d < 32 else colp
        ang = (pos * inv[sub % 16]).astype(np.float32)
        rc[pp] = np.cos(ang)
        rs[pp] = -np.sin(ang) if sub < 16 else np.sin(ang)
    return cst, rc, rs


def build(debug=False):
    nc = bass.Bass("TRN2", target_bir_lowering=False)
    EI = "ExternalInput"
    dbgk = "ExternalOutput" if debug else "Internal"

    def din(name, shape):
        return nc.dram_tensor(name, list(shape), F32, kind=EI).ap()

    x_in = din("x", [S_LAT, D])
    ctx_in = din("ctx", [N_CTX, D])
    ccT_in = din("ccT", [128, 8, 2])
    normg_in = din("norm_gT", [L, 128, 8])
    bada_in = din("b_adaT", [L, 128, 24])
    wada_in = din("w_ada", [L, D, 3 * D])
    win_in = din("w_in", [L, D, D_IN])
    wpa_in = din("w_proj_a", [L, 512, D])
    wpb_in = din("w_proj_b", [L, 512, D])
    wpc_in = din("w_proj_c", [L, 512, D])
    wo_in = din("w_o", [L, D, D])
    convw_in = din("conv_wT", [L, 128, 4 * 31])
    cvec_in = din("cvecT", [L, 128, 12])
    qkn_in = din("qk_norm", [L, 128, 4])
    sink_in = din("sink_bc", [L, 128, 8])
    natab_in = din("na_tab", [L, 5, 128, 8 * 640])
    cst_in = din("cst", [128, 1024])
    ropec_in = din("rope_c", [128, S_LAT])
    ropes_in = din("rope_s", [128, S_LAT])
    out_x = nc.dram_tensor("out", [S_LAT, D], F32, kind="ExternalOutput").ap()

    hT_d = nc.dram_tensor("hT_d", [128, 8, S_ALL], BF16, kind=dbgk).ap()
    Y_d = [nc.dram_tensor("Y%d_d" % i, [128, 8, S_ALL], BF16, kind=dbgk).ap() for i in range(3)]
    x1_d = nc.dram_tensor("x1_d", [S_LAT, D], F32, kind=dbgk).ap()
    ctx1_d = nc.dram_tensor("ctx1_d", [N_CTX, D], F32, kind=dbgk).ap()
    etab_d = nc.dram_tensor("etab_d", [5, 128, 5120], BF16, kind="Internal").ap()

    R_hT = [Region("hT%d" % i) for i in range(34)]
    R_Y = [[Region("Y%d_%d" % (br, b)) for b in range(9)] for br in range(3)]
    R_x1 = [Region("x1_%d" % i) for i in range(34)]
    R_etab = [Region("etab%d" % i) for i in range(5)]

    with ExitStack() as st:
        S = Sched(nc, st)

        def sb(name, shape, dt):
            return st.enter_context(nc.sbuf_tensor(name, list(shape), dt))

        BIG = Arena(sb("BIG", [128, 37120], BF16), 37120, "BIG")
        WB = Arena(sb("WB", [128, 24576], BF16), 24576, "WB")
        W32 = Arena(sb("W32", [128, 7168], F32), 7168, "W32")
        W16 = Arena(sb("W16", [128, 16384], BF16), 16384, "W16")
        HTB = sb("HTB", [128, 2, 8, 512], BF16)
        cst16 = sb("cst16", [128, 1024], BF16)
        cst32 = sb("cst32", [128, 256], F32)
        silc = sb("silc", [128, 8, 2], BF16)
        cc32 = sb("cc32", [128, 8, 2], F32)
        modsb = sb("modsb", [128, 24, 2], F32)
        gsb = sb("gsb", [128, 8, 2], F32)
        normg = sb("normg", [128, 8], F32)
        bada = sb("bada", [128, 24], F32)
        convw = sb("convw", [128, 124], F32)
        cvec = sb("cvec", [128, 12], F32)
        qkn = sb("qkn", [128, 4], F32)
        esink = sb("esink", [128, 8], F32)
        stat = sb("stat", [128, 8], F32)
        hmask = sb("hmask", [128, 2], F32)
        ps = [st.enter_context(nc.psum_tensor("ps%d" % i, [128, 512], F32)) for i in range(7)]
        psT = st.enter_context(nc.psum_tensor("psT", [128, 1024], BF16))
        R_ps = [Region("ps%d" % i) for i in range(7)]
        R_psT = Region("psT")
        R_HTB = [Region("HTB0"), Region("HTB1")]
        R_c = Region("consts")
        R_small = Region("small")
        R_gate = Region("gate_bc")

        ident16 = cst16[:, 0:128]
        pmat16 = cst16[:, 128:256]
        bones16 = cst16[:, 256:384]
        o512_16 = cst16[:, 384:512]
        wamask16 = cst16[:, 640:1024]
        ident32 = cst32[:, 0:128]
        ones32 = cst32[:, 128:256]

        def mm(out, lhsT, rhs, start, stop, rd, wr):
            S.op("pe", lambda e: e.matmul(out, lhsT=lhsT, rhs=rhs, start=start, stop=stop), rd, wr)

        def tr(out, in_, rd, wr):
            S.op("pe", lambda e: e.transpose(out=out, in_=in_, identity=ident16), rd + [R_c], wr)

        def act(out, in_, func, rd, wr, **kw):
            S.op("act", lambda e: e.activation(out=out, in_=in_, func=func, **kw), rd, wr)

        def tt(eng, out, in0, in1, op, rd, wr):
            S.op(eng, lambda e: e.tensor_tensor(out=out, in0=in0, in1=in1, op=op), rd, wr)

        def ts(eng, out, in0, s1, s2, op0, op1, rd, wr):
            if op1 is None:
                S.op(eng, lambda e: e.tensor_scalar(out=out, in0=in0, scalar1=s1, scalar2=None, op0=op0), rd, wr)
            else:
                S.op(eng, lambda e: e.tensor_scalar(out=out, in0=in0, scalar1=s1, scalar2=s2, op0=op0, op1=op1), rd, wr)

        def stt(eng, out, in0, scalar, in1, op0, op1, rd, wr):
            S.op(eng, lambda e: e.scalar_tensor_tensor(out=out, in0=in0, scalar=scalar, in1=in1, op0=op0, op1=op1), rd, wr)

        def cp(eng, out, in_, rd, wr):
            if eng == "act":
                S.op(eng, lambda e: e.activation(out=out, in_=in_, func=AF.Copy), rd, wr)
            else:
                S.op(eng, lambda e: e.tensor_copy(out=out, in_=in_), rd, wr)

        def recip(out, in_, rd, wr):
            S.op("dve", lambda e: e.reciprocal(out=out, in_=in_), rd, wr)

        def mset(eng, ap, val, wr):
            S.op(eng, lambda e: e.memset(ap, val), [], wr)

        def dma(q, out, in_, rd, wr, is_output=False):
            S.dma(q, lambda e: e.dma_start(out=out, in_=in_), rd, wr, is_output=is_output)

        class WReg:
            def __init__(self):
                self.rs = []

            def add(self, a, b):
                r = Region("w%d" % a)
                self.rs.append((a, b, r))
                return r

            def __call__(self, c):
                for a, b, r in self.rs:
                    if a <= c < b:
                        return r
                raise KeyError(c)

        def load_w(dst3, src2, wreg, base=0, order=None, region=None):
            n = src2.shape[1]
            starts = list(range(0, n, 512))
            if order is not None:
                starts = [starts[i] for i in order]
            for c0 in starts:
                cw = min(512, n - c0)
                dma("pool", dst3[:, :, c0:c0 + cw],
                    src2[:, c0:c0 + cw].rearrange("(k p) n -> p k n", p=128), [],
                    [region if region is not None else wreg.add(base + c0, base + c0 + cw)])

        def load_hT(bi, buf):
            t0, n = BLOCKS[bi]
            rr = R_hT[t0 // 128:(t0 + n) // 128]
            dma("sp", HTB[:, buf, :, 0:n], hT_d[:, :, t0:t0 + n], rr, [R_HTB[buf]])

        def proj_fm(psi, wview, c0, hbuf, n, wreg):
            for k in range(8):
                mm(ps[psi][:, 0:n], wview[:, k, c0:c0 + 128], HTB[:, hbuf, k, 0:n],
                   k == 0, k == 7, [wreg(c0), R_HTB[hbuf]], [R_ps[psi]])

        def proj_tm(psi, wview, c0, ncols, hbuf, tt_, wreg):
            for k in range(8):
                mm(ps[psi][:, 0:ncols], HTB[:, hbuf, k, tt_ * 128:(tt_ + 1) * 128],
                   wview[:, k, c0:c0 + ncols], k == 0, k == 7, [wreg(c0), R_HTB[hbuf]], [R_ps[psi]])

        R_hm = Region("hmask")
        mset("pool", hmask[:, :], 0.0, [R_hm])
        mset("pool", hmask[0:64, 0:1], 1.0, [R_hm])
        mset("pool", hmask[64:128, 1:2], 1.0, [R_hm])
        dma("pool", cst16[:], cst_in, [], [R_c])
        dma("sp", cst32[:, 0:128], cst_in[:, 0:128], [], [R_c])
        dma("sp", cst32[:, 128:256], cst_in[:, 512:640], [], [R_c])
        dma("sp", cc32[:], ccT_in, [], [R_small])
        act(silc[:], cc32[:], AF.Silu, [R_small], [R_small])

        for l in range(L):
            last = (l == L - 1)
            nblk = 8 if last else 9
            xin = x_in if l == 0 else x1_d
            cin = ctx_in if l == 0 else ctx1_d
            xout = out_x if last else x1_d
            for a in (BIG, WB, W32, W16):
                a.reset()
            S.barrier()

            R_w = WReg()
            wad = WB.take(8 * 3072).rearrange("p (k n) -> p k n", k=8)
            load_w(wad, wada_in[l], R_w)
            dma("sp", normg[:], normg_in[l], [], [R_small])
            dma("sp", bada[:], bada_in[l], [], [R_small])
            dma("sp", convw[:], convw_in[l], [], [R_small])
            dma("sp", cvec[:], cvec_in[l], [], [R_small])
            dma("sp", qkn[:], qkn_in[l], [], [R_small])
            dma("sp", esink[:], sink_in[l], [], [R_small])
            modps = ps[0][:, 0:48].rearrange("p (f r) -> p f r", f=24)
            for f in range(16):
                for k in range(8):
                    mm(modps[:, f, :], wad[:, k, f * 128:(f + 1) * 128], silc[:, k, :],
                       k == 0, k == 7, [R_w(f * 128), R_small], [R_ps[0]])
            tt("dve", modsb[:, 0:16, :], modps[:, 0:16, :], bada[:, 0:16].unsqueeze(2).to_broadcast([128, 16, 2]), ALU.add,
               [R_ps[0], R_small], [R_small])
            ts("dve", gsb[:], modsb[:, 8:16, :], 1.0, None, ALU.add, None, [R_small], [R_small])
            tt("dve", gsb[:], gsb[:], normg[:].unsqueeze(2).to_broadcast([128, 8, 2]), ALU.mult,
               [R_small], [R_small])
            ts("dve", qkn[:, 0:1], qkn[:, 0:1], 0.125, None, ALU.mult, None, [R_small], [R_small])
            ts("dve", qkn[:, 2:3], qkn[:, 2:3], 0.125, None, ALU.mult, None, [R_small], [R_small])
            act(esink[:], esink[:], AF.Exp, [R_small], [R_small])

            W16.reset()
            dgs = BIG.take(124 * 128).rearrange("p (j k n) -> p j k n", j=4, k=31)
            R_dgs = Region("dgs")
            dg_todo = [(j, k) for j in range(4) for k in range(31)]

            def build_dg(cnt):
                for i_ in range(cnt):
                    if dg_todo:
                        j, k = dg_todo.pop(0)
                        if False:
                            ts("dve", dgs[:, j, k, :], ident32, convw[:, j * 31 + k:j * 31 + k + 1], None,
                               ALU.mult, None, [R_c, R_small], [R_dgs])
                        else:
                            act(dgs[:, j, k, :], ident32, AF.Copy, [R_c, R_small], [R_dgs],
                                scale=convw[:, j * 31 + k:j * 31 + k + 1])
            xt = [W32.take(1024) for _ in range(3)]
            t32 = [W32.take(1024) for _ in range(2)]
            junk16 = W16.take(1024)
            xn16 = [W16.take(1024) for _ in range(2)]
            hts16 = [W16.take(1024) for _ in range(2)]
            R_xt = [Region("xt%d" % i) for i in range(3)]
            R_t32, R_junk = [Region("t32a"), Region("t32b")], Region("junk")
            R_xn = [Region("xn0"), Region("xn1")]
            R_hts = [Region("hts0"), Region("hts1")]
            R_st = [Region("st0"), Region("st1")]
            psT3 = psT[:, :].rearrange("p (k n) -> p k n", k=8)

            def a_stages(ti):
                b = ti % 2
                b3 = ti % 3
                r = 0 if ti < 32 else 1
                t32v = t32[b].rearrange("p (k n) -> p k n", k=8)
                htv = hts16[b].rearrange("p (k n) -> p k n", k=8)

                def s0():
                    src = xin[ti * 128:(ti + 1) * 128, :] if ti < 32 else cin[(ti - 32) * 128:(ti - 31) * 128, :]
                    rd = [R_x1[ti]] if l > 0 else []
                    dma("sp", xt[b3], src, rd, [R_xt[b3]])

                def s1():
                    mset("dve", stat[:, b:b + 1], 0.0, [R_st[b]])
                    act(junk16, xt[b3], AF.Square, [R_xt[b3]], [R_junk, R_st[b]], accum_out=stat[:, b:b + 1])
                    act(stat[:, 2 + b:3 + b], stat[:, b:b + 1], AF.Sqrt, [R_st[b]], [R_st[b]],
                        scale=1.0 / D, bias=EPS)
                    recip(stat[:, 4 + b:5 + b], stat[:, 2 + b:3 + b], [R_st[b]], [R_st[b]])
                    ts("dve", xn16[b], xt[b3], stat[:, 4 + b:5 + b], None, ALU.mult, None,
                       [R_xt[b3], R_st[b]], [R_xn[b]])

                def s2():
                    for k in range(8):
                        tr(psT3[:, k, :], xn16[b][:, k * 128:(k + 1) * 128], [R_xn[b]], [R_psT])
                    tt("dve", t32v, psT3, gsb[:, :, r:r + 1].to_broadcast([128, 8, 128]), ALU.mult,
                       [R_psT, R_small], [R_t32[b]])

                def s3():
                    tt("pool", htv, t32v, modsb[:, 0:8, r:r + 1].to_broadcast([128, 8, 128]), ALU.add,
                       [R_t32[b], R_small], [R_hts[b]])
                    dma("sp", hT_d[:, :, ti * 128:(ti + 1) * 128], htv, [R_hts[b]], [R_hT[ti]])
                return [s0, s1, s2, s3]

            a_items = [a_stages(ti) for ti in range(34)]
            for step in range(34 + 3):
                for s_ in range(4):
                    i_ = step - s_
                    if 0 <= i_ < 34:
                        a_items[i_][s_]()
                build_dg(4)
            build_dg(124)

            R_mod2 = Region("mod2")
            modps2 = ps[1][:, 0:16].rearrange("p (f r) -> p f r", f=8)
            for f in range(16, 24):
                for k in range(8):
                    mm(modps2[:, f - 16, :], wad[:, k, f * 128:(f + 1) * 128], silc[:, k, :],
                       k == 0, k == 7, [R_w(f * 128), R_small], [R_ps[1]])
            tt("dve", modsb[:, 16:24, :], modps2, bada[:, 16:24].unsqueeze(2).to_broadcast([128, 8, 2]), ALU.add,
               [R_ps[1], R_small], [R_mod2])
            for a in (WB, W32, W16):
                a.reset()
            S.barrier()
            R_w = WReg()
            R_wp = WReg()
            wv = WB.take(8 * 1536).rearrange("p (k n) -> p k n", k=8)
            wpa = WB.take(4 * 1024).rearrange("p (k n) -> p k n", k=4)
            load_w(wv, win_in[l][:, 0:1536], R_w)
            load_w(wpa, wpa_in[l], R_wp)
            A_lat_f = BIG.take(4 * (S_LAT + 32))
            A_ctx_f = BIG.take(4 * (N_CTX + 32))
            A_lat = A_lat_f.rearrange("p (j n) -> p j n", j=4)
            A_ctx = A_ctx_f.rearrange("p (j n) -> p j n", j=4)
            R_A = Region("A")
            mset("pool", A_lat_f, 0.0, [R_A])
            mset("pool", A_ctx_f, 0.0, [R_A])
            sg32 = [W32.take(512) for _ in range(2)]
            R_sg = [Region("sg0"), Region("sg1")]
            load_hT(0, 0)
            for bi in range(nblk):
                t0, n = BLOCKS[bi]
                hb = bi % 2
                if bi + 1 < nblk:
                    load_hT(bi + 1, 1 - hb)
                Ab, a0 = (A_lat, t0) if bi < 8 else (A_ctx, 0)
                for j in range(4):
                    pa, pb_ = (0, 1) if j % 2 == 0 else (2, 3)
                    proj_fm(pb_, wv, 512 + j * 128, hb, n, R_w)
                    proj_fm(pa, wv, j * 128, hb, n, R_w)
                    sb_ = j % 2
                    act(sg32[sb_][:, 0:n], ps[pb_][:, 0:n], AF.Sigmoid, [R_ps[pb_]], [R_sg[sb_]])
                    tt("dve", Ab[:, j, 15 + a0:15 + a0 + n], ps[pa][:, 0:n], sg32[sb_][:, 0:n], ALU.mult,
                       [R_ps[pa], R_sg[sb_]], [R_A])
            v32 = W32.take(2048).rearrange("p (j n) -> p j n", j=4)
            mean32 = W32.take(512)
            var32 = W32.take(512)
            rstd32 = W32.take(512)
            t1 = [W32.take(512) for _ in range(2)]
            t2 = [W32.take(512) for _ in range(2)]
            sga = sg32
            v16 = W16.take(2048).rearrange("p (j n) -> p j n", j=4)
            sq16 = W16.take(2048).rearrange("p (j n) -> p j n", j=4)
            z16 = W16.take(2048).rearrange("p (j n) -> p j n", j=4)
            ya16 = WB.take(4096).rearrange("p (k n) -> p k n", k=8)
            R_v32, R_v16, R_sq16, R_z16, R_ya = (Region("v32"), Region("v16"), Region("sq16"),
                                                 Region("z16"), Region("ya"))
            R_mean, R_var, R_rstd = Region("mean"), Region("var"), Region("rstd")
            R_v32j = [Region("v32_%d" % j) for j in range(4)]
            NPE = 31
            R_t1 = [Region("t1a"), Region("t1b")]
            R_t2 = [Region("t2a"), Region("t2b")]
            load_hT(0, 0)
            for bi in range(nblk):
                t0, n = BLOCKS[bi]
                hb = bi % 2
                if bi + 1 < nblk:
                    load_hT(bi + 1, 1 - hb)
                Ab, a0 = (A_lat, t0) if bi < 8 else (A_ctx, 0)
                for j in range(4):
                    pi = j % 2
                    for k in range(NPE):
                        mm(ps[pi][:, 0:n], dgs[:, j, k, :], Ab[:, j, a0 + k:a0 + k + n], k == 0, k == NPE - 1,
                           [R_dgs, R_A], [R_ps[pi]])
                    act(v32[:, j, 0:n], ps[pi][:, 0:n], AF.Identity, [R_ps[pi], R_small], [R_v32j[j]],
                        bias=cvec[:, j:j + 1])
                    for k in range(NPE, 31):
                        stt("dve", v32[:, j, 0:n], Ab[:, j, a0 + k:a0 + k + n], convw[:, j * 31 + k:j * 31 + k + 1],
                            v32[:, j, 0:n], ALU.mult, ALU.add, [R_A, R_small, R_v32j[j]], [R_v32j[j]])
                    act(v16[:, j, 0:n], ps[pi][:, 0:n], AF.Identity, [R_ps[pi], R_small], [R_v16],
                        bias=cvec[:, j:j + 1])
                    act(sq16[:, j, 0:n], v32[:, j, 0:n], AF.Square, [R_v32j[j]], [R_sq16])
                for j in range(4):
                    mm(ps[2][:, 0:n], o512_16, v16[:, j, 0:n], j == 0, j == 3, [R_c, R_v16], [R_ps[2]])
                for j in range(4):
                    mm(ps[3][:, 0:n], o512_16, sq16[:, j, 0:n], j == 0, j == 3, [R_c, R_sq16], [R_ps[3]])
                cp("dve", mean32[:, 0:n], ps[2][:, 0:n], [R_ps[2]], [R_mean])
                tt("dve", var32[:, 0:n], mean32[:, 0:n], mean32[:, 0:n], ALU.mult, [R_mean], [R_var])
                tt("dve", var32[:, 0:n], ps[3][:, 0:n], var32[:, 0:n], ALU.subtract, [R_ps[3], R_var], [R_var])
                act(rstd32[:, 0:n], var32[:, 0:n], AF.Sqrt, [R_var], [R_rstd], bias=EPS)
                recip(rstd32[:, 0:n], rstd32[:, 0:n], [R_rstd], [R_rstd])
                for j in range(4):
                    b2 = j % 2
                    pi = 4 + b2
                    proj_fm(pi, wv, 1024 + j * 128, hb, n, R_w)
                    act(sga[b2][:, 0:n], ps[pi][:, 0:n], AF.Silu, [R_ps[pi]], [R_sg[b2]])
                    tt("dve", t1[b2][:, 0:n], v32[:, j, 0:n], mean32[:, 0:n], ALU.subtract,
                       [R_v32j[j], R_mean], [R_t1[b2]])
                    tt("dve", t1[b2][:, 0:n], t1[b2][:, 0:n], rstd32[:, 0:n], ALU.mult,
                       [R_t1[b2], R_rstd], [R_t1[b2]])
                    act(t2[b2][:, 0:n], t1[b2][:, 0:n], AF.Silu, [R_t1[b2], R_small], [R_t2[b2]],
                        scale=cvec[:, 4 + j:5 + j], bias=cvec[:, 8 + j:9 + j])
                    tt("pool" if j % 2 == 0 else "dve", z16[:, j, 0:n], t2[b2][:, 0:n], sga[b2][:, 0:n], ALU.mult,
                       [R_t2[b2], R_sg[b2]], [R_z16])
                for i in range(8):
                    pi = i % 2
                    for j in range(4):
                        mm(ps[pi][:, 0:n], wpa[:, j, i * 128:(i + 1) * 128], z16[:, j, 0:n], j == 0, j == 3,
                           [R_wp(i * 128), R_z16], [R_ps[pi]])
                    if i % 2 == 1:
                        cp("act", ya16[:, i, 0:n], ps[pi][:, 0:n], [R_ps[pi]], [R_ya])
                    else:
                        cp("dve", ya16[:, i, 0:n], ps[pi][:, 0:n], [R_ps[pi]], [R_ya])
                dma("sp", Y_d[0][:, :, t0:t0 + n], ya16[:, :, 0:n], [R_ya], [R_Y[0][bi]])

            for kind in (0, 1):
                for a in (BIG, WB, W32, W16):
                    a.reset()
                S.barrier()
                is_wa = kind == 1
                R_w = WReg()
                R_wp = WReg()
                if not is_wa:
                    ncol = 2048
                    wv = WB.take(8 * ncol).rearrange("p (k n) -> p k n", k=8)
                    load_w(wv, win_in[l][:, 1536:3584], R_w, order=[1, 2, 0, 3])
                    cq, ck, cv_, cg = 0, 512, 1024, 1536
                    nkc, nkv = 4, 8
                    gq, gk = qkn[:, 0:1], qkn[:, 1:2]
                    wp_src = wpb_in[l]
                else:
                    ncol = 1408
                    wv = WB.take(8 * ncol).rearrange("p (k n) -> p k n", k=8)
                    for g in range(2):
                        rg = R_w.add(512 + g * 128, 512 + (g + 1) * 128)
                        for dup in range(2):
                            c0 = 512 + g * 128 + dup * 64
                            load_w(wv[:, :, c0:c0 + 64], win_in[l][:, 4096 + g * 64:4096 + (g + 1) * 64], R_w, region=rg)
                    load_w(wv[:, :, 768:896], win_in[l][:, 4224:4352], R_w, base=768)
                    load_w(wv[:, :, 0:512], win_in[l][:, 3584:4096], R_w, base=0)
                    load_w(wv[:, :, 896:1408], win_in[l][:, 4352:4864], R_w, base=896)
                    cq, ck, cv_, cg = 0, 512, 768, 896
                    nkc, nkv = 2, 2
                    gq, gk = qkn[:, 2:3], qkn[:, 3:4]
                    wp_src = wpc_in[l]
                wpp = WB.take(4 * 1024).rearrange("p (k n) -> p k n", k=4)
                load_w(wpp, wp_src, R_wp)
                KT = BIG.take(nkc * S_ALL).rearrange("p (c n) -> p c n", c=nkc)
                VV_f = BIG.take(34 * nkv * 65)
                VV = VV_f.rearrange("p (t h d) -> p t h d", t=34, h=nkv)
                R_KT, R_VV = Region("KT"), Region("VV")
                mset("pool", VV_f, 1.0, [R_VV])
                R_tab = Region("tab")
                R_etb = Region("etb")
                prep_thunks = []
                sgt = [W32.take(512) for _ in range(4)]
                R_sgt = [Region("sgt%d" % i) for i in range(4)]
                if not is_wa:
                    tab16_f = W16.take(5120)
                    tab16 = tab16_f.rearrange("p (h n) -> p h n", h=8)
                    def mk_piece(ty, pc):
                        def f():
                            sb_i = pc % 4
                            dma("sp", sgt[sb_i], natab_in[l, ty][:, pc * 512:(pc + 1) * 512], [], [R_sgt[sb_i]])
                            act(tab16_f[:, pc * 512:(pc + 1) * 512], sgt[sb_i], AF.Exp, [R_sgt[sb_i]], [R_tab])
                            if pc == 9:
                                dma("sp", etab_d[ty], tab16_f, [R_tab], [R_etab[ty]])
                        return f
                    prep_thunks = [mk_piece(ty, pc) for ty in range(5) for pc in range(10)]
                sq16 = [W16.take(512) for _ in range(2)]
                R_sq = [Region("sq0"), Region("sq1")]
                sd32 = [W32.take(512) for _ in range(2)]
                R_sd = [Region("sd0"), Region("sd1")]
                rc32 = W32.take(512)
                rs32 = W32.take(512)
                R_rope = Region("rope")
                if is_wa:
                    kn16 = [W16.take(512) for _ in range(2)]
                    R_kn = [Region("kn0"), Region("kn1")]
                    ra32 = [W32.take(512) for _ in range(2)]
                    rb32 = [W32.take(512) for _ in range(2)]
                    R_ra = [Region("ra0"), Region("ra1")]
                    R_rb = [Region("rb0"), Region("rb1")]

                def skew(items):
                    ns = max(len(it) for it in items)
                    for step in range(len(items) + ns - 1):
                        for s_ in range(ns):
                            i_ = step - s_
                            if 0 <= i_ < len(items) and s_ < len(items[i_]):
                                items[i_][s_]()

                def normed_stages(idx, psi, wcol, hb, n, gain, out16, R_out, rope, post=None):
                    b = idx % 2
                    pst = 6

                    def s1():
                        proj_fm(psi, wv, wcol, hb, n, R_w)
                        act(sq16[b][:, 0:n], ps[psi][:, 0:n], AF.Square, [R_ps[psi]], [R_sq[b]])

                    def s2():
                        mm(ps[pst][:, 0:n], bones16, sq16[b][:, 0:n], True, True, [R_c, R_sq[b]], [R_ps[pst]])
                        act(sd32[b][:, 0:n], ps[pst][:, 0:n], AF.Sqrt, [R_ps[pst]], [R_sd[b]], bias=EPS)
                        recip(sd32[b][:, 0:n], sd32[b][:, 0:n], [R_sd[b]], [R_sd[b]])
                        if not rope:
                            stt("dve", out16, ps[psi][:, 0:n], gain, sd32[b][:, 0:n], ALU.mult, ALU.mult,
                                [R_ps[psi], R_small, R_sd[b]], [R_out])
                        else:
                            stt("dve", kn16[b][:, 0:n], ps[psi][:, 0:n], gain, sd32[b][:, 0:n], ALU.mult, ALU.mult,
                                [R_ps[psi], R_small, R_sd[b]], [R_kn[b]])

                    def s3():
                        mm(ps[pst][:, 0:n], pmat16, kn16[b][:, 0:n], True, True, [R_c, R_kn[b]], [R_ps[pst]])
                        tt("pool", ra32[b][:, 0:n], kn16[b][:, 0:n], rc32[:, 0:n], ALU.mult, [R_kn[b], R_rope], [R_ra[b]])
                        tt("dve", rb32[b][:, 0:n], ps[pst][:, 0:n], rs32[:, 0:n], ALU.mult, [R_ps[pst], R_rope], [R_rb[b]])
                        tt("dve", out16, ra32[b][:, 0:n], rb32[b][:, 0:n], ALU.add, [R_ra[b], R_rb[b]], [R_out])
                    st_ = [s1, s2, s3] if rope else [s1, s2]
                    if post is not None:
                        st_.append(post)
                    return st_

                load_hT(0, 0)
                for bi in range(9):
                    t0, n = BLOCKS[bi]
                    hb = bi % 2
                    if bi + 1 < 9:
                        load_hT(bi + 1, 1 - hb)
                    rope = is_wa and bi < 8
                    if rope:
                        dma("sp", rc32[:, 0:n], ropec_in[:, t0:t0 + n], [], [R_rope])
                        dma("sp", rs32[:, 0:n], ropes_in[:, t0:t0 + n], [], [R_rope])
                    items = []
                    for c in range(nkc):
                        items.append(normed_stages(c, c, ck + c * 128, hb, n, gk, KT[:, c, t0:t0 + n], R_KT, rope))
                    for t_ in range(n // 128):
                        def v1(t_=t_):
                            pi = 4 + t_ % 2
                            proj_tm(pi, wv, cv_, nkv * 64, hb, t_, R_w)

                        def v2(t_=t_):
                            pi = 4 + t_ % 2
                            gt = t0 // 128 + t_
                            src = ps[pi][:, 0:nkv * 64].rearrange("p (h d) -> p h d", h=nkv)
                            cp("act" if t_ % 2 == 0 else "dve", VV[:, gt, :, 0:64], src, [R_ps[pi]], [R_VV])
                        items.append([v1, v2])
                    skew(items)
                    for _ in range(7):
                        if prep_thunks:
                            prep_thunks.pop(0)()
                while prep_thunks:
                    prep_thunks.pop(0)()

                QT = W16.take(2048).rearrange("p (c n) -> p c n", c=4)
                R_QT = Region("QT")
                QTz = W16.take(4096).rearrange("p (c e n) -> p c e n", c=4, e=2)
                R_QTz = Region("QTz")
                R_QTc = [Region("QTc%d" % c) for c in range(4)]
                ogT = W16.take(2048).rearrange("p (c n) -> p c n", c=4)
                R_ogT = Region("ogT")
                og16 = W16.take(512)
                R_og16 = Region("og16")
                Pw = [W16.take(640) for _ in range(2)]
                R_Pw = [Region("Pw0"), Region("Pw1")]
                R_Pw5 = [Region("Pw5_0"), Region("Pw5_1")]
                PcE = [BIG.take(384) for _ in range(2)]
                R_Pc = [Region("Pc0"), Region("Pc1")]
                Ew = [BIG.take(512) for _ in range(2)]
                R_Ew = [Region("Ew0"), Region("Ew1")]
                og32 = W32.take(512)
                R_og32 = Region("og32")
                yb16 = WB.take(4096).rearrange("p (k n) -> p k n", k=8)
                R_yb = Region("yb")
                R_rec = Region("rec")
                cur_edge = [-1]
                load_hT(0, 0)
                for bi in range(nblk):
                    t0, n = BLOCKS[bi]
                    hb = bi % 2
                    if bi + 1 < nblk:
                        load_hT(bi + 1, 1 - hb)
                    rope = is_wa and bi < 8
                    if rope:
                        dma("sp", rc32[:, 0:n], ropec_in[:, t0:t0 + n], [], [R_rope])
                        dma("sp", rs32[:, 0:n], ropes_in[:, t0:t0 + n], [], [R_rope])
                    ntile = n // 128
                    items = []
                    for c in range(4):
                        def qz(c=c):
                            for e_ in range(2):
                                act(QTz[:, c, e_, 0:n], QT[:, c, 0:n], AF.Copy, [R_QTc[c], R_hm], [R_QTz],
                                    scale=hmask[:, e_:e_ + 1])
                        items.append(normed_stages(c, c, cq + c * 128, hb, n, gq, QT[:, c, 0:n], R_QTc[c], rope, post=qz))
                    for t_ in range(ntile):
                        def g1(t_=t_):
                            proj_tm(4 + t_ % 2, wv, cg, 512, hb, t_, R_w)

                        def g2(t_=t_):
                            pi = 4 + t_ % 2
                            act(sgt[t_], ps[pi][:, 0:512], AF.Silu, [R_ps[pi]], [R_sgt[t_]])
                        items.append([g1, g2])
                    skew(items)

                    def scores(t_, h):
                        qt = t0 // 128 + t_
                        if bi == 8:
                            wch, tabv, R_tb = [], None, None
                        elif not is_wa:
                            ty, wch = na_chunks(qt)
                            if cur_edge[0] != ty:
                                dma("sp", tab16_f, etab_d[ty], [R_etab[ty]], [R_tab])
                                cur_edge[0] = ty
                            tabv, R_tb = tab16, R_tab
                        else:
                            wch = [c_ for c_ in (qt - 1, qt, qt + 1) if 0 <= c_ < 32]
                            moff = 128 if qt == 0 else 0
                            tabv, R_tb = None, R_c
                        cch = [32, 33]
                        nw = len(wch)
                        c2 = h // 2
                        pb = 64 * (h % 2)
                        kc = (h // 4) if is_wa else c2
                        hp = h % 2
                        pw_i, pc_i = (2, 4) if hp == 0 else (3, 0)
                        qv = QTz[:, c2, h % 2, t_ * 128:(t_ + 1) * 128]
                        for i_, kch in enumerate(cch):
                            mm(ps[pc_i][:, i_ * 128:(i_ + 1) * 128],
                               KT[:, kc, kch * 128:(kch + 1) * 128], qv, True, True,
                               [R_KT, R_QTz], [R_ps[pc_i]])
                        if nw == 5:
                            kch = wch[4]
                            mm(ps[pc_i][:, 256:384],
                               KT[:, kc, kch * 128:(kch + 1) * 128], qv, True, True,
                               [R_KT, R_QTz], [R_ps[pc_i]])
                        nce = 384 if nw == 5 else 256
                        act(PcE[hp][:, 0:nce], ps[pc_i][:, 0:nce], AF.Exp, [R_ps[pc_i]], [R_Pc[hp]])
                        for i_, kch in enumerate(wch[:4]):
                            mm(ps[pw_i][:, i_ * 128:(i_ + 1) * 128],
                               KT[:, kc, kch * 128:(kch + 1) * 128], qv, True, True,
                               [R_KT, R_QTz], [R_ps[pw_i]])
                        if nw:
                            n4 = min(nw, 4) * 128
                            if is_wa:
                                mk = wamask16[:, moff:moff + nw * 128]
                            else:
                                mk = tabv[:, h, 0:nw * 128]
                            if nw == 5:
                                tt("dve", Pw[hp][:, 512:640], PcE[hp][:, 256:384], mk[:, 512:640], ALU.mult,
                                   [R_Pc[hp], R_tb], [R_Pw5[hp]])
                            act(Ew[hp][:, 0:n4], ps[pw_i][:, 0:n4], AF.Exp, [R_ps[pw_i]], [R_Ew[hp]])
                            tt("dve", Pw[hp][:, 0:n4], Ew[hp][:, 0:n4], mk[:, 0:n4], ALU.mult,
                               [R_Ew[hp], R_tb], [R_Pw[hp]])
                        allch = [(PcE[hp][:, i_ * 128:(i_ + 1) * 128], kch, R_Pc[hp]) for i_, kch in enumerate(cch)]
                        if nw == 5:
                            allch.append((Pw[hp][:, 512:640], wch[4], R_Pw5[hp]))
                        allch += [(Pw[hp][:, i_ * 128:(i_ + 1) * 128], kch, R_Pw[hp]) for i_, kch in enumerate(wch[:4])]
                        return allch

                    def pv(t_, h, allch):
                        vh = (h // 4) if is_wa else h
                        opi = 5 if h < 4 else 1
                        ov = ps[opi][:, 0:260].rearrange("p (h d) -> p h d", h=4)[:, h % 4, :]
                        for i_, (pap, kch, rg) in enumerate(allch):
                            mm(ov, pap, VV[:, kch, vh, :], i_ == 0, i_ == len(allch) - 1, [rg, R_VV], [R_ps[opi]])

                    def epi_half(t_, g4):
                        opi = 5 if g4 == 0 else 1
                        o3 = ps[opi][:, 0:260].rearrange("p (h d) -> p h d", h=4)
                        rec = stat[:, 0:4]
                        if is_wa:
                            tt("dve", rec, o3[:, :, 64], esink[:, g4 * 4:(g4 + 1) * 4], ALU.add,
                               [R_ps[opi], R_small], [R_rec])
                            recip(rec, rec, [R_rec], [R_rec])
                        else:
                            recip(rec, o3[:, :, 64], [R_ps[opi]], [R_rec])
                        o32 = og32[:, 0:256].rearrange("p (h d) -> p h d", h=4)
                        tt("dve", o32, o3[:, :, 0:64], rec.unsqueeze(2).to_broadcast([128, 4, 64]), ALU.mult,
                           [R_ps[opi], R_rec], [R_og32])
                        tt("pool", og16[:, g4 * 256:(g4 + 1) * 256], og32[:, 0:256],
                           sgt[t_][:, g4 * 256:(g4 + 1) * 256], ALU.mult, [R_og32, R_sgt[t_]], [R_og16])

                    def transposes(t_):
                        pT4 = psT[:, 0:512].rearrange("p (c n) -> p c n", c=4)
                        for c in range(4):
                            tr(pT4[:, c, :], og16[:, c * 128:(c + 1) * 128], [R_og16], [R_psT])
                        cp("act", ogT[:, :, t_ * 128:(t_ + 1) * 128], pT4, [R_psT], [R_ogT])

                    units = [(t_, h) for t_ in range(ntile) for h in range(8)]
                    prev = None
                    pending_tr = []
                    for ui, (t_, h) in enumerate(units):
                        allch = scores(t_, h)
                        if prev is not None:
                            pt, ph, pch = prev
                            pv(pt, ph, pch)
                            if ph == 3:
                                epi_half(pt, 0)
                            if ph == 7:
                                epi_half(pt, 1)
                                pending_tr.append((ui + 2, pt))
                        while pending_tr and pending_tr[0][0] <= ui:
                            transposes(pending_tr.pop(0)[1])
                        prev = (t_, h, allch)
                    pt, ph, pch = prev
                    pv(pt, ph, pch)
                    epi_half(pt, 1)
                    pending_tr.append((0, pt))
                    while pending_tr:
                        transposes(pending_tr.pop(0)[1])
                    for i in range(8):
                        pi = 2 + i % 2
                        for c in range(4):
                            mm(ps[pi][:, 0:n], wpp[:, c, i * 128:(i + 1) * 128], ogT[:, c, 0:n], c == 0, c == 3,
                               [R_wp(i * 128), R_ogT], [R_ps[pi]])
                        cp("act" if i % 2 == 1 else "dve", yb16[:, i, 0:n], ps[pi][:, 0:n], [R_ps[pi]], [R_yb])
                    dma("sp", Y_d[1 + kind][:, :, t0:t0 + n], yb16[:, :, 0:n], [R_yb], [R_Y[1 + kind][bi]])

            for a in (BIG, WB, W32, W16):
                a.reset()
            S.barrier()
            gate_bc = W16.take(4096).bitcast(F32).rearrange("p (r n) -> p r n", r=2)
            dg32 = W32.take(128)
            R_dg = Region("dg")
            for r in range(2):
                for k in range(8):
                    ts("dve", dg32, ident32, modsb[:, 16 + k, r:r + 1], None, ALU.mult, None,
                       [R_mod2, R_c], [R_dg])
                    pi = 1 + (k // 4)
                    mm(ps[pi][:, (k % 4) * 128:(k % 4 + 1) * 128], ones32, dg32, True, True,
                       [R_c, R_dg], [R_ps[pi]])
                    if k % 4 == 3:
                        cp("act", gate_bc[:, r, (k // 4) * 512:(k // 4 + 1) * 512], ps[pi][:],
                           [R_ps[pi]], [R_gate])

            R_w = WReg()
            R_wo = WReg()
            wv = WB.take(8 * 3072).rearrange("p (k n) -> p k n", k=8)
            load_w(wv, win_in[l][:, 4864:7936], R_w, order=[0, 2, 4, 1, 3, 5])
            wo = BIG.take(8 * 1024).rearrange("p (k n) -> p k n", k=8)
            load_w(wo, wo_in[l], R_wo)
            Yb = [[BIG.take(4096).rearrange("p (k n) -> p k n", k=8) for _ in range(3)] for _ in range(2)]
            R_Yb = [[Region("Yb%d%d" % (i, j)) for j in range(3)] for i in range(2)]
            yT = BIG.take(4096).rearrange("p (k n) -> p k n", k=8)
            R_yT = Region("yT")
            sgm = [W32.take(512) for _ in range(2)]
            R_sgm = [Region("sgm0"), Region("sgm1")]
            acc = W32.take(512)
            R_acc = Region("acc")
            xt = [W32.take(1024) for _ in range(2)]
            R_xt = [Region("xt0"), Region("xt1")]
            xo = [W32.take(1024) for _ in range(2)]
            R_xo = [Region("xo0"), Region("xo1")]
            tmp = W32.take(512)
            R_tmp = Region("tmp")

            def ld_blk(bi):
                load_hT(bi, bi % 2)
                t0, n = BLOCKS[bi]
                for br in range(3):
                    dma("sp", Yb[bi % 2][br][:, :, 0:n], Y_d[br][:, :, t0:t0 + n], [R_Y[br][bi]], [R_Yb[bi % 2][br]])
            ld_blk(0)
            xcnt = 0
            for bi in range(nblk):
                t0, n = BLOCKS[bi]
                hb = bi % 2
                if bi + 1 < nblk:
                    ld_blk(bi + 1)
                r = 0 if bi < 8 else 1
                for i in range(8):
                    for br in range(3):
                        pi = (i * 3 + br) % 2
                        proj_fm(pi, wv, br * 1024 + i * 128, hb, n, R_w)
                        act(sgm[pi][:, 0:n], ps[pi][:, 0:n], AF.Sigmoid, [R_ps[pi]], [R_sgm[pi]])
                        if br == 0:
                            tt("dve", acc[:, 0:n], sgm[pi][:, 0:n], Yb[hb][br][:, i, 0:n], ALU.mult,
                               [R_sgm[pi], R_Yb[hb][br]], [R_acc])
                        else:
                            tt("pool", sgm[pi][:, 0:n], sgm[pi][:, 0:n], Yb[hb][br][:, i, 0:n], ALU.mult,
                               [R_sgm[pi], R_Yb[hb][br]], [R_sgm[pi]])
                            if br == 1:
                                tt("dve", acc[:, 0:n], acc[:, 0:n], sgm[pi][:, 0:n], ALU.add,
                                   [R_acc, R_sgm[pi]], [R_acc])
                            else:
                                tt("dve", yT[:, i, 0:n], acc[:, 0:n], sgm[pi][:, 0:n], ALU.add,
                                   [R_acc, R_sgm[pi]], [R_yT])
                for t_ in range(n // 128):
                    gt = t0 // 128 + t_
                    xb = xcnt % 2
                    xcnt += 1
                    src = xin[gt * 128:(gt + 1) * 128, :] if gt < 32 else cin[(gt - 32) * 128:(gt - 31) * 128, :]
                    dst = xout[gt * 128:(gt + 1) * 128, :] if gt < 32 else ctx1_d[(gt - 32) * 128:(gt - 31) * 128, :]
                    dma("sp", xt[xb], src, [R_x1[gt]] if l > 0 else [], [R_xt[xb]])
                    for hf in range(2):
                        pi = 2 + hf
                        for i in range(8):
                            mm(ps[pi][:, :], yT[:, i, t_ * 128:(t_ + 1) * 128], wo[:, i, hf * 512:(hf + 1) * 512],
                               i == 0, i == 7, [R_yT, R_wo(hf * 512)], [R_ps[pi]])
                        tt("dve", tmp, ps[pi][:, :], gate_bc[:, r, hf * 512:(hf + 1) * 512], ALU.mult,
                           [R_ps[pi], R_gate], [R_tmp])
                        tt("pool", xo[xb][:, hf * 512:(hf + 1) * 512], tmp, xt[xb][:, hf * 512:(hf + 1) * 512],
                           ALU.add, [R_tmp, R_xt[xb]], [R_xo[xb]])
                    if last:
                        dma("sp", dst, xo[xb], [R_xo[xb]], [], is_output=True)
                    else:
                        dma("sp", dst, xo[xb], [R_xo[xb]], [R_x1[gt]])

        S.finish()
        S.emit_all()
    return nc


_CACHE = {}


def _fm(v, nchunk):
    sh = v.shape[:-1]
    return np.ascontiguousarray(np.swapaxes(v.reshape(*sh, nchunk, 128), -1, -2))


def prep_inputs(inp):
    global _NA_IDX
    if _NA_IDX is None:
        _NA_IDX = _na_index()
    if "cst" not in _CACHE:
        _CACHE["cst"] = _consts()
    cst, rc, rs = _CACHE["cst"]
    f = lambda a: np.ascontiguousarray(np.asarray(a, dtype=np.float32))
    x, c, ctx, c_ctx = f(inp["x"]), f(inp["c"]), f(inp["ctx"]), f(inp["c_ctx"])
    common = {
        "norm_gT": _fm(f(inp["norm_g"]), 8),
        "b_adaT": _fm(f(inp["b_ada"]), 24),
        "w_ada": f(inp["w_ada"]), "w_in": f(inp["w_in"]),
        "w_proj_a": f(inp["w_proj_a"]), "w_proj_b": f(inp["w_proj_b"]), "w_proj_c": f(inp["w_proj_c"]),
        "w_o": f(inp["w_o"]),
        "cst": cst, "rope_c": rc, "rope_s": rs,
    }
    cw = f(inp["conv_w"])
    cwT = np.transpose(cw.reshape(L, 31, 4, 128), (0, 3, 2, 1))
    common["conv_wT"] = np.ascontiguousarray(cwT.reshape(L, 128, 124))
    common["cvecT"] = np.ascontiguousarray(np.concatenate(
        [_fm(f(inp["conv_b"]), 4), _fm(f(inp["cln_g"]), 4), _fm(f(inp["cln_b"]), 4)], axis=-1))
    qk = np.stack([f(inp["na_q_norm"]), f(inp["na_k_norm"]), f(inp["wa_q_norm"]), f(inp["wa_k_norm"])], -1)
    common["qk_norm"] = np.ascontiguousarray(np.concatenate([qk, qk], axis=1))
    common["sink_bc"] = np.ascontiguousarray(np.broadcast_to(f(inp["wa_sink"])[:, None, :], (L, 128, 8)))
    rpb = f(inp["na_rpb"]).reshape(L, 8, 15 * 31)
    rpb_pad = np.concatenate([rpb, np.full((L, 8, 1), NEG, np.float32)], axis=-1)
    tab = rpb_pad[:, :, _NA_IDX]
    tab = np.transpose(tab, (0, 2, 3, 1, 4, 5))
    common["na_tab"] = np.ascontiguousarray(tab.reshape(L, 5, 128, 8 * 640))
    maps = []
    for b in range(8):
        m = dict(common)
        m["x"] = x[b]
        m["ctx"] = ctx[b]
        cc = np.stack([c[b], c_ctx], axis=-1)
        m["ccT"] = np.ascontiguousarray(np.transpose(cc.reshape(8, 128, 2), (1, 0, 2)))
        maps.append(m)
    return maps


def kernel(**inputs):
    if "nc" not in _CACHE:
        _CACHE["nc"] = build()
    maps = prep_inputs(inputs)
    res = run_bass_kernel_spmd(_CACHE["nc"], maps, core_ids=list(range(8)))
    return np.stack([np.asarray(r["out"], dtype=np.float32) for r in res.results], axis=0)
```

```python
import numpy as np
from contextlib import ExitStack
import concourse.bass as bass
import concourse.mybir as mybir
from concourse.bass_utils import run_bass_kernel_spmd

F32 = mybir.dt.float32
BF16 = mybir.dt.bfloat16
AF = mybir.ActivationFunctionType
ALU = mybir.AluOpType

L = 2
S_LAT = 4096
N_CTX = 256
S_ALL = S_LAT + N_CTX
D = 1024
D_IN = 7936
EPS = 1e-6
NEG = -30000.0
ENGS = ("pe", "act", "dve", "pool", "sp")
BLOCKS = [(b * 512, 512) for b in range(8)] + [(S_LAT, N_CTX)]


class Region:
    __slots__ = ("name", "lw", "reads")

    def __init__(self, name=""):
        self.name = name
        self.lw = None
        self.reads = {}


class Sched:
    NDS = 12

    def __init__(self, nc, stack):
        self.nc = nc
        self.ops = {e: [] for e in ENGS}
        self.cnt = {e: 0 for e in ENGS}
        self.semobj = {}
        for e in ENGS:
            self.semobj[("e", e)] = stack.enter_context(nc.semaphore("s_" + e))
        self.dq = {}
        for q in ("sp", "act", "pool"):
            lst = []
            for i in range(self.NDS):
                key = ("d", q, i)
                self.semobj[key] = stack.enter_context(nc.semaphore("d_%s%d" % (q, i)))
                lst.append([key, 0])
            self.dq[q] = [lst, 0]
        self.known = {e: {} for e in ENGS}
        self.out_tokens = []

    def _need(self, eng, tok, waits):
        if tok is None:
            return
        key, val = tok
        if eng == "pe" and key == ("e", "pe"):
            return
        if self.known[eng].get(key, 0) >= val:
            return
        if waits.get(key, 0) < val:
            waits[key] = val

    def _collect(self, eng, reads, writes, waits):
        for r in reads:
            self._need(eng, r.lw, waits)
        for w in writes:
            self._need(eng, w.lw, waits)
            for k, v in w.reads.items():
                self._need(eng, (k, v), waits)
        for k, v in waits.items():
            self.known[eng][k] = v
        return [(self.semobj[k], v) for k, v in waits.items()]

    def op(self, eng, fn, reads=(), writes=()):
        wl = self._collect(eng, reads, writes, {})
        self.cnt[eng] += 1
        seq = self.cnt[eng]
        key = ("e", eng)
        sem = self.semobj[key]
        for r in reads:
            r.reads[key] = seq
        for w in writes:
            w.lw = (key, seq)
            w.reads = {}

        def emit(e):
            for s, v in wl:
                e.wait_ge(s, v)
            fn(e).then_inc(sem, 1)
        self.ops[eng].append(emit)

    def dma(self, q, fn, reads=(), writes=(), is_output=False):
        lst, idx = self.dq[q]
        ent = lst[idx % self.NDS]
        self.dq[q][1] = idx + 1
        key = ent[0]
        waits = {}
        if ent[1] > 0:
            self._need(q, (key, ent[1]), waits)
        wl = self._collect(q, reads, writes, waits)
        ent[1] += 16
        val = ent[1]
        sem = self.semobj[key]
        for r in reads:
            if r.reads.get(key, 0) < val:
                r.reads[key] = val
        for w in writes:
            w.lw = (key, val)
            w.reads = {}
        if is_output:
            self.out_tokens.append((key, val))

        def emit(e):
            for s, v in wl:
                e.wait_ge(s, v)
            fn(e).then_inc(sem, 16)
        self.ops[q].append(emit)

    def barrier(self):
        toks = [(("e", x), self.cnt[x]) for x in ENGS if self.cnt[x] > 0]
        for q in self.dq:
            for key, val in self.dq[q][0]:
                if val > 0:
                    toks.append((key, val))
        for e in ENGS:
            waits = {}
            for key, val in toks:
                if key == ("e", e):
                    continue
                if self.known[e].get(key, 0) < val:
                    waits[key] = val
                    self.known[e][key] = val
            wl = [(self.semobj[k], v) for k, v in waits.items()]
            if wl:
                def emit(eo, wl=wl):
                    for s, v in wl:
                        eo.wait_ge(s, v)
                self.ops[e].append(emit)

    def finish(self):
        final = {}
        for k, v in self.out_tokens:
            final[k] = max(final.get(k, 0), v)
        wl = [(self.semobj[k], v) for k, v in final.items()]

        def emit(e):
            for s, v in wl:
                e.wait_ge(s, v)
        self.ops["sp"].append(emit)

    def emit_all(self):
        with self.nc.Block() as block:
            @block.tensor
            def _(e):
                for f in self.ops["pe"]:
                    f(e)

            @block.scalar
            def _(e):
                for f in self.ops["act"]:
                    f(e)

            @block.vector
            def _(e):
                for f in self.ops["dve"]:
                    f(e)

            @block.gpsimd
            def _(e):
                for f in self.ops["pool"]:
                    f(e)

            @block.sync
            def _(e):
                for f in self.ops["sp"]:
                    f(e)


class Arena:
    def __init__(self, t, size, name):
        self.t, self.size, self.name, self.off = t, size, name, 0

    def reset(self):
        self.off = 0

    def take(self, n, name=""):
        n16 = (n + 15) // 16 * 16
        assert self.off + n16 <= self.size, (self.name, name, self.off, n, self.size)
        ap = self.t[:, self.off:self.off + n]
        self.off += n16
        return ap


NA_TYPES = [None, 0, 2, 60, 62]


def na_chunks(qt):
    i0 = 2 * qt
    if i0 <= 2:
        return 1 + i0 // 2, [0, 1, 2, 3]
    if i0 >= 60:
        return 3 + (i0 - 60) // 2, [28, 29, 30, 31]
    return 0, [qt - 2, qt - 1, qt, qt + 1, qt + 2]


def _na_index():
    idx = np.full((5, 128, 5, 128), 15 * 31, dtype=np.int64)
    for ty in range(5):
        qt = 8 if ty == 0 else NA_TYPES[ty] // 2
        _, chunks = na_chunks(qt)
        for ci, kc in enumerate(chunks):
            for p in range(128):
                r = 2 * kc + p // 64
                c = p % 64
                for n in range(128):
                    i = 2 * qt + n // 64
                    j = n % 64
                    rs = min(max(i - 4, 0), 56)
                    cs = min(max(j - 8, 0), 48)
                    if rs <= r < rs + 8 and cs <= c < cs + 16:
                        idx[ty, p, ci, n] = (r - i + 7) * 31 + (c - j + 15)
    return idx


_NA_IDX = None


def _consts():
    cst = np.zeros((128, 5 * 128 + 384), np.float32)
    cst[:, 0:128] = np.eye(128)
    pm = np.zeros((128, 128), np.float32)
    for m in range(128):
        sub = m % 32
        if sub < 16:
            pm[m + 16, m] = 1.0
        else:
            pm[m - 16, m] = 1.0
    cst[:, 128:256] = pm
    bo = np.zeros((128, 128), np.float32)
    bo[0:64, 0:64] = 1.0 / 64
    bo[64:128, 64:128] = 1.0 / 64
    cst[:, 256:384] = bo
    cst[:, 384:512] = 1.0 / 512
    cst[:, 512:640] = 1.0
    p = np.arange(128)[:, None]
    n = np.arange(128)[None, :]
    cst[:, 640:768] = (n <= p)
    cst[:, 768:896] = 1.0
    cst[:, 896:1024] = (p <= n)
    t = np.arange(S_LAT)
    rowp = (t // 64).astype(np.float32)
    colp = (t % 64).astype(np.float32)
    inv = (10000.0 ** (-np.arange(16, dtype=np.float32) / 16)).astype(np.float32)
    rc = np.zeros((128, S_LAT), np.float32)
    rs = np.zeros((128, S_LAT), np.float32)
    for pp in range(128):
        d = pp % 64
        sub = d % 32
        pos = rowp if d < 32 else colp
        ang = (pos * inv[sub % 16]).astype(np.float32)
        rc[pp] = np.cos(ang)
        rs[pp] = -np.sin(ang) if sub < 16 else np.sin(ang)
    return cst, rc, rs


def build(debug=False):
    nc = bass.Bass("TRN2", target_bir_lowering=False)
    EI = "ExternalInput"
    dbgk = "ExternalOutput" if debug else "Internal"

    def din(name, shape):
        return nc.dram_tensor(name, list(shape), F32, kind=EI).ap()

    x_in = din("x", [S_LAT, D])
    ctx_in = din("ctx", [N_CTX, D])
    ccT_in = din("ccT", [128, 8, 2])
    normg_in = din("norm_gT", [L, 128, 8])
    bada_in = din("b_adaT", [L, 128, 24])
    wada_in = din("w_ada", [L, D, 3 * D])
    win_in = din("w_in", [L, D, D_IN])
    wpa_in = din("w_proj_a", [L, 512, D])
    wpb_in = din("w_proj_b", [L, 512, D])
    wpc_in = din("w_proj_c", [L, 512, D])
    wo_in = din("w_o", [L, D, D])
    convw_in = din("conv_wT", [L, 128, 4 * 31])
    cvec_in = din("cvecT", [L, 128, 12])
    qkn_in = din("qk_norm", [L, 128, 4])
    sink_in = din("sink_bc", [L, 128, 8])
    natab_in = din("na_tab", [L, 5, 128, 8 * 640])
    cst_in = din("cst", [128, 1024])
    ropec_in = din("rope_c", [128, S_LAT])
    ropes_in = din("rope_s", [128, S_LAT])
    out_x = nc.dram_tensor("out", [S_LAT, D], F32, kind="ExternalOutput").ap()

    hT_d = nc.dram_tensor("hT_d", [128, 8, S_ALL], BF16, kind=dbgk).ap()
    Y_d = [nc.dram_tensor("Y%d_d" % i, [128, 8, S_ALL], BF16, kind=dbgk).ap() for i in range(3)]
    x1_d = nc.dram_tensor("x1_d", [S_LAT, D], F32, kind=dbgk).ap()
    ctx1_d = nc.dram_tensor("ctx1_d", [N_CTX, D], F32, kind=dbgk).ap()
    etab_d = nc.dram_tensor("etab_d", [5, 128, 5120], BF16, kind="Internal").ap()

    R_hT = [Region("hT%d" % i) for i in range(34)]
    R_Y = [[Region("Y%d_%d" % (br, b)) for b in range(9)] for br in range(3)]
    R_x1 = [Region("x1_%d" % i) for i in range(34)]
    R_etab = [Region("etab%d" % i) for i in range(5)]

    with ExitStack() as st:
        S = Sched(nc, st)

        def sb(name, shape, dt):
            return st.enter_context(nc.sbuf_tensor(name, list(shape), dt))

        BIG = Arena(sb("BIG", [128, 37120], BF16), 37120, "BIG")
        WB = Arena(sb("WB", [128, 24576], BF16), 24576, "WB")
        W32 = Arena(sb("W32", [128, 7168], F32), 7168, "W32")
        W16 = Arena(sb("W16", [128, 16384], BF16), 16384, "W16")
        HTB = sb("HTB", [128, 2, 8, 512], BF16)
        cst16 = sb("cst16", [128, 1024], BF16)
        cst32 = sb("cst32", [128, 256], F32)
        silc = sb("silc", [128, 8, 2], BF16)
        cc32 = sb("cc32", [128, 8, 2], F32)
        modsb = sb("modsb", [128, 24, 2], F32)
        gsb = sb("gsb", [128, 8, 2], F32)
        normg = sb("normg", [128, 8], F32)
        bada = sb("bada", [128, 24], F32)
        convw = sb("convw", [128, 124], F32)
        cvec = sb("cvec", [128, 12], F32)
        qkn = sb("qkn", [128, 4], F32)
        esink = sb("esink", [128, 8], F32)
        stat = sb("stat", [128, 8], F32)
        hmask = sb("hmask", [128, 2], F32)
        ps = [st.enter_context(nc.psum_tensor("ps%d" % i, [128, 512], F32)) for i in range(7)]
        psT = st.enter_context(nc.psum_tensor("psT", [128, 1024], BF16))
        R_ps = [Region("ps%d" % i) for i in range(7)]
        R_psT = Region("psT")
        R_HTB = [Region("HTB0"), Region("HTB1")]
        R_c = Region("consts")
        R_small = Region("small")
        R_gate = Region("gate_bc")

        ident16 = cst16[:, 0:128]
        pmat16 = cst16[:, 128:256]
        bones16 = cst16[:, 256:384]
        o512_16 = cst16[:, 384:512]
        wamask16 = cst16[:, 640:1024]
        ident32 = cst32[:, 0:128]
        ones32 = cst32[:, 128:256]

        def mm(out, lhsT, rhs, start, stop, rd, wr):
            S.op("pe", lambda e: e.matmul(out, lhsT=lhsT, rhs=rhs, start=start, stop=stop), rd, wr)

        def tr(out, in_, rd, wr):
            S.op("pe", lambda e: e.transpose(out=out, in_=in_, identity=ident16), rd + [R_c], wr)

        def act(out, in_, func, rd, wr, **kw):
            S.op("act", lambda e: e.activation(out=out, in_=in_, func=func, **kw), rd, wr)

        def tt(eng, out, in0, in1, op, rd, wr):
            S.op(eng, lambda e: e.tensor_tensor(out=out, in0=in0, in1=in1, op=op), rd, wr)

        def ts(eng, out, in0, s1, s2, op0, op1, rd, wr):
            if op1 is None:
                S.op(eng, lambda e: e.tensor_scalar(out=out, in0=in0, scalar1=s1, scalar2=None, op0=op0), rd, wr)
            else:
                S.op(eng, lambda e: e.tensor_scalar(out=out, in0=in0, scalar1=s1, scalar2=s2, op0=op0, op1=op1), rd, wr)

        def stt(eng, out, in0, scalar, in1, op0, op1, rd, wr):
            S.op(eng, lambda e: e.scalar_tensor_tensor(out=out, in0=in0, scalar=scalar, in1=in1, op0=op0, op1=op1), rd, wr)

        def cp(eng, out, in_, rd, wr):
            if eng == "act":
                S.op(eng, lambda e: e.activation(out=out, in_=in_, func=AF.Copy), rd, wr)
            else:
                S.op(eng, lambda e: e.tensor_copy(out=out, in_=in_), rd, wr)

        def recip(out, in_, rd, wr):
            S.op("dve", lambda e: e.reciprocal(out=out, in_=in_), rd, wr)

        def mset(eng, ap, val, wr):
            S.op(eng, lambda e: e.memset(ap, val), [], wr)

        def dma(q, out, in_, rd, wr, is_output=False):
            S.dma(q, lambda e: e.dma_start(out=out, in_=in_), rd, wr, is_output=is_output)

        class WReg:
            def __init__(self):
                self.rs = []

            def add(self, a, b):
                r = Region("w%d" % a)
                self.rs.append((a, b, r))
                return r

            def __call__(self, c):
                for a, b, r in self.rs:
                    if a <= c < b:
                        return r
                raise KeyError(c)

        def load_w(dst3, src2, wreg, base=0, order=None, region=None, split_first=False):
            n = src2.shape[1]
            starts = list(range(0, n, 512))
            if order is not None:
                starts = [starts[i] for i in order]
            pieces = []
            for i_, c0 in enumerate(starts):
                cw = min(512, n - c0)
                if split_first and i_ == 0 and cw == 512:
                    pieces += [(c0 + q_ * 128, 128) for q_ in range(4)]
                else:
                    pieces.append((c0, cw))
            for c0, cw in pieces:
                dma("pool", dst3[:, :, c0:c0 + cw],
                    src2[:, c0:c0 + cw].rearrange("(k p) n -> p k n", p=128), [],
                    [region if region is not None else wreg.add(base + c0, base + c0 + cw)])

        def load_hT(bi, buf):
            t0, n = BLOCKS[bi]
            rr = R_hT[t0 // 128:(t0 + n) // 128]
            dma("sp", HTB[:, buf, :, 0:n], hT_d[:, :, t0:t0 + n], rr, [R_HTB[buf]])

        def proj_fm(psi, wview, c0, hbuf, n, wreg):
            for k in range(8):
                mm(ps[psi][:, 0:n], wview[:, k, c0:c0 + 128], HTB[:, hbuf, k, 0:n],
                   k == 0, k == 7, [wreg(c0), R_HTB[hbuf]], [R_ps[psi]])

        def proj_tm(psi, wview, c0, ncols, hbuf, tt_, wreg):
            for k in range(8):
                mm(ps[psi][:, 0:ncols], HTB[:, hbuf, k, tt_ * 128:(tt_ + 1) * 128],
                   wview[:, k, c0:c0 + ncols], k == 0, k == 7, [wreg(c0), R_HTB[hbuf]], [R_ps[psi]])

        R_hm = Region("hmask")
        mset("pool", hmask[:, :], 0.0, [R_hm])
        mset("pool", hmask[0:64, 0:1], 1.0, [R_hm])
        mset("pool", hmask[64:128, 1:2], 1.0, [R_hm])
        dma("pool", cst16[:], cst_in, [], [R_c])
        dma("sp", cst32[:, 0:128], cst_in[:, 0:128], [], [R_c])
        dma("sp", cst32[:, 128:256], cst_in[:, 512:640], [], [R_c])
        dma("sp", cc32[:], ccT_in, [], [R_small])
        act(silc[:], cc32[:], AF.Silu, [R_small], [R_small])

        for l in range(L):
            last = (l == L - 1)
            nblk = 8 if last else 9
            xin = x_in if l == 0 else x1_d
            cin = ctx_in if l == 0 else ctx1_d
            xout = out_x if last else x1_d
            for a in (BIG, WB, W32, W16):
                a.reset()
            S.barrier()

            R_w = WReg()
            wad = WB.take(8 * 3072).rearrange("p (k n) -> p k n", k=8)
            load_w(wad, wada_in[l], R_w, split_first=True)
            dma("sp", normg[:], normg_in[l], [], [R_small])
            dma("sp", bada[:], bada_in[l], [], [R_small])
            dma("sp", convw[:], convw_in[l], [], [R_small])
            dma("sp", cvec[:], cvec_in[l], [], [R_small])
            dma("sp", qkn[:], qkn_in[l], [], [R_small])
            dma("sp", esink[:], sink_in[l], [], [R_small])
            modps = ps[0][:, 0:48].rearrange("p (f r) -> p f r", f=24)
            for f in range(16):
                for k in range(8):
                    mm(modps[:, f, :], wad[:, k, f * 128:(f + 1) * 128], silc[:, k, :],
                       k == 0, k == 7, [R_w(f * 128), R_small], [R_ps[0]])
            tt("dve", modsb[:, 0:16, :], modps[:, 0:16, :], bada[:, 0:16].unsqueeze(2).to_broadcast([128, 16, 2]), ALU.add,
               [R_ps[0], R_small], [R_small])
            ts("dve", gsb[:], modsb[:, 8:16, :], 1.0, None, ALU.add, None, [R_small], [R_small])
            tt("dve", gsb[:], gsb[:], normg[:].unsqueeze(2).to_broadcast([128, 8, 2]), ALU.mult,
               [R_small], [R_small])
            ts("dve", qkn[:, 0:1], qkn[:, 0:1], 0.125, None, ALU.mult, None, [R_small], [R_small])
            ts("dve", qkn[:, 2:3], qkn[:, 2:3], 0.125, None, ALU.mult, None, [R_small], [R_small])
            act(esink[:], esink[:], AF.Exp, [R_small], [R_small])

            W16.reset()
            dgs = BIG.take(124 * 128).rearrange("p (j k n) -> p j k n", j=4, k=31)
            R_dgs = Region("dgs")
            dg_todo = [(j, k) for j in range(4) for k in range(31)]

            def build_dg(cnt):
                for i_ in range(cnt):
                    if dg_todo:
                        j, k = dg_todo.pop(0)
                        if False:
                            ts("dve", dgs[:, j, k, :], ident32, convw[:, j * 31 + k:j * 31 + k + 1], None,
                               ALU.mult, None, [R_c, R_small], [R_dgs])
                        else:
                            act(dgs[:, j, k, :], ident32, AF.Copy, [R_c, R_small], [R_dgs],
                                scale=convw[:, j * 31 + k:j * 31 + k + 1])
            xt = [W32.take(1024) for _ in range(3)]
            t32 = [W32.take(1024) for _ in range(2)]
            junk16 = W16.take(1024)
            xn16 = [W16.take(1024) for _ in range(2)]
            hts16 = [W16.take(1024) for _ in range(2)]
            R_xt = [Region("xt%d" % i) for i in range(3)]
            R_t32, R_junk = [Region("t32a"), Region("t32b")], Region("junk")
            R_xn = [Region("xn0"), Region("xn1")]
            R_hts = [Region("hts0"), Region("hts1")]
            R_st = [Region("st0"), Region("st1")]
            psT3 = psT[:, :].rearrange("p (k n) -> p k n", k=8)

            def a_stages(ti):
                b = ti % 2
                b3 = ti % 3
                r = 0 if ti < 32 else 1
                t32v = t32[b].rearrange("p (k n) -> p k n", k=8)
                htv = hts16[b].rearrange("p (k n) -> p k n", k=8)

                def s0():
                    src = xin[ti * 128:(ti + 1) * 128, :] if ti < 32 else cin[(ti - 32) * 128:(ti - 31) * 128, :]
                    rd = [R_x1[ti]] if l > 0 else []
                    dma("sp", xt[b3], src, rd, [R_xt[b3]])

                def s1():
                    mset("dve", stat[:, b:b + 1], 0.0, [R_st[b]])
                    act(junk16, xt[b3], AF.Square, [R_xt[b3]], [R_junk, R_st[b]], accum_out=stat[:, b:b + 1])
                    act(stat[:, 2 + b:3 + b], stat[:, b:b + 1], AF.Sqrt, [R_st[b]], [R_st[b]],
                        scale=1.0 / D, bias=EPS)
                    recip(stat[:, 4 + b:5 + b], stat[:, 2 + b:3 + b], [R_st[b]], [R_st[b]])
                    ts("dve", xn16[b], xt[b3], stat[:, 4 + b:5 + b], None, ALU.mult, None,
                       [R_xt[b3], R_st[b]], [R_xn[b]])

                def s2():
                    for k in range(8):
                        tr(psT3[:, k, :], xn16[b][:, k * 128:(k + 1) * 128], [R_xn[b]], [R_psT])
                    tt("dve", t32v, psT3, gsb[:, :, r:r + 1].to_broadcast([128, 8, 128]), ALU.mult,
                       [R_psT, R_small], [R_t32[b]])

                def s3():
                    tt("pool", htv, t32v, modsb[:, 0:8, r:r + 1].to_broadcast([128, 8, 128]), ALU.add,
                       [R_t32[b], R_small], [R_hts[b]])
                    dma("sp", hT_d[:, :, ti * 128:(ti + 1) * 128], htv, [R_hts[b]], [R_hT[ti]])
                return [s0, s1, s2, s3]

            a_items = [a_stages(ti) for ti in range(34)]
            for step in range(34 + 3):
                for s_ in range(4):
                    i_ = step - s_
                    if 0 <= i_ < 34:
                        a_items[i_][s_]()
                build_dg(4)
            build_dg(124)

            R_mod2 = Region("mod2")
            modps2 = ps[1][:, 0:16].rearrange("p (f r) -> p f r", f=8)
            for f in range(16, 24):
                for k in range(8):
                    mm(modps2[:, f - 16, :], wad[:, k, f * 128:(f + 1) * 128], silc[:, k, :],
                       k == 0, k == 7, [R_w(f * 128), R_small], [R_ps[1]])
            tt("dve", modsb[:, 16:24, :], modps2, bada[:, 16:24].unsqueeze(2).to_broadcast([128, 8, 2]), ALU.add,
               [R_ps[1], R_small], [R_mod2])
            for a in (WB, W32, W16):
                a.reset()
            S.barrier()
            R_w = WReg()
            R_wp = WReg()
            wv = WB.take(8 * 1536).rearrange("p (k n) -> p k n", k=8)
            wpa = WB.take(4 * 1024).rearrange("p (k n) -> p k n", k=4)
            load_w(wv, win_in[l][:, 0:1536], R_w, split_first=True)
            load_w(wpa, wpa_in[l], R_wp)
            A_lat_f = BIG.take(4 * (S_LAT + 32))
            A_ctx_f = BIG.take(4 * (N_CTX + 32))
            A_lat = A_lat_f.rearrange("p (j n) -> p j n", j=4)
            A_ctx = A_ctx_f.rearrange("p (j n) -> p j n", j=4)
            R_A = Region("A")
            mset("pool", A_lat_f, 0.0, [R_A])
            mset("pool", A_ctx_f, 0.0, [R_A])
            sg32 = [W32.take(512) for _ in range(2)]
            R_sg = [Region("sg0"), Region("sg1")]
            load_hT(0, 0)
            for bi in range(nblk):
                t0, n = BLOCKS[bi]
                hb = bi % 2
                if bi + 1 < nblk:
                    load_hT(bi + 1, 1 - hb)
                Ab, a0 = (A_lat, t0) if bi < 8 else (A_ctx, 0)
                for j in range(4):
                    pa, pb_ = (0, 1) if j % 2 == 0 else (2, 3)
                    proj_fm(pb_, wv, 512 + j * 128, hb, n, R_w)
                    proj_fm(pa, wv, j * 128, hb, n, R_w)
                    sb_ = j % 2
                    act(sg32[sb_][:, 0:n], ps[pb_][:, 0:n], AF.Sigmoid, [R_ps[pb_]], [R_sg[sb_]])
                    tt("dve", Ab[:, j, 15 + a0:15 + a0 + n], ps[pa][:, 0:n], sg32[sb_][:, 0:n], ALU.mult,
                       [R_ps[pa], R_sg[sb_]], [R_A])
            v32 = W32.take(2048).rearrange("p (j n) -> p j n", j=4)
            mean32 = W32.take(512)
            var32 = W32.take(512)
            rstd32 = W32.take(512)
            t1 = [W32.take(512) for _ in range(2)]
            t2 = [W32.take(512) for _ in range(2)]
            sga = sg32
            v16 = W16.take(2048).rearrange("p (j n) -> p j n", j=4)
            sq16 = W16.take(2048).rearrange("p (j n) -> p j n", j=4)
            z16 = W16.take(2048).rearrange("p (j n) -> p j n", j=4)
            ya16 = WB.take(4096).rearrange("p (k n) -> p k n", k=8)
            R_v32, R_v16, R_sq16, R_z16, R_ya = (Region("v32"), Region("v16"), Region("sq16"),
                                                 Region("z16"), Region("ya"))
            R_mean, R_var, R_rstd = Region("mean"), Region("var"), Region("rstd")
            R_v32j = [Region("v32_%d" % j) for j in range(4)]
            NPE = 31
            R_t1 = [Region("t1a"), Region("t1b")]
            R_t2 = [Region("t2a"), Region("t2b")]
            load_hT(0, 0)
            for bi in range(nblk):
                t0, n = BLOCKS[bi]
                hb = bi % 2
                if bi + 1 < nblk:
                    load_hT(bi + 1, 1 - hb)
                Ab, a0 = (A_lat, t0) if bi < 8 else (A_ctx, 0)
                for j in range(4):
                    pi = j % 2
                    for k in range(NPE):
                        mm(ps[pi][:, 0:n], dgs[:, j, k, :], Ab[:, j, a0 + k:a0 + k + n], k == 0, k == NPE - 1,
                           [R_dgs, R_A], [R_ps[pi]])
                    act(v32[:, j, 0:n], ps[pi][:, 0:n], AF.Identity, [R_ps[pi], R_small], [R_v32j[j]],
                        bias=cvec[:, j:j + 1])
                    for k in range(NPE, 31):
                        stt("dve", v32[:, j, 0:n], Ab[:, j, a0 + k:a0 + k + n], convw[:, j * 31 + k:j * 31 + k + 1],
                            v32[:, j, 0:n], ALU.mult, ALU.add, [R_A, R_small, R_v32j[j]], [R_v32j[j]])
                    act(v16[:, j, 0:n], ps[pi][:, 0:n], AF.Identity, [R_ps[pi], R_small], [R_v16],
                        bias=cvec[:, j:j + 1])
                    act(sq16[:, j, 0:n], v32[:, j, 0:n], AF.Square, [R_v32j[j]], [R_sq16])
                for j in range(4):
                    mm(ps[2][:, 0:n], o512_16, v16[:, j, 0:n], j == 0, j == 3, [R_c, R_v16], [R_ps[2]])
                for j in range(4):
                    mm(ps[3][:, 0:n], o512_16, sq16[:, j, 0:n], j == 0, j == 3, [R_c, R_sq16], [R_ps[3]])
                cp("dve", mean32[:, 0:n], ps[2][:, 0:n], [R_ps[2]], [R_mean])
                tt("dve", var32[:, 0:n], mean32[:, 0:n], mean32[:, 0:n], ALU.mult, [R_mean], [R_var])
                tt("dve", var32[:, 0:n], ps[3][:, 0:n], var32[:, 0:n], ALU.subtract, [R_ps[3], R_var], [R_var])
                act(rstd32[:, 0:n], var32[:, 0:n], AF.Sqrt, [R_var], [R_rstd], bias=EPS)
                recip(rstd32[:, 0:n], rstd32[:, 0:n], [R_rstd], [R_rstd])
                for j in range(4):
                    b2 = j % 2
                    pi = 4 + b2
                    proj_fm(pi, wv, 1024 + j * 128, hb, n, R_w)
                    act(sga[b2][:, 0:n], ps[pi][:, 0:n], AF.Silu, [R_ps[pi]], [R_sg[b2]])
                    tt("dve", t1[b2][:, 0:n], v32[:, j, 0:n], mean32[:, 0:n], ALU.subtract,
                       [R_v32j[j], R_mean], [R_t1[b2]])
                    tt("dve", t1[b2][:, 0:n], t1[b2][:, 0:n], rstd32[:, 0:n], ALU.mult,
                       [R_t1[b2], R_rstd], [R_t1[b2]])
                    act(t2[b2][:, 0:n], t1[b2][:, 0:n], AF.Silu, [R_t1[b2], R_small], [R_t2[b2]],
                        scale=cvec[:, 4 + j:5 + j], bias=cvec[:, 8 + j:9 + j])
                    tt("pool" if j % 2 == 0 else "dve", z16[:, j, 0:n], t2[b2][:, 0:n], sga[b2][:, 0:n], ALU.mult,
                       [R_t2[b2], R_sg[b2]], [R_z16])
                for i in range(8):
                    pi = i % 2
                    for j in range(4):
                        mm(ps[pi][:, 0:n], wpa[:, j, i * 128:(i + 1) * 128], z16[:, j, 0:n], j == 0, j == 3,
                           [R_wp(i * 128), R_z16], [R_ps[pi]])
                    if i % 2 == 1:
                        cp("act", ya16[:, i, 0:n], ps[pi][:, 0:n], [R_ps[pi]], [R_ya])
                    else:
                        cp("dve", ya16[:, i, 0:n], ps[pi][:, 0:n], [R_ps[pi]], [R_ya])
                dma("sp", Y_d[0][:, :, t0:t0 + n], ya16[:, :, 0:n], [R_ya], [R_Y[0][bi]])

            for kind in (0, 1):
                for a in (BIG, WB, W32, W16):
                    a.reset()
                S.barrier()
                is_wa = kind == 1
                R_w = WReg()
                R_wp = WReg()
                if not is_wa:
                    ncol = 2048
                    wv = WB.take(8 * ncol).rearrange("p (k n) -> p k n", k=8)
                    load_w(wv, win_in[l][:, 1536:3584], R_w, order=[1, 2, 0, 3], split_first=True)
                    cq, ck, cv_, cg = 0, 512, 1024, 1536
                    nkc, nkv = 4, 8
                    gq, gk = qkn[:, 0:1], qkn[:, 1:2]
                    wp_src = wpb_in[l]
                else:
                    ncol = 1408
                    wv = WB.take(8 * ncol).rearrange("p (k n) -> p k n", k=8)
                    for g in range(2):
                        rg = R_w.add(512 + g * 128, 512 + (g + 1) * 128)
                        for dup in range(2):
                            c0 = 512 + g * 128 + dup * 64
                            load_w(wv[:, :, c0:c0 + 64], win_in[l][:, 4096 + g * 64:4096 + (g + 1) * 64], R_w, region=rg)
                    load_w(wv[:, :, 768:896], win_in[l][:, 4224:4352], R_w, base=768)
                    load_w(wv[:, :, 0:512], win_in[l][:, 3584:4096], R_w, base=0)
                    load_w(wv[:, :, 896:1408], win_in[l][:, 4352:4864], R_w, base=896)
                    cq, ck, cv_, cg = 0, 512, 768, 896
                    nkc, nkv = 2, 2
                    gq, gk = qkn[:, 2:3], qkn[:, 3:4]
                    wp_src = wpc_in[l]
                wpp = WB.take(4 * 1024).rearrange("p (k n) -> p k n", k=4)
                load_w(wpp, wp_src, R_wp)
                KT = BIG.take(nkc * S_ALL).rearrange("p (c n) -> p c n", c=nkc)
                VV_f = BIG.take(34 * nkv * 65)
                VV = VV_f.rearrange("p (t h d) -> p t h d", t=34, h=nkv)
                R_KT, R_VV = Region("KT"), Region("VV")
                mset("pool", VV_f, 1.0, [R_VV])
                R_tab = Region("tab")
                R_etb = Region("etb")
                prep_thunks = []
                sgt = [W32.take(512) for _ in range(4)]
                R_sgt = [Region("sgt%d" % i) for i in range(4)]
                if not is_wa:
                    tab16_f = W16.take(5120)
                    tab16 = tab16_f.rearrange("p (h n) -> p h n", h=8)
                    def mk_piece(ty, pc):
                        def f():
                            sb_i = pc % 4
                            dma("sp", sgt[sb_i], natab_in[l, ty][:, pc * 512:(pc + 1) * 512], [], [R_sgt[sb_i]])
                            act(tab16_f[:, pc * 512:(pc + 1) * 512], sgt[sb_i], AF.Exp, [R_sgt[sb_i]], [R_tab])
                            if pc == 9:
                                dma("sp", etab_d[ty], tab16_f, [R_tab], [R_etab[ty]])
                        return f
                    prep_thunks = [mk_piece(ty, pc) for ty in range(5) for pc in range(10)]
                sq16 = [W16.take(512) for _ in range(2)]
                R_sq = [Region("sq0"), Region("sq1")]
                sd32 = [W32.take(512) for _ in range(2)]
                R_sd = [Region("sd0"), Region("sd1")]
                rc32 = W32.take(512)
                rs32 = W32.take(512)
                R_rope = Region("rope")
                if is_wa:
                    kn16 = [W16.take(512) for _ in range(2)]
                    R_kn = [Region("kn0"), Region("kn1")]
                    ra32 = [W32.take(512) for _ in range(2)]
                    rb32 = [W32.take(512) for _ in range(2)]
                    R_ra = [Region("ra0"), Region("ra1")]
                    R_rb = [Region("rb0"), Region("rb1")]

                def skew(items):
                    ns = max(len(it) for it in items)
                    for step in range(len(items) + ns - 1):
                        for s_ in range(ns):
                            i_ = step - s_
                            if 0 <= i_ < len(items) and s_ < len(items[i_]):
                                items[i_][s_]()

                def normed_stages(idx, psi, wcol, hb, n, gain, out16, R_out, rope, post=None):
                    b = idx % 2
                    pst = 6

                    def s1():
                        proj_fm(psi, wv, wcol, hb, n, R_w)
                        act(sq16[b][:, 0:n], ps[psi][:, 0:n], AF.Square, [R_ps[psi]], [R_sq[b]])

                    def s2():
                        mm(ps[pst][:, 0:n], bones16, sq16[b][:, 0:n], True, True, [R_c, R_sq[b]], [R_ps[pst]])
                        act(sd32[b][:, 0:n], ps[pst][:, 0:n], AF.Sqrt, [R_ps[pst]], [R_sd[b]], bias=EPS)
                        recip(sd32[b][:, 0:n], sd32[b][:, 0:n], [R_sd[b]], [R_sd[b]])
                        if not rope:
                            stt("dve", out16, ps[psi][:, 0:n], gain, sd32[b][:, 0:n], ALU.mult, ALU.mult,
                                [R_ps[psi], R_small, R_sd[b]], [R_out])
                        else:
                            stt("dve", kn16[b][:, 0:n], ps[psi][:, 0:n], gain, sd32[b][:, 0:n], ALU.mult, ALU.mult,
                                [R_ps[psi], R_small, R_sd[b]], [R_kn[b]])

                    def s3():
                        mm(ps[pst][:, 0:n], pmat16, kn16[b][:, 0:n], True, True, [R_c, R_kn[b]], [R_ps[pst]])
                        tt("pool", ra32[b][:, 0:n], kn16[b][:, 0:n], rc32[:, 0:n], ALU.mult, [R_kn[b], R_rope], [R_ra[b]])
                        tt("dve", rb32[b][:, 0:n], ps[pst][:, 0:n], rs32[:, 0:n], ALU.mult, [R_ps[pst], R_rope], [R_rb[b]])
                        tt("pool", out16, ra32[b][:, 0:n], rb32[b][:, 0:n], ALU.add, [R_ra[b], R_rb[b]], [R_out])
                    st_ = [s1, s2, s3] if rope else [s1, s2]
                    if post is not None:
                        st_.append(post)
                    return st_

                load_hT(0, 0)
                for bi in range(9):
                    t0, n = BLOCKS[bi]
                    hb = bi % 2
                    if bi + 1 < 9:
                        load_hT(bi + 1, 1 - hb)
                    rope = is_wa and bi < 8
                    if rope:
                        dma("sp", rc32[:, 0:n], ropec_in[:, t0:t0 + n], [], [R_rope])
                        dma("sp", rs32[:, 0:n], ropes_in[:, t0:t0 + n], [], [R_rope])
                    items = []
                    for c in range(nkc):
                        items.append(normed_stages(c, c, ck + c * 128, hb, n, gk, KT[:, c, t0:t0 + n], R_KT, rope))
                    for t_ in range(n // 128):
                        def v1(t_=t_):
                            pi = 4 + t_ % 2
                            proj_tm(pi, wv, cv_, nkv * 64, hb, t_, R_w)

                        def v2(t_=t_):
                            pi = 4 + t_ % 2
                            gt = t0 // 128 + t_
                            src = ps[pi][:, 0:nkv * 64].rearrange("p (h d) -> p h d", h=nkv)
                            cp("act" if t_ % 2 == 0 else "dve", VV[:, gt, :, 0:64], src, [R_ps[pi]], [R_VV])
                        items.append([v1, v2])
                    skew(items)
                    for _ in range(7):
                        if prep_thunks:
                            prep_thunks.pop(0)()
                while prep_thunks:
                    prep_thunks.pop(0)()

                QT = W16.take(2048).rearrange("p (c n) -> p c n", c=4)
                R_QT = Region("QT")
                QTz = W16.take(4096).rearrange("p (c e n) -> p c e n", c=4, e=2)
                R_QTz = Region("QTz")
                R_QTc = [Region("QTc%d" % c) for c in range(4)]
                ogT = W16.take(2048).rearrange("p (c n) -> p c n", c=4)
                R_ogT = Region("ogT")
                og16 = W16.take(512)
                R_og16 = Region("og16")
                Pw = [W16.take(640) for _ in range(2)]
                R_Pw = [Region("Pw0"), Region("Pw1")]
                R_Pw5 = [Region("Pw5_0"), Region("Pw5_1")]
                PcE = [BIG.take(384) for _ in range(2)]
                R_Pc = [Region("Pc0"), Region("Pc1")]
                Ew = [BIG.take(512) for _ in range(2)]
                R_Ew = [Region("Ew0"), Region("Ew1")]
                og32 = W32.take(512)
                R_og32 = Region("og32")
                yb16 = WB.take(4096).rearrange("p (k n) -> p k n", k=8)
                R_yb = Region("yb")
                R_rec = Region("rec")
                cur_edge = [-1]
                load_hT(0, 0)
                for bi in range(nblk):
                    t0, n = BLOCKS[bi]
                    hb = bi % 2
                    if bi + 1 < nblk:
                        load_hT(bi + 1, 1 - hb)
                    rope = is_wa and bi < 8
                    if rope:
                        dma("sp", rc32[:, 0:n], ropec_in[:, t0:t0 + n], [], [R_rope])
                        dma("sp", rs32[:, 0:n], ropes_in[:, t0:t0 + n], [], [R_rope])
                    ntile = n // 128
                    items = []
                    for c in range(4):
                        def qz(c=c):
                            for e_ in range(2):
                                act(QTz[:, c, e_, 0:n], QT[:, c, 0:n], AF.Copy, [R_QTc[c], R_hm], [R_QTz],
                                    scale=hmask[:, e_:e_ + 1])
                        items.append(normed_stages(c, c, cq + c * 128, hb, n, gq, QT[:, c, 0:n], R_QTc[c], rope, post=qz))
                    for t_ in range(ntile):
                        def g1(t_=t_):
                            proj_tm(4 + t_ % 2, wv, cg, 512, hb, t_, R_w)

                        def g2(t_=t_):
                            pi = 4 + t_ % 2
                            act(sgt[t_], ps[pi][:, 0:512], AF.Silu, [R_ps[pi]], [R_sgt[t_]])
                        items.append([g1, g2])
                    skew(items)

                    def scores(t_, h):
                        qt = t0 // 128 + t_
                        if bi == 8:
                            wch, tabv, R_tb = [], None, None
                        elif not is_wa:
                            ty, wch = na_chunks(qt)
                            if cur_edge[0] != ty:
                                dma("sp", tab16_f, etab_d[ty], [R_etab[ty]], [R_tab])
                                cur_edge[0] = ty
                            tabv, R_tb = tab16, R_tab
                        else:
                            wch = [c_ for c_ in (qt - 1, qt, qt + 1) if 0 <= c_ < 32]
                            moff = 128 if qt == 0 else 0
                            tabv, R_tb = None, R_c
                        cch = [32, 33]
                        nw = len(wch)
                        c2 = h // 2
                        pb = 64 * (h % 2)
                        kc = (h // 4) if is_wa else c2
                        hp = h % 2
                        pw_i, pc_i = (2, 4) if hp == 0 else (3, 0)
                        qv = QTz[:, c2, h % 2, t_ * 128:(t_ + 1) * 128]
                        for i_, kch in enumerate(cch):
                            mm(ps[pc_i][:, i_ * 128:(i_ + 1) * 128],
                               KT[:, kc, kch * 128:(kch + 1) * 128], qv, True, True,
                               [R_KT, R_QTz], [R_ps[pc_i]])
                        if nw == 5:
                            kch = wch[4]
                            mm(ps[pc_i][:, 256:384],
                               KT[:, kc, kch * 128:(kch + 1) * 128], qv, True, True,
                               [R_KT, R_QTz], [R_ps[pc_i]])
                        nce = 384 if nw == 5 else 256
                        act(PcE[hp][:, 0:nce], ps[pc_i][:, 0:nce], AF.Exp, [R_ps[pc_i]], [R_Pc[hp]])
                        for i_, kch in enumerate(wch[:4]):
                            mm(ps[pw_i][:, i_ * 128:(i_ + 1) * 128],
                               KT[:, kc, kch * 128:(kch + 1) * 128], qv, True, True,
                               [R_KT, R_QTz], [R_ps[pw_i]])
                        if nw:
                            n4 = min(nw, 4) * 128
                            if is_wa:
                                mk = wamask16[:, moff:moff + nw * 128]
                            else:
                                mk = tabv[:, h, 0:nw * 128]
                            if nw == 5:
                                tt("dve", Pw[hp][:, 512:640], PcE[hp][:, 256:384], mk[:, 512:640], ALU.mult,
                                   [R_Pc[hp], R_tb], [R_Pw5[hp]])
                            act(Ew[hp][:, 0:n4], ps[pw_i][:, 0:n4], AF.Exp, [R_ps[pw_i]], [R_Ew[hp]])
                            tt("dve", Pw[hp][:, 0:n4], Ew[hp][:, 0:n4], mk[:, 0:n4], ALU.mult,
                               [R_Ew[hp], R_tb], [R_Pw[hp]])
                        allch = [(PcE[hp][:, i_ * 128:(i_ + 1) * 128], kch, R_Pc[hp]) for i_, kch in enumerate(cch)]
                        if nw == 5:
                            allch.append((Pw[hp][:, 512:640], wch[4], R_Pw5[hp]))
                        allch += [(Pw[hp][:, i_ * 128:(i_ + 1) * 128], kch, R_Pw[hp]) for i_, kch in enumerate(wch[:4])]
                        return allch

                    def pv(t_, h, allch):
                        vh = (h // 4) if is_wa else h
                        opi = 5 if h < 4 else 1
                        ov = ps[opi][:, 0:260].rearrange("p (h d) -> p h d", h=4)[:, h % 4, :]
                        for i_, (pap, kch, rg) in enumerate(allch):
                            mm(ov, pap, VV[:, kch, vh, :], i_ == 0, i_ == len(allch) - 1, [rg, R_VV], [R_ps[opi]])

                    def epi_half(t_, g4):
                        opi = 5 if g4 == 0 else 1
                        o3 = ps[opi][:, 0:260].rearrange("p (h d) -> p h d", h=4)
                        rec = stat[:, 0:4]
                        if is_wa:
                            tt("dve", rec, o3[:, :, 64], esink[:, g4 * 4:(g4 + 1) * 4], ALU.add,
                               [R_ps[opi], R_small], [R_rec])
                            recip(rec, rec, [R_rec], [R_rec])
                        else:
                            recip(rec, o3[:, :, 64], [R_ps[opi]], [R_rec])
                        o32 = og32[:, 0:256].rearrange("p (h d) -> p h d", h=4)
                        tt("dve", o32, o3[:, :, 0:64], rec.unsqueeze(2).to_broadcast([128, 4, 64]), ALU.mult,
                           [R_ps[opi], R_rec], [R_og32])
                        tt("pool", og16[:, g4 * 256:(g4 + 1) * 256], og32[:, 0:256],
                           sgt[t_][:, g4 * 256:(g4 + 1) * 256], ALU.mult, [R_og32, R_sgt[t_]], [R_og16])

                    def transposes(t_):
                        pT4 = psT[:, 0:512].rearrange("p (c n) -> p c n", c=4)
                        for c in range(4):
                            tr(pT4[:, c, :], og16[:, c * 128:(c + 1) * 128], [R_og16], [R_psT])
                        cp("act", ogT[:, :, t_ * 128:(t_ + 1) * 128], pT4, [R_psT], [R_ogT])

                    units = [(t_, h) for t_ in range(ntile) for h in range(8)]
                    prev = None
                    pending_tr = []
                    for ui, (t_, h) in enumerate(units):
                        allch = scores(t_, h)
                        if prev is not None:
                            pt, ph, pch = prev
                            pv(pt, ph, pch)
                            if ph == 3:
                                epi_half(pt, 0)
                            if ph == 7:
                                epi_half(pt, 1)
                                pending_tr.append((ui + 2, pt))
                        while pending_tr and pending_tr[0][0] <= ui:
                            transposes(pending_tr.pop(0)[1])
                        prev = (t_, h, allch)
                    pt, ph, pch = prev
                    pv(pt, ph, pch)
                    epi_half(pt, 1)
                    pending_tr.append((0, pt))
                    while pending_tr:
                        transposes(pending_tr.pop(0)[1])
                    for i in range(8):
                        pi = 2 + i % 2
                        for c in range(4):
                            mm(ps[pi][:, 0:n], wpp[:, c, i * 128:(i + 1) * 128], ogT[:, c, 0:n], c == 0, c == 3,
                               [R_wp(i * 128), R_ogT], [R_ps[pi]])
                        cp("act" if i % 2 == 1 else "dve", yb16[:, i, 0:n], ps[pi][:, 0:n], [R_ps[pi]], [R_yb])
                    dma("sp", Y_d[1 + kind][:, :, t0:t0 + n], yb16[:, :, 0:n], [R_yb], [R_Y[1 + kind][bi]])

            for a in (BIG, WB, W32, W16):
                a.reset()
            S.barrier()
            gate_bc = W16.take(4096).bitcast(F32).rearrange("p (r n) -> p r n", r=2)
            dg32 = W32.take(128)
            R_dg = Region("dg")
            for r in range(2):
                for k in range(8):
                    ts("dve", dg32, ident32, modsb[:, 16 + k, r:r + 1], None, ALU.mult, None,
                       [R_mod2, R_c], [R_dg])
                    pi = 1 + (k // 4)
                    mm(ps[pi][:, (k % 4) * 128:(k % 4 + 1) * 128], ones32, dg32, True, True,
                       [R_c, R_dg], [R_ps[pi]])
                    if k % 4 == 3:
                        cp("act", gate_bc[:, r, (k // 4) * 512:(k // 4 + 1) * 512], ps[pi][:],
                           [R_ps[pi]], [R_gate])

            R_w = WReg()
            R_wo = WReg()
            wv = WB.take(8 * 3072).rearrange("p (k n) -> p k n", k=8)
            load_w(wv, win_in[l][:, 4864:7936], R_w, order=[0, 2, 4, 1, 3, 5], split_first=True)
            wo = BIG.take(8 * 1024).rearrange("p (k n) -> p k n", k=8)
            load_w(wo, wo_in[l], R_wo)
            Yb = [[BIG.take(4096).rearrange("p (k n) -> p k n", k=8) for _ in range(3)] for _ in range(2)]
            R_Yb = [[Region("Yb%d%d" % (i, j)) for j in range(3)] for i in range(2)]
            yT = BIG.take(4096).rearrange("p (k n) -> p k n", k=8)
            R_yT = Region("yT")
            sgm = [W32.take(512) for _ in range(2)]
            R_sgm = [Region("sgm0"), Region("sgm1")]
            acc = W32.take(512)
            R_acc = Region("acc")
            xt = [W32.take(1024) for _ in range(2)]
            R_xt = [Region("xt0"), Region("xt1")]
            xo = [W32.take(1024) for _ in range(2)]
            R_xo = [Region("xo0"), Region("xo1")]
            tmp = W32.take(512)
            R_tmp = Region("tmp")

            def ld_blk(bi):
                load_hT(bi, bi % 2)
                t0, n = BLOCKS[bi]
                for br in range(3):
                    dma("sp", Yb[bi % 2][br][:, :, 0:n], Y_d[br][:, :, t0:t0 + n], [R_Y[br][bi]], [R_Yb[bi % 2][br]])
            ld_blk(0)
            xcnt = 0
            for bi in range(nblk):
                t0, n = BLOCKS[bi]
                hb = bi % 2
                if bi + 1 < nblk:
                    ld_blk(bi + 1)
                r = 0 if bi < 8 else 1
                for i in range(8):
                    for br in range(3):
                        pi = (i * 3 + br) % 2
                        proj_fm(pi, wv, br * 1024 + i * 128, hb, n, R_w)
                        act(sgm[pi][:, 0:n], ps[pi][:, 0:n], AF.Sigmoid, [R_ps[pi]], [R_sgm[pi]])
                        if br == 0:
                            tt("dve", acc[:, 0:n], sgm[pi][:, 0:n], Yb[hb][br][:, i, 0:n], ALU.mult,
                               [R_sgm[pi], R_Yb[hb][br]], [R_acc])
                        else:
                            tt("pool", sgm[pi][:, 0:n], sgm[pi][:, 0:n], Yb[hb][br][:, i, 0:n], ALU.mult,
                               [R_sgm[pi], R_Yb[hb][br]], [R_sgm[pi]])
                            if br == 1:
                                tt("dve", acc[:, 0:n], acc[:, 0:n], sgm[pi][:, 0:n], ALU.add,
                                   [R_acc, R_sgm[pi]], [R_acc])
                            else:
                                tt("dve", yT[:, i, 0:n], acc[:, 0:n], sgm[pi][:, 0:n], ALU.add,
                                   [R_acc, R_sgm[pi]], [R_yT])
                for t_ in range(n // 128):
                    gt = t0 // 128 + t_
                    xb = xcnt % 2
                    xcnt += 1
                    src = xin[gt * 128:(gt + 1) * 128, :] if gt < 32 else cin[(gt - 32) * 128:(gt - 31) * 128, :]
                    dst = xout[gt * 128:(gt + 1) * 128, :] if gt < 32 else ctx1_d[(gt - 32) * 128:(gt - 31) * 128, :]
                    dma("sp", xt[xb], src, [R_x1[gt]] if l > 0 else [], [R_xt[xb]])
                    for hf in range(2):
                        pi = 2 + hf
                        for i in range(8):
                            mm(ps[pi][:, :], yT[:, i, t_ * 128:(t_ + 1) * 128], wo[:, i, hf * 512:(hf + 1) * 512],
                               i == 0, i == 7, [R_yT, R_wo(hf * 512)], [R_ps[pi]])
                        tt("dve", tmp, ps[pi][:, :], gate_bc[:, r, hf * 512:(hf + 1) * 512], ALU.mult,
                           [R_ps[pi], R_gate], [R_tmp])
                        tt("pool", xo[xb][:, hf * 512:(hf + 1) * 512], tmp, xt[xb][:, hf * 512:(hf + 1) * 512],
                           ALU.add, [R_tmp, R_xt[xb]], [R_xo[xb]])
                    if last:
                        dma("sp", dst, xo[xb], [R_xo[xb]], [], is_output=True)
                    else:
                        dma("sp", dst, xo[xb], [R_xo[xb]], [R_x1[gt]])

        S.finish()
        S.emit_all()
    return nc


_CACHE = {}


def _fm(v, nchunk):
    sh = v.shape[:-1]
    return np.ascontiguousarray(np.swapaxes(v.reshape(*sh, nchunk, 128), -1, -2))


def prep_inputs(inp):
    global _NA_IDX
    if _NA_IDX is None:
        _NA_IDX = _na_index()
    if "cst" not in _CACHE:
        _CACHE["cst"] = _consts()
    cst, rc, rs = _CACHE["cst"]
    f = lambda a: np.ascontiguousarray(np.asarray(a, dtype=np.float32))
    x, c, ctx, c_ctx = f(inp["x"]), f(inp["c"]), f(inp["ctx"]), f(inp["c_ctx"])
    common = {
        "norm_gT": _fm(f(inp["norm_g"]), 8),
        "b_adaT": _fm(f(inp["b_ada"]), 24),
        "w_ada": f(inp["w_ada"]), "w_in": f(inp["w_in"]),
        "w_proj_a": f(inp["w_proj_a"]), "w_proj_b": f(inp["w_proj_b"]), "w_proj_c": f(inp["w_proj_c"]),
        "w_o": f(inp["w_o"]),
        "cst": cst, "rope_c": rc, "rope_s": rs,
    }
    cw = f(inp["conv_w"])
    cwT = np.transpose(cw.reshape(L, 31, 4, 128), (0, 3, 2, 1))
    common["conv_wT"] = np.ascontiguousarray(cwT.reshape(L, 128, 124))
    common["cvecT"] = np.ascontiguousarray(np.concatenate(
        [_fm(f(inp["conv_b"]), 4), _fm(f(inp["cln_g"]), 4), _fm(f(inp["cln_b"]), 4)], axis=-1))
    qk = np.stack([f(inp["na_q_norm"]), f(inp["na_k_norm"]), f(inp["wa_q_norm"]), f(inp["wa_k_norm"])], -1)
    common["qk_norm"] = np.ascontiguousarray(np.concatenate([qk, qk], axis=1))
    common["sink_bc"] = np.ascontiguousarray(np.broadcast_to(f(inp["wa_sink"])[:, None, :], (L, 128, 8)))
    rpb = f(inp["na_rpb"]).reshape(L, 8, 15 * 31)
    rpb_pad = np.concatenate([rpb, np.full((L, 8, 1), NEG, np.float32)], axis=-1)
    tab = rpb_pad[:, :, _NA_IDX]
    tab = np.transpose(tab, (0, 2, 3, 1, 4, 5))
    common["na_tab"] = np.ascontiguousarray(tab.reshape(L, 5, 128, 8 * 640))
    maps = []
    for b in range(8):
        m = dict(common)
        m["x"] = x[b]
        m["ctx"] = ctx[b]
        cc = np.stack([c[b], c_ctx], axis=-1)
        m["ccT"] = np.ascontiguousarray(np.transpose(cc.reshape(8, 128, 2), (1, 0, 2)))
        maps.append(m)
    return maps


def kernel(**inputs):
    if "nc" not in _CACHE:
        _CACHE["nc"] = build()
    maps = prep_inputs(inputs)
    res = run_bass_kernel_spmd(_CACHE["nc"], maps, core_ids=list(range(8)))
    return np.stack([np.asarray(r["out"], dtype=np.float32) for r in res.results], axis=0)
```

```python
import numpy as np
from contextlib import ExitStack
import concourse.bass as bass
import concourse.mybir as mybir
from concourse.bass_utils import run_bass_kernel_spmd

F32 = mybir.dt.float32
BF16 = mybir.dt.bfloat16
AF = mybir.ActivationFunctionType
ALU = mybir.AluOpType

L = 2
S_LAT = 4096
N_CTX = 256
S_ALL = S_LAT + N_CTX
D = 1024
D_IN = 7936
EPS = 1e-6
NEG = -30000.0
ENGS = ("pe", "act", "dve", "pool", "sp")
BLOCKS = [(b * 512, 512) for b in range(8)] + [(S_LAT, N_CTX)]


class Region:
    __slots__ = ("name", "lw", "reads")

    def __init__(self, name=""):
        self.name = name
        self.lw = None
        self.reads = {}


class Sched:
    NDS = 12

    def __init__(self, nc, stack):
        self.nc = nc
        self.ops = {e: [] for e in ENGS}
        self.cnt = {e: 0 for e in ENGS}
        self.semobj = {}
        for e in ENGS:
            self.semobj[("e", e)] = stack.enter_context(nc.semaphore("s_" + e))
        self.dq = {}
        for q in ("sp", "act", "pool"):
            lst = []
            for i in range(self.NDS):
                key = ("d", q, i)
                self.semobj[key] = stack.enter_context(nc.semaphore("d_%s%d" % (q, i)))
                lst.append([key, 0])
            self.dq[q] = [lst, 0]
        self.known = {e: {} for e in ENGS}
        self.out_tokens = []

    def _need(self, eng, tok, waits):
        if tok is None:
            return
        key, val = tok
        if eng == "pe" and key == ("e", "pe"):
            return
        if self.known[eng].get(key, 0) >= val:
            return
        if waits.get(key, 0) < val:
            waits[key] = val

    def _collect(self, eng, reads, writes, waits):
        for r in reads:
            self._need(eng, r.lw, waits)
        for w in writes:
            self._need(eng, w.lw, waits)
            for k, v in w.reads.items():
                self._need(eng, (k, v), waits)
        for k, v in waits.items():
            self.known[eng][k] = v
        return [(self.semobj[k], v) for k, v in waits.items()]

    def op(self, eng, fn, reads=(), writes=()):
        wl = self._collect(eng, reads, writes, {})
        self.cnt[eng] += 1
        seq = self.cnt[eng]
        key = ("e", eng)
        sem = self.semobj[key]
        for r in reads:
            r.reads[key] = seq
        for w in writes:
            w.lw = (key, seq)
            w.reads = {}

        def emit(e):
            for s, v in wl:
                e.wait_ge(s, v)
            fn(e).then_inc(sem, 1)
        self.ops[eng].append(emit)

    def dma(self, q, fn, reads=(), writes=(), is_output=False):
        lst, idx = self.dq[q]
        ent = lst[idx % self.NDS]
        self.dq[q][1] = idx + 1
        key = ent[0]
        waits = {}
        if ent[1] > 0:
            self._need(q, (key, ent[1]), waits)
        wl = self._collect(q, reads, writes, waits)
        ent[1] += 16
        val = ent[1]
        sem = self.semobj[key]
        for r in reads:
            if r.reads.get(key, 0) < val:
                r.reads[key] = val
        for w in writes:
            w.lw = (key, val)
            w.reads = {}
        if is_output:
            self.out_tokens.append((key, val))

        def emit(e):
            for s, v in wl:
                e.wait_ge(s, v)
            fn(e).then_inc(sem, 16)
        self.ops[q].append(emit)

    def barrier(self):
        toks = [(("e", x), self.cnt[x]) for x in ENGS if self.cnt[x] > 0]
        for q in self.dq:
            for key, val in self.dq[q][0]:
                if val > 0:
                    toks.append((key, val))
        for e in ENGS:
            waits = {}
            for key, val in toks:
                if key == ("e", e):
                    continue
                if self.known[e].get(key, 0) < val:
                    waits[key] = val
                    self.known[e][key] = val
            wl = [(self.semobj[k], v) for k, v in waits.items()]
            if wl:
                def emit(eo, wl=wl):
                    for s, v in wl:
                        eo.wait_ge(s, v)
                self.ops[e].append(emit)

    def finish(self):
        final = {}
        for k, v in self.out_tokens:
            final[k] = max(final.get(k, 0), v)
        wl = [(self.semobj[k], v) for k, v in final.items()]

        def emit(e):
            for s, v in wl:
                e.wait_ge(s, v)
        self.ops["sp"].append(emit)

    def emit_all(self):
        with self.nc.Block() as block:
            @block.tensor
            def _(e):
                for f in self.ops["pe"]:
                    f(e)

            @block.scalar
            def _(e):
                for f in self.ops["act"]:
                    f(e)

            @block.vector
            def _(e):
                for f in self.ops["dve"]:
                    f(e)

            @block.gpsimd
            def _(e):
                for f in self.ops["pool"]:
                    f(e)

            @block.sync
            def _(e):
                for f in self.ops["sp"]:
                    f(e)


class Arena:
    def __init__(self, t, size, name):
        self.t, self.size, self.name, self.off = t, size, name, 0

    def reset(self):
        self.off = 0

    def take(self, n, name=""):
        n16 = (n + 15) // 16 * 16
        assert self.off + n16 <= self.size, (self.name, name, self.off, n, self.size)
        ap = self.t[:, self.off:self.off + n]
        self.off += n16
        return ap


NA_TYPES = [None, 0, 2, 60, 62]


def na_chunks(qt):
    i0 = 2 * qt
    if i0 <= 2:
        return 1 + i0 // 2, [0, 1, 2, 3]
    if i0 >= 60:
        return 3 + (i0 - 60) // 2, [28, 29, 30, 31]
    return 0, [qt - 2, qt - 1, qt, qt + 1, qt + 2]


def _na_index():
    idx = np.full((5, 128, 5, 128), 15 * 31, dtype=np.int64)
    for ty in range(5):
        qt = 8 if ty == 0 else NA_TYPES[ty] // 2
        _, chunks = na_chunks(qt)
        for ci, kc in enumerate(chunks):
            for p in range(128):
                r = 2 * kc + p // 64
                c = p % 64
                for n in range(128):
                    i = 2 * qt + n // 64
                    j = n % 64
                    rs = min(max(i - 4, 0), 56)
                    cs = min(max(j - 8, 0), 48)
                    if rs <= r < rs + 8 and cs <= c < cs + 16:
                        idx[ty, p, ci, n] = (r - i + 7) * 31 + (c - j + 15)
    return idx


_NA_IDX = None


def _consts():
    cst = np.zeros((128, 5 * 128 + 384), np.float32)
    cst[:, 0:128] = np.eye(128)
    pm = np.zeros((128, 128), np.float32)
    for m in range(128):
        sub = m % 32
        if sub < 16:
            pm[m + 16, m] = 1.0
        else:
            pm[m - 16, m] = 1.0
    cst[:, 128:256] = pm
    bo = np.zeros((128, 128), np.float32)
    bo[0:64, 0:64] = 1.0 / 64
    bo[64:128, 64:128] = 1.0 / 64
    cst[:, 256:384] = bo
    cst[:, 384:512] = 1.0 / 512
    cst[:, 512:640] = 1.0
    p = np.arange(128)[:, None]
    n = np.arange(128)[None, :]
    cst[:, 640:768] = (n <= p)
    cst[:, 768:896] = 1.0
    cst[:, 896:1024] = (p <= n)
    t = np.arange(S_LAT)
    rowp = (t // 64).astype(np.float32)
    colp = (t % 64).astype(np.float32)
    inv = (10000.0 ** (-np.arange(16, dtype=np.float32) / 16)).astype(np.float32)
    rc = np.zeros((128, S_LAT), np.float32)
    rs = np.zeros((128, S_LAT), np.float32)
    for pp in range(128):
        d = pp % 64
        sub = d % 32
        pos = rowp if d < 32 else colp
        ang = (pos * inv[sub % 16]).astype(np.float32)
        rc[pp] = np.cos(ang)
        rs[pp] = -np.sin(ang) if sub < 16 else np.sin(ang)
    return cst, rc, rs


def build(debug=False):
    nc = bass.Bass("TRN2", target_bir_lowering=False)
    EI = "ExternalInput"
    dbgk = "ExternalOutput" if debug else "Internal"

    def din(name, shape):
        return nc.dram_tensor(name, list(shape), F32, kind=EI).ap()

    x_in = din("x", [S_LAT, D])
    ctx_in = din("ctx", [N_CTX, D])
    ccT_in = din("ccT", [128, 8, 2])
    normg_in = din("norm_gT", [L, 128, 8])
    bada_in = din("b_adaT", [L, 128, 24])
    wada_in = din("w_ada", [L, D, 3 * D])
    win_in = din("w_in", [L, D, D_IN])
    wpa_in = din("w_proj_a", [L, 512, D])
    wpb_in = din("w_proj_b", [L, 512, D])
    wpc_in = din("w_proj_c", [L, 512, D])
    wo_in = din("w_o", [L, D, D])
    convw_in = din("conv_wT", [L, 128, 4 * 31])
    cvec_in = din("cvecT", [L, 128, 12])
    qkn_in = din("qk_norm", [L, 128, 4])
    sink_in = din("sink_bc", [L, 128, 8])
    natab_in = din("na_tab", [L, 5, 128, 8 * 640])
    cst_in = din("cst", [128, 1024])
    ropec_in = din("rope_c", [128, S_LAT])
    ropes_in = din("rope_s", [128, S_LAT])
    out_x = nc.dram_tensor("out", [S_LAT, D], F32, kind="ExternalOutput").ap()

    hT_d = nc.dram_tensor("hT_d", [128, 8, S_ALL], BF16, kind=dbgk).ap()
    Y_d = [nc.dram_tensor("Y%d_d" % i, [128, 8, S_ALL], BF16, kind=dbgk).ap() for i in range(3)]
    x1_d = nc.dram_tensor("x1_d", [S_LAT, D], F32, kind=dbgk).ap()
    ctx1_d = nc.dram_tensor("ctx1_d", [N_CTX, D], F32, kind=dbgk).ap()
    etab_d = nc.dram_tensor("etab_d", [5, 128, 5120], BF16, kind="Internal").ap()

    R_hT = [Region("hT%d" % i) for i in range(34)]
    R_Y = [[Region("Y%d_%d" % (br, b)) for b in range(9)] for br in range(3)]
    R_x1 = [Region("x1_%d" % i) for i in range(34)]
    R_etab = [Region("etab%d" % i) for i in range(5)]

    with ExitStack() as st:
        S = Sched(nc, st)

        def sb(name, shape, dt):
            return st.enter_context(nc.sbuf_tensor(name, list(shape), dt))

        BIG = Arena(sb("BIG", [128, 37120], BF16), 37120, "BIG")
        WB = Arena(sb("WB", [128, 24576], BF16), 24576, "WB")
        W32 = Arena(sb("W32", [128, 7168], F32), 7168, "W32")
        W16 = Arena(sb("W16", [128, 16384], BF16), 16384, "W16")
        HTB = sb("HTB", [128, 2, 8, 512], BF16)
        cst16 = sb("cst16", [128, 1024], BF16)
        cst32 = sb("cst32", [128, 256], F32)
        silc = sb("silc", [128, 8, 2], BF16)
        cc32 = sb("cc32", [128, 8, 2], F32)
        modsb = sb("modsb", [128, 24, 2], F32)
        gsb = sb("gsb", [128, 8, 2], F32)
        normg = sb("normg", [128, 8], F32)
        bada = sb("bada", [128, 24], F32)
        convw = sb("convw", [128, 124], F32)
        cvec = sb("cvec", [128, 12], F32)
        qkn = sb("qkn", [128, 4], F32)
        esink = sb("esink", [128, 8], F32)
        stat = sb("stat", [128, 8], F32)
        hmask = sb("hmask", [128, 2], F32)
        ps = [st.enter_context(nc.psum_tensor("ps%d" % i, [128, 512], F32)) for i in range(7)]
        psT = st.enter_context(nc.psum_tensor("psT", [128, 1024], BF16))
        R_ps = [Region("ps%d" % i) for i in range(7)]
        R_psT = Region("psT")
        R_HTB = [Region("HTB0"), Region("HTB1")]
        R_c = Region("consts")
        R_small = Region("small")
        R_gate = Region("gate_bc")

        ident16 = cst16[:, 0:128]
        pmat16 = cst16[:, 128:256]
        bones16 = cst16[:, 256:384]
        o512_16 = cst16[:, 384:512]
        wamask16 = cst16[:, 640:1024]
        ident32 = cst32[:, 0:128]
        ones32 = cst32[:, 128:256]

        def mm(out, lhsT, rhs, start, stop, rd, wr):
            S.op("pe", lambda e: e.matmul(out, lhsT=lhsT, rhs=rhs, start=start, stop=stop), rd, wr)

        def tr(out, in_, rd, wr):
            S.op("pe", lambda e: e.transpose(out=out, in_=in_, identity=ident16), rd + [R_c], wr)

        def act(out, in_, func, rd, wr, **kw):
            S.op("act", lambda e: e.activation(out=out, in_=in_, func=func, **kw), rd, wr)

        def tt(eng, out, in0, in1, op, rd, wr):
            S.op(eng, lambda e: e.tensor_tensor(out=out, in0=in0, in1=in1, op=op), rd, wr)

        def ts(eng, out, in0, s1, s2, op0, op1, rd, wr):
            if op1 is None:
                S.op(eng, lambda e: e.tensor_scalar(out=out, in0=in0, scalar1=s1, scalar2=None, op0=op0), rd, wr)
            else:
                S.op(eng, lambda e: e.tensor_scalar(out=out, in0=in0, scalar1=s1, scalar2=s2, op0=op0, op1=op1), rd, wr)

        def stt(eng, out, in0, scalar, in1, op0, op1, rd, wr):
            S.op(eng, lambda e: e.scalar_tensor_tensor(out=out, in0=in0, scalar=scalar, in1=in1, op0=op0, op1=op1), rd, wr)

        def cp(eng, out, in_, rd, wr):
            if eng == "act":
                S.op(eng, lambda e: e.activation(out=out, in_=in_, func=AF.Copy), rd, wr)
            else:
                S.op(eng, lambda e: e.tensor_copy(out=out, in_=in_), rd, wr)

        def recip(out, in_, rd, wr):
            S.op("dve", lambda e: e.reciprocal(out=out, in_=in_), rd, wr)

        def mset(eng, ap, val, wr):
            S.op(eng, lambda e: e.memset(ap, val), [], wr)

        def dma(q, out, in_, rd, wr, is_output=False):
            S.dma(q, lambda e: e.dma_start(out=out, in_=in_), rd, wr, is_output=is_output)

        class WReg:
            def __init__(self):
                self.rs = []

            def add(self, a, b):
                r = Region("w%d" % a)
                self.rs.append((a, b, r))
                return r

            def __call__(self, c):
                for a, b, r in self.rs:
                    if a <= c < b:
                        return r
                raise KeyError(c)

        def load_w(dst3, src2, wreg, base=0, order=None, region=None, split_first=False):
            n = src2.shape[1]
            starts = list(range(0, n, 512))
            if order is not None:
                starts = [starts[i] for i in order]
            pieces = []
            for i_, c0 in enumerate(starts):
                cw = min(512, n - c0)
                if split_first and i_ == 0 and cw == 512:
                    pieces += [(c0 + q_ * 128, 128) for q_ in range(4)]
                else:
                    pieces.append((c0, cw))
            for c0, cw in pieces:
                dma("pool", dst3[:, :, c0:c0 + cw],
                    src2[:, c0:c0 + cw].rearrange("(k p) n -> p k n", p=128), [],
                    [region if region is not None else wreg.add(base + c0, base + c0 + cw)])

        def load_hT(bi, buf):
            t0, n = BLOCKS[bi]
            rr = R_hT[t0 // 128:(t0 + n) // 128]
            dma("sp", HTB[:, buf, :, 0:n], hT_d[:, :, t0:t0 + n], rr, [R_HTB[buf]])

        def proj_fm(psi, wview, c0, hbuf, n, wreg):
            for k in range(8):
                mm(ps[psi][:, 0:n], wview[:, k, c0:c0 + 128], HTB[:, hbuf, k, 0:n],
                   k == 0, k == 7, [wreg(c0), R_HTB[hbuf]], [R_ps[psi]])

        def proj_tm(psi, wview, c0, ncols, hbuf, tt_, wreg):
            for k in range(8):
                mm(ps[psi][:, 0:ncols], HTB[:, hbuf, k, tt_ * 128:(tt_ + 1) * 128],
                   wview[:, k, c0:c0 + ncols], k == 0, k == 7, [wreg(c0), R_HTB[hbuf]], [R_ps[psi]])

        R_hm = Region("hmask")
        mset("pool", hmask[:, :], 0.0, [R_hm])
        mset("pool", hmask[0:64, 0:1], 1.0, [R_hm])
        mset("pool", hmask[64:128, 1:2], 1.0, [R_hm])
        dma("pool", cst16[:], cst_in, [], [R_c])
        dma("sp", cst32[:, 0:128], cst_in[:, 0:128], [], [R_c])
        dma("sp", cst32[:, 128:256], cst_in[:, 512:640], [], [R_c])
        dma("sp", cc32[:], ccT_in, [], [R_small])
        act(silc[:], cc32[:], AF.Silu, [R_small], [R_small])

        for l in range(L):
            last = (l == L - 1)
            nblk = 8 if last else 9
            xin = x_in if l == 0 else x1_d
            cin = ctx_in if l == 0 else ctx1_d
            xout = out_x if last else x1_d
            for a in (BIG, WB, W32, W16):
                a.reset()
            S.barrier()

            R_w = WReg()
            wad = WB.take(8 * 3072).rearrange("p (k n) -> p k n", k=8)
            load_w(wad, wada_in[l], R_w, split_first=True)
            dma("sp", normg[:], normg_in[l], [], [R_small])
            dma("sp", bada[:], bada_in[l], [], [R_small])
            dma("sp", convw[:], convw_in[l], [], [R_small])
            dma("sp", cvec[:], cvec_in[l], [], [R_small])
            dma("sp", qkn[:], qkn_in[l], [], [R_small])
            dma("sp", esink[:], sink_in[l], [], [R_small])
            modps = ps[0][:, 0:48].rearrange("p (f r) -> p f r", f=24)
            for f in range(16):
                for k in range(8):
                    mm(modps[:, f, :], wad[:, k, f * 128:(f + 1) * 128], silc[:, k, :],
                       k == 0, k == 7, [R_w(f * 128), R_small], [R_ps[0]])
            tt("dve", modsb[:, 0:16, :], modps[:, 0:16, :], bada[:, 0:16].unsqueeze(2).to_broadcast([128, 16, 2]), ALU.add,
               [R_ps[0], R_small], [R_small])
            ts("dve", gsb[:], modsb[:, 8:16, :], 1.0, None, ALU.add, None, [R_small], [R_small])
            tt("dve", gsb[:], gsb[:], normg[:].unsqueeze(2).to_broadcast([128, 8, 2]), ALU.mult,
               [R_small], [R_small])
            ts("dve", qkn[:, 0:1], qkn[:, 0:1], 0.125, None, ALU.mult, None, [R_small], [R_small])
            ts("dve", qkn[:, 2:3], qkn[:, 2:3], 0.125, None, ALU.mult, None, [R_small], [R_small])
            act(esink[:], esink[:], AF.Exp, [R_small], [R_small])

            W16.reset()
            dgs = BIG.take(124 * 128).rearrange("p (j k n) -> p j k n", j=4, k=31)
            R_dgs = Region("dgs")
            dg_todo = [(j, k) for j in range(4) for k in range(31)]

            def build_dg(cnt):
                for i_ in range(cnt):
                    if dg_todo:
                        j, k = dg_todo.pop(0)
                        if False:
                            ts("dve", dgs[:, j, k, :], ident32, convw[:, j * 31 + k:j * 31 + k + 1], None,
                               ALU.mult, None, [R_c, R_small], [R_dgs])
                        else:
                            act(dgs[:, j, k, :], ident32, AF.Copy, [R_c, R_small], [R_dgs],
                                scale=convw[:, j * 31 + k:j * 31 + k + 1])
            xt = [W32.take(1024) for _ in range(3)]
            t32 = [W32.take(1024) for _ in range(2)]
            junk16 = W16.take(1024)
            xn16 = [W16.take(1024) for _ in range(2)]
            hts16 = [W16.take(1024) for _ in range(2)]
            R_xt = [Region("xt%d" % i) for i in range(3)]
            R_t32, R_junk = [Region("t32a"), Region("t32b")], Region("junk")
            R_xn = [Region("xn0"), Region("xn1")]
            R_hts = [Region("hts0"), Region("hts1")]
            R_st = [Region("st0"), Region("st1")]
            psT3 = psT[:, :].rearrange("p (k n) -> p k n", k=8)

            def a_stages(ti):
                b = ti % 2
                b3 = ti % 3
                r = 0 if ti < 32 else 1
                t32v = t32[b].rearrange("p (k n) -> p k n", k=8)
                htv = hts16[b].rearrange("p (k n) -> p k n", k=8)

                def s0():
                    src = xin[ti * 128:(ti + 1) * 128, :] if ti < 32 else cin[(ti - 32) * 128:(ti - 31) * 128, :]
                    rd = [R_x1[ti]] if l > 0 else []
                    dma("sp", xt[b3], src, rd, [R_xt[b3]])

                def s1():
                    mset("dve", stat[:, b:b + 1], 0.0, [R_st[b]])
                    act(junk16, xt[b3], AF.Square, [R_xt[b3]], [R_junk, R_st[b]], accum_out=stat[:, b:b + 1])
                    act(stat[:, 2 + b:3 + b], stat[:, b:b + 1], AF.Sqrt, [R_st[b]], [R_st[b]],
                        scale=1.0 / D, bias=EPS)
                    recip(stat[:, 4 + b:5 + b], stat[:, 2 + b:3 + b], [R_st[b]], [R_st[b]])
                    ts("dve", xn16[b], xt[b3], stat[:, 4 + b:5 + b], None, ALU.mult, None,
                       [R_xt[b3], R_st[b]], [R_xn[b]])

                def s2():
                    for k in range(8):
                        tr(psT3[:, k, :], xn16[b][:, k * 128:(k + 1) * 128], [R_xn[b]], [R_psT])
                    tt("dve", t32v, psT3, gsb[:, :, r:r + 1].to_broadcast([128, 8, 128]), ALU.mult,
                       [R_psT, R_small], [R_t32[b]])

                def s3():
                    tt("pool", htv, t32v, modsb[:, 0:8, r:r + 1].to_broadcast([128, 8, 128]), ALU.add,
                       [R_t32[b], R_small], [R_hts[b]])
                    dma("sp", hT_d[:, :, ti * 128:(ti + 1) * 128], htv, [R_hts[b]], [R_hT[ti]])
                return [s0, s1, s2, s3]

            a_items = [a_stages(ti) for ti in range(34)]
            for step in range(34 + 3):
                for s_ in range(4):
                    i_ = step - s_
                    if 0 <= i_ < 34:
                        a_items[i_][s_]()
                build_dg(4)
            build_dg(124)

            R_mod2 = Region("mod2")
            modps2 = ps[1][:, 0:16].rearrange("p (f r) -> p f r", f=8)
            for f in range(16, 24):
                for k in range(8):
                    mm(modps2[:, f - 16, :], wad[:, k, f * 128:(f + 1) * 128], silc[:, k, :],
                       k == 0, k == 7, [R_w(f * 128), R_small], [R_ps[1]])
            tt("dve", modsb[:, 16:24, :], modps2, bada[:, 16:24].unsqueeze(2).to_broadcast([128, 8, 2]), ALU.add,
               [R_ps[1], R_small], [R_mod2])
            for a in (WB, W32, W16):
                a.reset()
            S.barrier()
            R_w = WReg()
            R_wp = WReg()
            wv = WB.take(8 * 1536).rearrange("p (k n) -> p k n", k=8)
            wpa = WB.take(4 * 1024).rearrange("p (k n) -> p k n", k=4)
            load_w(wv, win_in[l][:, 0:1536], R_w, split_first=True)
            load_w(wpa, wpa_in[l], R_wp)
            A_lat_f = BIG.take(4 * (S_LAT + 32))
            A_ctx_f = BIG.take(4 * (N_CTX + 32))
            A_lat = A_lat_f.rearrange("p (j n) -> p j n", j=4)
            A_ctx = A_ctx_f.rearrange("p (j n) -> p j n", j=4)
            R_A = Region("A")
            mset("pool", A_lat_f, 0.0, [R_A])
            mset("pool", A_ctx_f, 0.0, [R_A])
            sg32 = [W32.take(512) for _ in range(2)]
            R_sg = [Region("sg0"), Region("sg1")]
            load_hT(0, 0)
            for bi in range(nblk):
                t0, n = BLOCKS[bi]
                hb = bi % 2
                if bi + 1 < nblk:
                    load_hT(bi + 1, 1 - hb)
                Ab, a0 = (A_lat, t0) if bi < 8 else (A_ctx, 0)
                for j in range(4):
                    pa, pb_ = (0, 1) if j % 2 == 0 else (2, 3)
                    proj_fm(pb_, wv, 512 + j * 128, hb, n, R_w)
                    proj_fm(pa, wv, j * 128, hb, n, R_w)
                    sb_ = j % 2
                    act(sg32[sb_][:, 0:n], ps[pb_][:, 0:n], AF.Sigmoid, [R_ps[pb_]], [R_sg[sb_]])
                    tt("dve", Ab[:, j, 15 + a0:15 + a0 + n], ps[pa][:, 0:n], sg32[sb_][:, 0:n], ALU.mult,
                       [R_ps[pa], R_sg[sb_]], [R_A])
            v32 = W32.take(2048).rearrange("p (j n) -> p j n", j=4)
            mean32 = W32.take(512)
            var32 = W32.take(512)
            rstd32 = W32.take(512)
            t1 = [W32.take(512) for _ in range(2)]
            t2 = [W32.take(512) for _ in range(2)]
            sga = sg32
            v16 = W16.take(2048).rearrange("p (j n) -> p j n", j=4)
            sq16 = W16.take(2048).rearrange("p (j n) -> p j n", j=4)
            z16 = W16.take(2048).rearrange("p (j n) -> p j n", j=4)
            ya16 = WB.take(4096).rearrange("p (k n) -> p k n", k=8)
            R_v32, R_v16, R_sq16, R_z16, R_ya = (Region("v32"), Region("v16"), Region("sq16"),
                                                 Region("z16"), Region("ya"))
            R_mean, R_var, R_rstd = Region("mean"), Region("var"), Region("rstd")
            R_v32j = [Region("v32_%d" % j) for j in range(4)]
            NPE = 31
            R_t1 = [Region("t1a"), Region("t1b")]
            R_t2 = [Region("t2a"), Region("t2b")]
            load_hT(0, 0)
            for bi in range(nblk):
                t0, n = BLOCKS[bi]
                hb = bi % 2
                if bi + 1 < nblk:
                    load_hT(bi + 1, 1 - hb)
                Ab, a0 = (A_lat, t0) if bi < 8 else (A_ctx, 0)
                for j in range(4):
                    pi = j % 2
                    for k in range(NPE):
                        mm(ps[pi][:, 0:n], dgs[:, j, k, :], Ab[:, j, a0 + k:a0 + k + n], k == 0, k == NPE - 1,
                           [R_dgs, R_A], [R_ps[pi]])
                    act(v32[:, j, 0:n], ps[pi][:, 0:n], AF.Identity, [R_ps[pi], R_small], [R_v32j[j]],
                        bias=cvec[:, j:j + 1])
                    for k in range(NPE, 31):
                        stt("dve", v32[:, j, 0:n], Ab[:, j, a0 + k:a0 + k + n], convw[:, j * 31 + k:j * 31 + k + 1],
                            v32[:, j, 0:n], ALU.mult, ALU.add, [R_A, R_small, R_v32j[j]], [R_v32j[j]])
                    act(v16[:, j, 0:n], ps[pi][:, 0:n], AF.Identity, [R_ps[pi], R_small], [R_v16],
                        bias=cvec[:, j:j + 1])
                    act(sq16[:, j, 0:n], v32[:, j, 0:n], AF.Square, [R_v32j[j]], [R_sq16])
                for j in range(4):
                    mm(ps[2][:, 0:n], o512_16, v16[:, j, 0:n], j == 0, j == 3, [R_c, R_v16], [R_ps[2]])
                for j in range(4):
                    mm(ps[3][:, 0:n], o512_16, sq16[:, j, 0:n], j == 0, j == 3, [R_c, R_sq16], [R_ps[3]])
                cp("dve", mean32[:, 0:n], ps[2][:, 0:n], [R_ps[2]], [R_mean])
                tt("dve", var32[:, 0:n], mean32[:, 0:n], mean32[:, 0:n], ALU.mult, [R_mean], [R_var])
                tt("dve", var32[:, 0:n], ps[3][:, 0:n], var32[:, 0:n], ALU.subtract, [R_ps[3], R_var], [R_var])
                act(rstd32[:, 0:n], var32[:, 0:n], AF.Sqrt, [R_var], [R_rstd], bias=EPS)
                recip(rstd32[:, 0:n], rstd32[:, 0:n], [R_rstd], [R_rstd])
                for j in range(4):
                    b2 = j % 2
                    pi = 4 + b2
                    proj_fm(pi, wv, 1024 + j * 128, hb, n, R_w)
                    act(sga[b2][:, 0:n], ps[pi][:, 0:n], AF.Silu, [R_ps[pi]], [R_sg[b2]])
                    tt("dve", t1[b2][:, 0:n], v32[:, j, 0:n], mean32[:, 0:n], ALU.subtract,
                       [R_v32j[j], R_mean], [R_t1[b2]])
                    tt("dve", t1[b2][:, 0:n], t1[b2][:, 0:n], rstd32[:, 0:n], ALU.mult,
                       [R_t1[b2], R_rstd], [R_t1[b2]])
                    act(t2[b2][:, 0:n], t1[b2][:, 0:n], AF.Silu, [R_t1[b2], R_small], [R_t2[b2]],
                        scale=cvec[:, 4 + j:5 + j], bias=cvec[:, 8 + j:9 + j])
                    tt("pool" if j % 2 == 0 else "dve", z16[:, j, 0:n], t2[b2][:, 0:n], sga[b2][:, 0:n], ALU.mult,
                       [R_t2[b2], R_sg[b2]], [R_z16])
                for i in range(8):
                    pi = i % 2
                    for j in range(4):
                        mm(ps[pi][:, 0:n], wpa[:, j, i * 128:(i + 1) * 128], z16[:, j, 0:n], j == 0, j == 3,
                           [R_wp(i * 128), R_z16], [R_ps[pi]])
                    if i % 2 == 1:
                        cp("act", ya16[:, i, 0:n], ps[pi][:, 0:n], [R_ps[pi]], [R_ya])
                    else:
                        cp("dve", ya16[:, i, 0:n], ps[pi][:, 0:n], [R_ps[pi]], [R_ya])
                dma("sp", Y_d[0][:, :, t0:t0 + n], ya16[:, :, 0:n], [R_ya], [R_Y[0][bi]])

            for kind in (0, 1):
                for a in (BIG, WB, W32, W16):
                    a.reset()
                S.barrier()
                is_wa = kind == 1
                R_w = WReg()
                R_wp = WReg()
                if not is_wa:
                    ncol = 2048
                    wv = WB.take(8 * ncol).rearrange("p (k n) -> p k n", k=8)
                    load_w(wv, win_in[l][:, 1536:3584], R_w, order=[1, 2, 0, 3], split_first=True)
                    cq, ck, cv_, cg = 0, 512, 1024, 1536
                    nkc, nkv = 4, 8
                    gq, gk = qkn[:, 0:1], qkn[:, 1:2]
                    wp_src = wpb_in[l]
                else:
                    ncol = 1408
                    wv = WB.take(8 * ncol).rearrange("p (k n) -> p k n", k=8)
                    for g in range(2):
                        rg = R_w.add(512 + g * 128, 512 + (g + 1) * 128)
                        for dup in range(2):
                            c0 = 512 + g * 128 + dup * 64
                            load_w(wv[:, :, c0:c0 + 64], win_in[l][:, 4096 + g * 64:4096 + (g + 1) * 64], R_w, region=rg)
                    load_w(wv[:, :, 768:896], win_in[l][:, 4224:4352], R_w, base=768)
                    load_w(wv[:, :, 0:512], win_in[l][:, 3584:4096], R_w, base=0)
                    load_w(wv[:, :, 896:1408], win_in[l][:, 4352:4864], R_w, base=896)
                    cq, ck, cv_, cg = 0, 512, 768, 896
                    nkc, nkv = 2, 2
                    gq, gk = qkn[:, 2:3], qkn[:, 3:4]
                    wp_src = wpc_in[l]
                wpp = WB.take(4 * 1024).rearrange("p (k n) -> p k n", k=4)
                load_w(wpp, wp_src, R_wp)
                KT = BIG.take(nkc * S_ALL).rearrange("p (c n) -> p c n", c=nkc)
                VV_f = BIG.take(34 * nkv * 65)
                VV = VV_f.rearrange("p (t h d) -> p t h d", t=34, h=nkv)
                R_KT, R_VV = Region("KT"), Region("VV")
                mset("pool", VV_f, 1.0, [R_VV])
                R_tab = Region("tab")
                R_etb = Region("etb")
                prep_thunks = []
                sgt = [W32.take(512) for _ in range(4)]
                R_sgt = [Region("sgt%d" % i) for i in range(4)]
                if not is_wa:
                    tab16_f = W16.take(5120)
                    tab16 = tab16_f.rearrange("p (h n) -> p h n", h=8)
                    def mk_piece(ty, pc):
                        def f():
                            sb_i = pc % 4
                            dma("sp", sgt[sb_i], natab_in[l, ty][:, pc * 512:(pc + 1) * 512], [], [R_sgt[sb_i]])
                            act(tab16_f[:, pc * 512:(pc + 1) * 512], sgt[sb_i], AF.Exp, [R_sgt[sb_i]], [R_tab])
                            if pc == 9:
                                dma("sp", etab_d[ty], tab16_f, [R_tab], [R_etab[ty]])
                        return f
                    prep_thunks = [mk_piece(ty, pc) for ty in range(5) for pc in range(10)]
                sq16 = [W16.take(512) for _ in range(2)]
                R_sq = [Region("sq0"), Region("sq1")]
                sd32 = [W32.take(512) for _ in range(2)]
                R_sd = [Region("sd0"), Region("sd1")]
                rc32 = W32.take(512)
                rs32 = W32.take(512)
                R_rope = Region("rope")
                if is_wa:
                    kn16 = [W16.take(512) for _ in range(2)]
                    R_kn = [Region("kn0"), Region("kn1")]
                    ra32 = [W32.take(512) for _ in range(2)]
                    rb32 = [W32.take(512) for _ in range(2)]
                    R_ra = [Region("ra0"), Region("ra1")]
                    R_rb = [Region("rb0"), Region("rb1")]

                def skew(items):
                    ns = max(len(it) for it in items)
                    for step in range(len(items) + ns - 1):
                        for s_ in range(ns):
                            i_ = step - s_
                            if 0 <= i_ < len(items) and s_ < len(items[i_]):
                                items[i_][s_]()

                def normed_stages(idx, psi, wcol, hb, n, gain, out16, R_out, rope, post=None):
                    b = idx % 2
                    pst = 6

                    def s1():
                        proj_fm(psi, wv, wcol, hb, n, R_w)
                        act(sq16[b][:, 0:n], ps[psi][:, 0:n], AF.Square, [R_ps[psi]], [R_sq[b]])

                    def s2():
                        mm(ps[pst][:, 0:n], bones16, sq16[b][:, 0:n], True, True, [R_c, R_sq[b]], [R_ps[pst]])
                        act(sd32[b][:, 0:n], ps[pst][:, 0:n], AF.Sqrt, [R_ps[pst]], [R_sd[b]], bias=EPS)
                        recip(sd32[b][:, 0:n], sd32[b][:, 0:n], [R_sd[b]], [R_sd[b]])
                        if not rope:
                            stt("dve", out16, ps[psi][:, 0:n], gain, sd32[b][:, 0:n], ALU.mult, ALU.mult,
                                [R_ps[psi], R_small, R_sd[b]], [R_out])
                        else:
                            stt("dve", kn16[b][:, 0:n], ps[psi][:, 0:n], gain, sd32[b][:, 0:n], ALU.mult, ALU.mult,
                                [R_ps[psi], R_small, R_sd[b]], [R_kn[b]])

                    def s3():
                        mm(ps[pst][:, 0:n], pmat16, kn16[b][:, 0:n], True, True, [R_c, R_kn[b]], [R_ps[pst]])
                        tt("pool", ra32[b][:, 0:n], kn16[b][:, 0:n], rc32[:, 0:n], ALU.mult, [R_kn[b], R_rope], [R_ra[b]])
                        tt("dve", rb32[b][:, 0:n], ps[pst][:, 0:n], rs32[:, 0:n], ALU.mult, [R_ps[pst], R_rope], [R_rb[b]])
                        tt("pool", out16, ra32[b][:, 0:n], rb32[b][:, 0:n], ALU.add, [R_ra[b], R_rb[b]], [R_out])
                    st_ = [s1, s2, s3] if rope else [s1, s2]
                    if post is not None:
                        st_.append(post)
                    return st_

                load_hT(0, 0)
                for bi in range(9):
                    t0, n = BLOCKS[bi]
                    hb = bi % 2
                    if bi + 1 < 9:
                        load_hT(bi + 1, 1 - hb)
                    rope = is_wa and bi < 8
                    if rope:
                        dma("sp", rc32[:, 0:n], ropec_in[:, t0:t0 + n], [], [R_rope])
                        dma("sp", rs32[:, 0:n], ropes_in[:, t0:t0 + n], [], [R_rope])
                    items = []
                    for c in range(nkc):
                        items.append(normed_stages(c, c, ck + c * 128, hb, n, gk, KT[:, c, t0:t0 + n], R_KT, rope))
                    for t_ in range(n // 128):
                        def v1(t_=t_):
                            pi = 4 + t_ % 2
                            proj_tm(pi, wv, cv_, nkv * 64, hb, t_, R_w)

                        def v2(t_=t_):
                            pi = 4 + t_ % 2
                            gt = t0 // 128 + t_
                            src = ps[pi][:, 0:nkv * 64].rearrange("p (h d) -> p h d", h=nkv)
                            cp("act" if t_ % 2 == 0 else "dve", VV[:, gt, :, 0:64], src, [R_ps[pi]], [R_VV])
                        items.append([v1, v2])
                    skew(items)
                    for _ in range(7):
                        if prep_thunks:
                            prep_thunks.pop(0)()
                while prep_thunks:
                    prep_thunks.pop(0)()

                QT = W16.take(2048).rearrange("p (c n) -> p c n", c=4)
                R_QT = Region("QT")
                QTz = W16.take(4096).rearrange("p (c e n) -> p c e n", c=4, e=2)
                R_QTz = Region("QTz")
                R_QTc = [Region("QTc%d" % c) for c in range(4)]
                ogT = W16.take(2048).rearrange("p (c n) -> p c n", c=4)
                R_ogT = Region("ogT")
                og16 = W16.take(512)
                R_og16 = Region("og16")
                Pw = [W16.take(640) for _ in range(2)]
                R_Pw = [Region("Pw0"), Region("Pw1")]
                R_Pw5 = [Region("Pw5_0"), Region("Pw5_1")]
                PcE = [BIG.take(384) for _ in range(2)]
                R_Pc = [Region("Pc0"), Region("Pc1")]
                Ew = [BIG.take(512) for _ in range(2)]
                R_Ew = [Region("Ew0"), Region("Ew1")]
                og32 = W32.take(512)
                R_og32 = Region("og32")
                yb16 = WB.take(4096).rearrange("p (k n) -> p k n", k=8)
                R_yb = Region("yb")
                R_rec = Region("rec")
                cur_edge = [-1]
                load_hT(0, 0)
                for bi in range(nblk):
                    t0, n = BLOCKS[bi]
                    hb = bi % 2
                    if bi + 1 < nblk:
                        load_hT(bi + 1, 1 - hb)
                    rope = is_wa and bi < 8
                    if rope:
                        dma("sp", rc32[:, 0:n], ropec_in[:, t0:t0 + n], [], [R_rope])
                        dma("sp", rs32[:, 0:n], ropes_in[:, t0:t0 + n], [], [R_rope])
                    ntile = n // 128
                    items = []
                    for c in range(4):
                        def qz(c=c):
                            for e_ in range(2):
                                act(QTz[:, c, e_, 0:n], QT[:, c, 0:n], AF.Copy, [R_QTc[c], R_hm], [R_QTz],
                                    scale=hmask[:, e_:e_ + 1])
                        items.append(normed_stages(c, c, cq + c * 128, hb, n, gq, QT[:, c, 0:n], R_QTc[c], rope, post=qz))
                    for t_ in range(ntile):
                        def g1(t_=t_):
                            proj_tm(4 + t_ % 2, wv, cg, 512, hb, t_, R_w)

                        def g2(t_=t_):
                            pi = 4 + t_ % 2
                            act(sgt[t_], ps[pi][:, 0:512], AF.Silu, [R_ps[pi]], [R_sgt[t_]])
                        items.append([g1, g2])
                    skew(items)

                    def scores(t_, h):
                        qt = t0 // 128 + t_
                        if bi == 8:
                            wch, tabv, R_tb = [], None, None
                        elif not is_wa:
                            ty, wch = na_chunks(qt)
                            if cur_edge[0] != ty:
                                dma("sp", tab16_f, etab_d[ty], [R_etab[ty]], [R_tab])
                                cur_edge[0] = ty
                            tabv, R_tb = tab16, R_tab
                        else:
                            wch = [c_ for c_ in (qt - 1, qt, qt + 1) if 0 <= c_ < 32]
                            moff = 128 if qt == 0 else 0
                            tabv, R_tb = None, R_c
                        cch = [32, 33]
                        nw = len(wch)
                        c2 = h // 2
                        pb = 64 * (h % 2)
                        kc = (h // 4) if is_wa else c2
                        hp = h % 2
                        pw_i, pc_i = (2, 4) if hp == 0 else (3, 0)
                        qv = QTz[:, c2, h % 2, t_ * 128:(t_ + 1) * 128]
                        for i_, kch in enumerate(cch):
                            mm(ps[pc_i][:, i_ * 128:(i_ + 1) * 128],
                               KT[:, kc, kch * 128:(kch + 1) * 128], qv, True, True,
                               [R_KT, R_QTz], [R_ps[pc_i]])
                        if nw == 5:
                            kch = wch[4]
                            mm(ps[pc_i][:, 256:384],
                               KT[:, kc, kch * 128:(kch + 1) * 128], qv, True, True,
                               [R_KT, R_QTz], [R_ps[pc_i]])
                        nce = 384 if nw == 5 else 256
                        act(PcE[hp][:, 0:nce], ps[pc_i][:, 0:nce], AF.Exp, [R_ps[pc_i]], [R_Pc[hp]])
                        for i_, kch in enumerate(wch[:4]):
                            mm(ps[pw_i][:, i_ * 128:(i_ + 1) * 128],
                               KT[:, kc, kch * 128:(kch + 1) * 128], qv, True, True,
                               [R_KT, R_QTz], [R_ps[pw_i]])
                        if nw:
                            n4 = min(nw, 4) * 128
                            if is_wa:
                                mk = wamask16[:, moff:moff + nw * 128]
                            else:
                                mk = tabv[:, h, 0:nw * 128]
                            if nw == 5:
                                tt("dve", Pw[hp][:, 512:640], PcE[hp][:, 256:384], mk[:, 512:640], ALU.mult,
                                   [R_Pc[hp], R_tb], [R_Pw5[hp]])
                            act(Ew[hp][:, 0:n4], ps[pw_i][:, 0:n4], AF.Exp, [R_ps[pw_i]], [R_Ew[hp]])
                            tt("dve", Pw[hp][:, 0:n4], Ew[hp][:, 0:n4], mk[:, 0:n4], ALU.mult,
                               [R_Ew[hp], R_tb], [R_Pw[hp]])
                        allch = [(PcE[hp][:, i_ * 128:(i_ + 1) * 128], kch, R_Pc[hp]) for i_, kch in enumerate(cch)]
                        if nw == 5:
                            allch.append((Pw[hp][:, 512:640], wch[4], R_Pw5[hp]))
                        allch += [(Pw[hp][:, i_ * 128:(i_ + 1) * 128], kch, R_Pw[hp]) for i_, kch in enumerate(wch[:4])]
                        return allch

                    def pv(t_, h, allch):
                        vh = (h // 4) if is_wa else h
                        opi = 5 if h < 4 else 1
                        ov = ps[opi][:, 0:260].rearrange("p (h d) -> p h d", h=4)[:, h % 4, :]
                        for i_, (pap, kch, rg) in enumerate(allch):
                            mm(ov, pap, VV[:, kch, vh, :], i_ == 0, i_ == len(allch) - 1, [rg, R_VV], [R_ps[opi]])

                    def epi_half(t_, g4):
                        opi = 5 if g4 == 0 else 1
                        o3 = ps[opi][:, 0:260].rearrange("p (h d) -> p h d", h=4)
                        rec = stat[:, 0:4]
                        if is_wa:
                            tt("dve", rec, o3[:, :, 64], esink[:, g4 * 4:(g4 + 1) * 4], ALU.add,
                               [R_ps[opi], R_small], [R_rec])
                            recip(rec, rec, [R_rec], [R_rec])
                        else:
                            recip(rec, o3[:, :, 64], [R_ps[opi]], [R_rec])
                        o32 = og32[:, 0:256].rearrange("p (h d) -> p h d", h=4)
                        tt("dve", o32, o3[:, :, 0:64], rec.unsqueeze(2).to_broadcast([128, 4, 64]), ALU.mult,
                           [R_ps[opi], R_rec], [R_og32])
                        tt("pool", og16[:, g4 * 256:(g4 + 1) * 256], og32[:, 0:256],
                           sgt[t_][:, g4 * 256:(g4 + 1) * 256], ALU.mult, [R_og32, R_sgt[t_]], [R_og16])

                    def transposes(t_):
                        pT4 = psT[:, 0:512].rearrange("p (c n) -> p c n", c=4)
                        for c in range(4):
                            tr(pT4[:, c, :], og16[:, c * 128:(c + 1) * 128], [R_og16], [R_psT])
                        cp("act", ogT[:, :, t_ * 128:(t_ + 1) * 128], pT4, [R_psT], [R_ogT])

                    units = [(t_, h) for t_ in range(ntile) for h in range(8)]
                    prev = None
                    pending_tr = []
                    for ui, (t_, h) in enumerate(units):
                        allch = scores(t_, h)
                        if prev is not None:
                            pt, ph, pch = prev
                            pv(pt, ph, pch)
                            if ph == 3:
                                epi_half(pt, 0)
                            if ph == 7:
                                epi_half(pt, 1)
                                pending_tr.append((ui + 2, pt))
                        while pending_tr and pending_tr[0][0] <= ui:
                            transposes(pending_tr.pop(0)[1])
                        prev = (t_, h, allch)
                    pt, ph, pch = prev
                    pv(pt, ph, pch)
                    epi_half(pt, 1)
                    pending_tr.append((0, pt))
                    while pending_tr:
                        transposes(pending_tr.pop(0)[1])
                    for i in range(8):
                        pi = 2 + i % 2
                        for c in range(4):
                            mm(ps[pi][:, 0:n], wpp[:, c, i * 128:(i + 1) * 128], ogT[:, c, 0:n], c == 0, c == 3,
                               [R_wp(i * 128), R_ogT], [R_ps[pi]])
                        cp("act" if i % 2 == 1 else "dve", yb16[:, i, 0:n], ps[pi][:, 0:n], [R_ps[pi]], [R_yb])
                    dma("sp", Y_d[1 + kind][:, :, t0:t0 + n], yb16[:, :, 0:n], [R_yb], [R_Y[1 + kind][bi]])

            for a in (BIG, WB, W32, W16):
                a.reset()
            S.barrier()
            gate_bc = W16.take(4096).bitcast(F32).rearrange("p (r n) -> p r n", r=2)
            dg32 = W32.take(128)
            R_dg = Region("dg")
            for r in range(2):
                for k in range(8):
                    ts("dve", dg32, ident32, modsb[:, 16 + k, r:r + 1], None, ALU.mult, None,
                       [R_mod2, R_c], [R_dg])
                    pi = 1 + (k // 4)
                    mm(ps[pi][:, (k % 4) * 128:(k % 4 + 1) * 128], ones32, dg32, True, True,
                       [R_c, R_dg], [R_ps[pi]])
                    if k % 4 == 3:
                        cp("act", gate_bc[:, r, (k // 4) * 512:(k // 4 + 1) * 512], ps[pi][:],
                           [R_ps[pi]], [R_gate])

            R_w = WReg()
            R_wo = WReg()
            wv = WB.take(8 * 3072).rearrange("p (k n) -> p k n", k=8)
            load_w(wv, win_in[l][:, 4864:7936], R_w, order=[0, 2, 4, 1, 3, 5], split_first=True)
            wo = BIG.take(8 * 1024).rearrange("p (k n) -> p k n", k=8)
            load_w(wo, wo_in[l], R_wo)
            Yb = [[BIG.take(4096).rearrange("p (k n) -> p k n", k=8) for _ in range(3)] for _ in range(2)]
            R_Yb = [[Region("Yb%d%d" % (i, j)) for j in range(3)] for i in range(2)]
            yT = BIG.take(4096).rearrange("p (k n) -> p k n", k=8)
            R_yT = Region("yT")
            sgm = [W32.take(512) for _ in range(2)]
            R_sgm = [Region("sgm0"), Region("sgm1")]
            acc = W32.take(512)
            R_acc = Region("acc")
            xt = [W32.take(1024) for _ in range(2)]
            R_xt = [Region("xt0"), Region("xt1")]
            xo = [W32.take(1024) for _ in range(2)]
            R_xo = [Region("xo0"), Region("xo1")]
            tmp2 = [W32.take(512) for _ in range(2)]
            R_tmp2 = [Region("tmp0"), Region("tmp1")]

            def ld_blk(bi):
                load_hT(bi, bi % 2)
                t0, n = BLOCKS[bi]
                for br in range(3):
                    dma("sp", Yb[bi % 2][br][:, :, 0:n], Y_d[br][:, :, t0:t0 + n], [R_Y[br][bi]], [R_Yb[bi % 2][br]])
            ld_blk(0)
            xcnt = 0
            for bi in range(nblk):
                t0, n = BLOCKS[bi]
                hb = bi % 2
                if bi + 1 < nblk:
                    ld_blk(bi + 1)
                r = 0 if bi < 8 else 1
                for i in range(8):
                    for br in range(3):
                        pi = (i * 3 + br) % 2
                        proj_fm(pi, wv, br * 1024 + i * 128, hb, n, R_w)
                        act(sgm[pi][:, 0:n], ps[pi][:, 0:n], AF.Sigmoid, [R_ps[pi]], [R_sgm[pi]])
                        if br == 0:
                            tt("dve", acc[:, 0:n], sgm[pi][:, 0:n], Yb[hb][br][:, i, 0:n], ALU.mult,
                               [R_sgm[pi], R_Yb[hb][br]], [R_acc])
                        else:
                            tt("pool", sgm[pi][:, 0:n], sgm[pi][:, 0:n], Yb[hb][br][:, i, 0:n], ALU.mult,
                               [R_sgm[pi], R_Yb[hb][br]], [R_sgm[pi]])
                            if br == 1:
                                tt("dve", acc[:, 0:n], acc[:, 0:n], sgm[pi][:, 0:n], ALU.add,
                                   [R_acc, R_sgm[pi]], [R_acc])
                            else:
                                tt("dve", yT[:, i, 0:n], acc[:, 0:n], sgm[pi][:, 0:n], ALU.add,
                                   [R_acc, R_sgm[pi]], [R_yT])
                for t_ in range(n // 128):
                    gt = t0 // 128 + t_
                    xb = xcnt % 2
                    xcnt += 1
                    src = xin[gt * 128:(gt + 1) * 128, :] if gt < 32 else cin[(gt - 32) * 128:(gt - 31) * 128, :]
                    dst = xout[gt * 128:(gt + 1) * 128, :] if gt < 32 else ctx1_d[(gt - 32) * 128:(gt - 31) * 128, :]
                    dma("sp", xt[xb], src, [R_x1[gt]] if l > 0 else [], [R_xt[xb]])
                    for hf in range(2):
                        pi = 2 + hf + 2 * (xb % 2)
                        tmp, R_tmp = tmp2[hf], R_tmp2[hf]
                        for i in range(8):
                            mm(ps[pi][:, :], yT[:, i, t_ * 128:(t_ + 1) * 128], wo[:, i, hf * 512:(hf + 1) * 512],
                               i == 0, i == 7, [R_yT, R_wo(hf * 512)], [R_ps[pi]])
                        tt("dve", tmp, ps[pi][:, :], gate_bc[:, r, hf * 512:(hf + 1) * 512], ALU.mult,
                           [R_ps[pi], R_gate], [R_tmp])
                        tt("pool", xo[xb][:, hf * 512:(hf + 1) * 512], tmp, xt[xb][:, hf * 512:(hf + 1) * 512],
                           ALU.add, [R_tmp, R_xt[xb]], [R_xo[xb]])
                    if last:
                        dma("sp", dst, xo[xb], [R_xo[xb]], [], is_output=True)
                    else:
                        dma("sp", dst, xo[xb], [R_xo[xb]], [R_x1[gt]])

        S.finish()
        S.emit_all()
    return nc


_CACHE = {}


def _fm(v, nchunk):
    sh = v.shape[:-1]
    return np.ascontiguousarray(np.swapaxes(v.reshape(*sh, nchunk, 128), -1, -2))


def prep_inputs(inp):
    global _NA_IDX
    if _NA_IDX is None:
        _NA_IDX = _na_index()
    if "cst" not in _CACHE:
        _CACHE["cst"] = _consts()
    cst, rc, rs = _CACHE["cst"]
    f = lambda a: np.ascontiguousarray(np.asarray(a, dtype=np.float32))
    x, c, ctx, c_ctx = f(inp["x"]), f(inp["c"]), f(inp["ctx"]), f(inp["c_ctx"])
    common = {
        "norm_gT": _fm(f(inp["norm_g"]), 8),
        "b_adaT": _fm(f(inp["b_ada"]), 24),
        "w_ada": f(inp["w_ada"]), "w_in": f(inp["w_in"]),
        "w_proj_a": f(inp["w_proj_a"]), "w_proj_b": f(inp["w_proj_b"]), "w_proj_c": f(inp["w_proj_c"]),
        "w_o": f(inp["w_o"]),
        "cst": cst, "rope_c": rc, "rope_s": rs,
    }
    cw = f(inp["conv_w"])
    cwT = np.transpose(cw.reshape(L, 31, 4, 128), (0, 3, 2, 1))
    common["conv_wT"] = np.ascontiguousarray(cwT.reshape(L, 128, 124))
    common["cvecT"] = np.ascontiguousarray(np.concatenate(
        [_fm(f(inp["conv_b"]), 4), _fm(f(inp["cln_g"]), 4), _fm(f(inp["cln_b"]), 4)], axis=-1))
    qk = np.stack([f(inp["na_q_norm"]), f(inp["na_k_norm"]), f(inp["wa_q_norm"]), f(inp["wa_k_norm"])], -1)
    common["qk_norm"] = np.ascontiguousarray(np.concatenate([qk, qk], axis=1))
    common["sink_bc"] = np.ascontiguousarray(np.broadcast_to(f(inp["wa_sink"])[:, None, :], (L, 128, 8)))
    rpb = f(inp["na_rpb"]).reshape(L, 8, 15 * 31)
    rpb_pad = np.concatenate([rpb, np.full((L, 8, 1), NEG, np.float32)], axis=-1)
    tab = rpb_pad[:, :, _NA_IDX]
    tab = np.transpose(tab, (0, 2, 3, 1, 4, 5))
    common["na_tab"] = np.ascontiguousarray(tab.reshape(L, 5, 128, 8 * 640))
    maps = []
    for b in range(8):
        m = dict(common)
        m["x"] = x[b]
        m["ctx"] = ctx[b]
        cc = np.stack([c[b], c_ctx], axis=-1)
        m["ccT"] = np.ascontiguousarray(np.transpose(cc.reshape(8, 128, 2), (1, 0, 2)))
        maps.append(m)
    return maps


def kernel(**inputs):
    if "nc" not in _CACHE:
        _CACHE["nc"] = build()
    maps = prep_inputs(inputs)
    res = run_bass_kernel_spmd(_CACHE["nc"], maps, core_ids=list(range(8)))
    return np.stack([np.asarray(r["out"], dtype=np.float32) for r in res.results], axis=0)
```

```python
import numpy as np
from contextlib import ExitStack
import concourse.bass as bass
import concourse.mybir as mybir
from concourse.bass_utils import run_bass_kernel_spmd

F32 = mybir.dt.float32
BF16 = mybir.dt.bfloat16
AF = mybir.ActivationFunctionType
ALU = mybir.AluOpType

L = 2
S_LAT = 4096
N_CTX = 256
S_ALL = S_LAT + N_CTX
D = 1024
D_IN = 7936
EPS = 1e-6
NEG = -30000.0
ENGS = ("pe", "act", "dve", "pool", "sp")
BLOCKS = [(b * 512, 512) for b in range(8)] + [(S_LAT, N_CTX)]


class Region:
    __slots__ = ("name", "lw", "reads")

    def __init__(self, name=""):
        self.name = name
        self.lw = None
        self.reads = {}


class Sched:
    NDS = 12

    def __init__(self, nc, stack):
        self.nc = nc
        self.ops = {e: [] for e in ENGS}
        self.cnt = {e: 0 for e in ENGS}
        self.semobj = {}
        for e in ENGS:
            self.semobj[("e", e)] = stack.enter_context(nc.semaphore("s_" + e))
        self.dq = {}
        for q in ("sp", "act", "pool"):
            lst = []
            for i in range(self.NDS):
                key = ("d", q, i)
                self.semobj[key] = stack.enter_context(nc.semaphore("d_%s%d" % (q, i)))
                lst.append([key, 0])
            self.dq[q] = [lst, 0]
        self.known = {e: {} for e in ENGS}
        self.out_tokens = []

    def _need(self, eng, tok, waits):
        if tok is None:
            return
        key, val = tok
        if eng == "pe" and key == ("e", "pe"):
            return
        if self.known[eng].get(key, 0) >= val:
            return
        if waits.get(key, 0) < val:
            waits[key] = val

    def _collect(self, eng, reads, writes, waits):
        for r in reads:
            self._need(eng, r.lw, waits)
        for w in writes:
            self._need(eng, w.lw, waits)
            for k, v in w.reads.items():
                self._need(eng, (k, v), waits)
        for k, v in waits.items():
            self.known[eng][k] = v
        return [(self.semobj[k], v) for k, v in waits.items()]

    def op(self, eng, fn, reads=(), writes=()):
        wl = self._collect(eng, reads, writes, {})
        self.cnt[eng] += 1
        seq = self.cnt[eng]
        key = ("e", eng)
        sem = self.semobj[key]
        for r in reads:
            r.reads[key] = seq
        for w in writes:
            w.lw = (key, seq)
            w.reads = {}

        def emit(e):
            for s, v in wl:
                e.wait_ge(s, v)
            fn(e).then_inc(sem, 1)
        self.ops[eng].append(emit)

    def dma(self, q, fn, reads=(), writes=(), is_output=False):
        lst, idx = self.dq[q]
        ent = lst[idx % self.NDS]
        self.dq[q][1] = idx + 1
        key = ent[0]
        waits = {}
        if ent[1] > 0:
            self._need(q, (key, ent[1]), waits)
        wl = self._collect(q, reads, writes, waits)
        ent[1] += 16
        val = ent[1]
        sem = self.semobj[key]
        for r in reads:
            if r.reads.get(key, 0) < val:
                r.reads[key] = val
        for w in writes:
            w.lw = (key, val)
            w.reads = {}
        if is_output:
            self.out_tokens.append((key, val))

        def emit(e):
            for s, v in wl:
                e.wait_ge(s, v)
            fn(e).then_inc(sem, 16)
        self.ops[q].append(emit)

    def barrier(self):
        toks = [(("e", x), self.cnt[x]) for x in ENGS if self.cnt[x] > 0]
        for q in self.dq:
            for key, val in self.dq[q][0]:
                if val > 0:
                    toks.append((key, val))
        for e in ENGS:
            waits = {}
            for key, val in toks:
                if key == ("e", e):
                    continue
                if self.known[e].get(key, 0) < val:
                    waits[key] = val
                    self.known[e][key] = val
            wl = [(self.semobj[k], v) for k, v in waits.items()]
            if wl:
                def emit(eo, wl=wl):
                    for s, v in wl:
                        eo.wait_ge(s, v)
                self.ops[e].append(emit)

    def finish(self):
        final = {}
        for k, v in self.out_tokens:
            final[k] = max(final.get(k, 0), v)
        wl = [(self.semobj[k], v) for k, v in final.items()]

        def emit(e):
            for s, v in wl:
                e.wait_ge(s, v)
        self.ops["sp"].append(emit)

    def emit_all(self):
        with self.nc.Block() as block:
            @block.tensor
            def _(e):
                for f in self.ops["pe"]:
                    f(e)

            @block.scalar
            def _(e):
                for f in self.ops["act"]:
                    f(e)

            @block.vector
            def _(e):
                for f in self.ops["dve"]:
                    f(e)

            @block.gpsimd
            def _(e):
                for f in self.ops["pool"]:
                    f(e)

            @block.sync
            def _(e):
                for f in self.ops["sp"]:
                    f(e)


class Arena:
    def __init__(self, t, size, name):
        self.t, self.size, self.name, self.off = t, size, name, 0

    def reset(self):
        self.off = 0

    def take(self, n, name=""):
        n16 = (n + 15) // 16 * 16
        assert self.off + n16 <= self.size, (self.name, name, self.off, n, self.size)
        ap = self.t[:, self.off:self.off + n]
        self.off += n16
        return ap


NA_TYPES = [None, 0, 2, 60, 62]


def na_chunks(qt):
    i0 = 2 * qt
    if i0 <= 2:
        return 1 + i0 // 2, [0, 1, 2, 3]
    if i0 >= 60:
        return 3 + (i0 - 60) // 2, [28, 29, 30, 31]
    return 0, [qt - 2, qt - 1, qt, qt + 1, qt + 2]


def _na_index():
    idx = np.full((5, 128, 5, 128), 15 * 31, dtype=np.int64)
    for ty in range(5):
        qt = 8 if ty == 0 else NA_TYPES[ty] // 2
        _, chunks = na_chunks(qt)
        for ci, kc in enumerate(chunks):
            for p in range(128):
                r = 2 * kc + p // 64
                c = p % 64
                for n in range(128):
                    i = 2 * qt + n // 64
                    j = n % 64
                    rs = min(max(i - 4, 0), 56)
                    cs = min(max(j - 8, 0), 48)
                    if rs <= r < rs + 8 and cs <= c < cs + 16:
                        idx[ty, p, ci, n] = (r - i + 7) * 31 + (c - j + 15)
    return idx


_NA_IDX = None


def _consts():
    cst = np.zeros((128, 5 * 128 + 384), np.float32)
    cst[:, 0:128] = np.eye(128)
    pm = np.zeros((128, 128), np.float32)
    for m in range(128):
        sub = m % 32
        if sub < 16:
            pm[m + 16, m] = 1.0
        else:
            pm[m - 16, m] = 1.0
    cst[:, 128:256] = pm
    bo = np.zeros((128, 128), np.float32)
    bo[0:64, 0:64] = 1.0 / 64
    bo[64:128, 64:128] = 1.0 / 64
    cst[:, 256:384] = bo
    cst[:, 384:512] = 1.0 / 512
    cst[:, 512:640] = 1.0
    p = np.arange(128)[:, None]
    n = np.arange(128)[None, :]
    cst[:, 640:768] = (n <= p)
    cst[:, 768:896] = 1.0
    cst[:, 896:1024] = (p <= n)
    t = np.arange(S_LAT)
    rowp = (t // 64).astype(np.float32)
    colp = (t % 64).astype(np.float32)
    inv = (10000.0 ** (-np.arange(16, dtype=np.float32) / 16)).astype(np.float32)
    rc = np.zeros((128, S_LAT), np.float32)
    rs = np.zeros((128, S_LAT), np.float32)
    for pp in range(128):
        d = pp % 64
        sub = d % 32
        pos = rowp if d < 32 else colp
        ang = (pos * inv[sub % 16]).astype(np.float32)
        rc[pp] = np.cos(ang)
        rs[pp] = -np.sin(ang) if sub < 16 else np.sin(ang)
    return cst, rc, rs


def build(debug=False):
    nc = bass.Bass("TRN2", target_bir_lowering=False)
    EI = "ExternalInput"
    dbgk = "ExternalOutput" if debug else "Internal"

    def din(name, shape):
        return nc.dram_tensor(name, list(shape), F32, kind=EI).ap()

    x_in = din("x", [S_LAT, D])
    ctx_in = din("ctx", [N_CTX, D])
    ccT_in = din("ccT", [128, 8, 2])
    normg_in = din("norm_gT", [L, 128, 8])
    bada_in = din("b_adaT", [L, 128, 24])
    wada_in = din("w_ada", [L, D, 3 * D])
    win_in = din("w_in", [L, D, D_IN])
    wpa_in = din("w_proj_a", [L, 512, D])
    wpb_in = din("w_proj_b", [L, 512, D])
    wpc_in = din("w_proj_c", [L, 512, D])
    wo_in = din("w_o", [L, D, D])
    convw_in = din("conv_wT", [L, 128, 4 * 31])
    cvec_in = din("cvecT", [L, 128, 12])
    qkn_in = din("qk_norm", [L, 128, 4])
    sink_in = din("sink_bc", [L, 128, 8])
    natab_in = din("na_tab", [L, 5, 128, 8 * 640])
    cst_in = din("cst", [128, 1024])
    ropec_in = din("rope_c", [128, S_LAT])
    ropes_in = din("rope_s", [128, S_LAT])
    out_x = nc.dram_tensor("out", [S_LAT, D], F32, kind="ExternalOutput").ap()

    hT_d = nc.dram_tensor("hT_d", [128, 8, S_ALL], BF16, kind=dbgk).ap()
    Y_d = [nc.dram_tensor("Y%d_d" % i, [128, 8, S_ALL], BF16, kind=dbgk).ap() for i in range(3)]
    x1_d = nc.dram_tensor("x1_d", [S_LAT, D], F32, kind=dbgk).ap()
    ctx1_d = nc.dram_tensor("ctx1_d", [N_CTX, D], F32, kind=dbgk).ap()
    etab_d = nc.dram_tensor("etab_d", [5, 128, 5120], BF16, kind="Internal").ap()

    R_hT = [Region("hT%d" % i) for i in range(34)]
    R_Y = [[Region("Y%d_%d" % (br, b)) for b in range(9)] for br in range(3)]
    R_x1 = [Region("x1_%d" % i) for i in range(34)]
    R_etab = [Region("etab%d" % i) for i in range(5)]

    with ExitStack() as st:
        S = Sched(nc, st)

        def sb(name, shape, dt):
            return st.enter_context(nc.sbuf_tensor(name, list(shape), dt))

        BIG = Arena(sb("BIG", [128, 37120], BF16), 37120, "BIG")
        WB = Arena(sb("WB", [128, 24576], BF16), 24576, "WB")
        W32 = Arena(sb("W32", [128, 7168], F32), 7168, "W32")
        W16 = Arena(sb("W16", [128, 16384], BF16), 16384, "W16")
        HTB = sb("HTB", [128, 2, 8, 512], BF16)
        cst16 = sb("cst16", [128, 1024], BF16)
        cst32 = sb("cst32", [128, 256], F32)
        silc = sb("silc", [128, 8, 2], BF16)
        cc32 = sb("cc32", [128, 8, 2], F32)
        modsb = sb("modsb", [128, 24, 2], F32)
        gsb = sb("gsb", [128, 8, 2], F32)
        normg = sb("normg", [128, 8], F32)
        bada = sb("bada", [128, 24], F32)
        convw = sb("convw", [128, 124], F32)
        cvec = sb("cvec", [128, 12], F32)
        qkn = sb("qkn", [128, 4], F32)
        esink = sb("esink", [128, 8], F32)
        stat = sb("stat", [128, 8], F32)
        hmask = sb("hmask", [128, 2], F32)
        ps = [st.enter_context(nc.psum_tensor("ps%d" % i, [128, 512], F32)) for i in range(7)]
        psT = st.enter_context(nc.psum_tensor("psT", [128, 1024], BF16))
        R_ps = [Region("ps%d" % i) for i in range(7)]
        R_psT = Region("psT")
        R_HTB = [Region("HTB0"), Region("HTB1")]
        R_c = Region("consts")
        R_small = Region("small")
        R_gate = Region("gate_bc")

        ident16 = cst16[:, 0:128]
        pmat16 = cst16[:, 128:256]
        bones16 = cst16[:, 256:384]
        o512_16 = cst16[:, 384:512]
        wamask16 = cst16[:, 640:1024]
        ident32 = cst32[:, 0:128]
        ones32 = cst32[:, 128:256]

        def mm(out, lhsT, rhs, start, stop, rd, wr):
            S.op("pe", lambda e: e.matmul(out, lhsT=lhsT, rhs=rhs, start=start, stop=stop), rd, wr)

        def tr(out, in_, rd, wr):
            S.op("pe", lambda e: e.transpose(out=out, in_=in_, identity=ident16), rd + [R_c], wr)

        def act(out, in_, func, rd, wr, **kw):
            S.op("act", lambda e: e.activation(out=out, in_=in_, func=func, **kw), rd, wr)

        def tt(eng, out, in0, in1, op, rd, wr):
            S.op(eng, lambda e: e.tensor_tensor(out=out, in0=in0, in1=in1, op=op), rd, wr)

        def ts(eng, out, in0, s1, s2, op0, op1, rd, wr):
            if op1 is None:
                S.op(eng, lambda e: e.tensor_scalar(out=out, in0=in0, scalar1=s1, scalar2=None, op0=op0), rd, wr)
            else:
                S.op(eng, lambda e: e.tensor_scalar(out=out, in0=in0, scalar1=s1, scalar2=s2, op0=op0, op1=op1), rd, wr)

        def stt(eng, out, in0, scalar, in1, op0, op1, rd, wr):
            S.op(eng, lambda e: e.scalar_tensor_tensor(out=out, in0=in0, scalar=scalar, in1=in1, op0=op0, op1=op1), rd, wr)

        def cp(eng, out, in_, rd, wr):
            if eng == "act":
                S.op(eng, lambda e: e.activation(out=out, in_=in_, func=AF.Copy), rd, wr)
            else:
                S.op(eng, lambda e: e.tensor_copy(out=out, in_=in_), rd, wr)

        def recip(out, in_, rd, wr):
            S.op("dve", lambda e: e.reciprocal(out=out, in_=in_), rd, wr)

        def mset(eng, ap, val, wr):
            S.op(eng, lambda e: e.memset(ap, val), [], wr)

        def dma(q, out, in_, rd, wr, is_output=False):
            S.dma(q, lambda e: e.dma_start(out=out, in_=in_), rd, wr, is_output=is_output)

        class WReg:
            def __init__(self):
                self.rs = []

            def add(self, a, b):
                r = Region("w%d" % a)
                self.rs.append((a, b, r))
                return r

            def __call__(self, c):
                for a, b, r in self.rs:
                    if a <= c < b:
                        return r
                raise KeyError(c)

        def load_w(dst3, src2, wreg, base=0, order=None, region=None, split_first=False):
            n = src2.shape[1]
            starts = list(range(0, n, 512))
            if order is not None:
                starts = [starts[i] for i in order]
            pieces = []
            for i_, c0 in enumerate(starts):
                cw = min(512, n - c0)
                if split_first and i_ == 0 and cw == 512:
                    pieces += [(c0 + q_ * 128, 128) for q_ in range(4)]
                else:
                    pieces.append((c0, cw))
            for c0, cw in pieces:
                dma("pool", dst3[:, :, c0:c0 + cw],
                    src2[:, c0:c0 + cw].rearrange("(k p) n -> p k n", p=128), [],
                    [region if region is not None else wreg.add(base + c0, base + c0 + cw)])

        def load_hT(bi, buf):
            t0, n = BLOCKS[bi]
            rr = R_hT[t0 // 128:(t0 + n) // 128]
            dma("sp", HTB[:, buf, :, 0:n], hT_d[:, :, t0:t0 + n], rr, [R_HTB[buf]])

        def proj_fm(psi, wview, c0, hbuf, n, wreg):
            for k in range(8):
                mm(ps[psi][:, 0:n], wview[:, k, c0:c0 + 128], HTB[:, hbuf, k, 0:n],
                   k == 0, k == 7, [wreg(c0), R_HTB[hbuf]], [R_ps[psi]])

        def proj_tm(psi, wview, c0, ncols, hbuf, tt_, wreg):
            for k in range(8):
                mm(ps[psi][:, 0:ncols], HTB[:, hbuf, k, tt_ * 128:(tt_ + 1) * 128],
                   wview[:, k, c0:c0 + ncols], k == 0, k == 7, [wreg(c0), R_HTB[hbuf]], [R_ps[psi]])

        R_hm = Region("hmask")
        mset("pool", hmask[:, :], 0.0, [R_hm])
        mset("pool", hmask[0:64, 0:1], 1.0, [R_hm])
        mset("pool", hmask[64:128, 1:2], 1.0, [R_hm])
        dma("pool", cst16[:], cst_in, [], [R_c])
        dma("sp", cst32[:, 0:128], cst_in[:, 0:128], [], [R_c])
        dma("sp", cst32[:, 128:256], cst_in[:, 512:640], [], [R_c])
        dma("sp", cc32[:], ccT_in, [], [R_small])
        act(silc[:], cc32[:], AF.Silu, [R_small], [R_small])

        for l in range(L):
            last = (l == L - 1)
            nblk = 8 if last else 9
            xin = x_in if l == 0 else x1_d
            cin = ctx_in if l == 0 else ctx1_d
            xout = out_x if last else x1_d
            for a in (BIG, WB, W32, W16):
                a.reset()
            S.barrier()

            R_w = WReg()
            wad = WB.take(8 * 3072).rearrange("p (k n) -> p k n", k=8)
            load_w(wad, wada_in[l], R_w, split_first=True)
            dma("sp", normg[:], normg_in[l], [], [R_small])
            dma("sp", bada[:], bada_in[l], [], [R_small])
            dma("sp", convw[:], convw_in[l], [], [R_small])
            dma("sp", cvec[:], cvec_in[l], [], [R_small])
            dma("sp", qkn[:], qkn_in[l], [], [R_small])
            dma("sp", esink[:], sink_in[l], [], [R_small])
            modps = ps[0][:, 0:48].rearrange("p (f r) -> p f r", f=24)
            for f in range(16):
                for k in range(8):
                    mm(modps[:, f, :], wad[:, k, f * 128:(f + 1) * 128], silc[:, k, :],
                       k == 0, k == 7, [R_w(f * 128), R_small], [R_ps[0]])
            tt("dve", modsb[:, 0:16, :], modps[:, 0:16, :], bada[:, 0:16].unsqueeze(2).to_broadcast([128, 16, 2]), ALU.add,
               [R_ps[0], R_small], [R_small])
            ts("dve", gsb[:], modsb[:, 8:16, :], 1.0, None, ALU.add, None, [R_small], [R_small])
            tt("dve", gsb[:], gsb[:], normg[:].unsqueeze(2).to_broadcast([128, 8, 2]), ALU.mult,
               [R_small], [R_small])
            ts("dve", qkn[:, 0:1], qkn[:, 0:1], 0.125, None, ALU.mult, None, [R_small], [R_small])
            ts("dve", qkn[:, 2:3], qkn[:, 2:3], 0.125, None, ALU.mult, None, [R_small], [R_small])
            act(esink[:], esink[:], AF.Exp, [R_small], [R_small])

            W16.reset()
            dgs = BIG.take(124 * 128).rearrange("p (j k n) -> p j k n", j=4, k=31)
            R_dgs = Region("dgs")
            dg_todo = [(j, k) for j in range(4) for k in range(31)]

            def build_dg(cnt):
                for i_ in range(cnt):
                    if dg_todo:
                        j, k = dg_todo.pop(0)
                        if False:
                            ts("dve", dgs[:, j, k, :], ident32, convw[:, j * 31 + k:j * 31 + k + 1], None,
                               ALU.mult, None, [R_c, R_small], [R_dgs])
                        else:
                            act(dgs[:, j, k, :], ident32, AF.Copy, [R_c, R_small], [R_dgs],
                                scale=convw[:, j * 31 + k:j * 31 + k + 1])
            xt = [W32.take(1024) for _ in range(3)]
            t32 = [W32.take(1024) for _ in range(2)]
            junk16 = W16.take(1024)
            xn16 = [W16.take(1024) for _ in range(2)]
            hts16 = [W16.take(1024) for _ in range(2)]
            R_xt = [Region("xt%d" % i) for i in range(3)]
            R_t32, R_junk = [Region("t32a"), Region("t32b")], Region("junk")
            R_xn = [Region("xn0"), Region("xn1")]
            R_hts = [Region("hts0"), Region("hts1")]
            R_st = [Region("st0"), Region("st1")]
            psT3 = psT[:, :].rearrange("p (k n) -> p k n", k=8)

            def a_stages(ti):
                b = ti % 2
                b3 = ti % 3
                r = 0 if ti < 32 else 1
                t32v = t32[b].rearrange("p (k n) -> p k n", k=8)
                htv = hts16[b].rearrange("p (k n) -> p k n", k=8)

                def s0():
                    src = xin[ti * 128:(ti + 1) * 128, :] if ti < 32 else cin[(ti - 32) * 128:(ti - 31) * 128, :]
                    rd = [R_x1[ti]] if l > 0 else []
                    dma("sp", xt[b3], src, rd, [R_xt[b3]])

                def s1():
                    mset("dve", stat[:, b:b + 1], 0.0, [R_st[b]])
                    act(junk16, xt[b3], AF.Square, [R_xt[b3]], [R_junk, R_st[b]], accum_out=stat[:, b:b + 1])
                    act(stat[:, 2 + b:3 + b], stat[:, b:b + 1], AF.Sqrt, [R_st[b]], [R_st[b]],
                        scale=1.0 / D, bias=EPS)
                    recip(stat[:, 4 + b:5 + b], stat[:, 2 + b:3 + b], [R_st[b]], [R_st[b]])
                    ts("dve", xn16[b], xt[b3], stat[:, 4 + b:5 + b], None, ALU.mult, None,
                       [R_xt[b3], R_st[b]], [R_xn[b]])

                def s2():
                    for k in range(8):
                        tr(psT3[:, k, :], xn16[b][:, k * 128:(k + 1) * 128], [R_xn[b]], [R_psT])
                    tt("dve", t32v, psT3, gsb[:, :, r:r + 1].to_broadcast([128, 8, 128]), ALU.mult,
                       [R_psT, R_small], [R_t32[b]])

                def s3():
                    tt("pool", htv, t32v, modsb[:, 0:8, r:r + 1].to_broadcast([128, 8, 128]), ALU.add,
                       [R_t32[b], R_small], [R_hts[b]])
                    dma("sp", hT_d[:, :, ti * 128:(ti + 1) * 128], htv, [R_hts[b]], [R_hT[ti]])
                return [s0, s1, s2, s3]

            a_items = [a_stages(ti) for ti in range(34)]
            for step in range(34 + 3):
                for s_ in range(4):
                    i_ = step - s_
                    if 0 <= i_ < 34:
                        a_items[i_][s_]()
                build_dg(4)
            build_dg(124)

            R_mod2 = Region("mod2")
            modps2 = ps[1][:, 0:16].rearrange("p (f r) -> p f r", f=8)
            for f in range(16, 24):
                for k in range(8):
                    mm(modps2[:, f - 16, :], wad[:, k, f * 128:(f + 1) * 128], silc[:, k, :],
                       k == 0, k == 7, [R_w(f * 128), R_small], [R_ps[1]])
            tt("dve", modsb[:, 16:24, :], modps2, bada[:, 16:24].unsqueeze(2).to_broadcast([128, 8, 2]), ALU.add,
               [R_ps[1], R_small], [R_mod2])
            for a in (WB, W32, W16):
                a.reset()
            S.barrier()
            R_w = WReg()
            R_wp = WReg()
            wv = WB.take(8 * 1536).rearrange("p (k n) -> p k n", k=8)
            wpa = WB.take(4 * 1024).rearrange("p (k n) -> p k n", k=4)
            load_w(wv, win_in[l][:, 0:1536], R_w, split_first=True)
            load_w(wpa, wpa_in[l], R_wp)
            A_lat_f = BIG.take(4 * (S_LAT + 32))
            A_ctx_f = BIG.take(4 * (N_CTX + 32))
            A_lat = A_lat_f.rearrange("p (j n) -> p j n", j=4)
            A_ctx = A_ctx_f.rearrange("p (j n) -> p j n", j=4)
            R_A = Region("A")
            mset("pool", A_lat_f, 0.0, [R_A])
            mset("pool", A_ctx_f, 0.0, [R_A])
            sg32 = [W32.take(512) for _ in range(2)]
            R_sg = [Region("sg0"), Region("sg1")]
            load_hT(0, 0)
            for bi in range(nblk):
                t0, n = BLOCKS[bi]
                hb = bi % 2
                if bi + 1 < nblk:
                    load_hT(bi + 1, 1 - hb)
                Ab, a0 = (A_lat, t0) if bi < 8 else (A_ctx, 0)
                for j in range(4):
                    pa, pb_ = (0, 1) if j % 2 == 0 else (2, 3)
                    proj_fm(pb_, wv, 512 + j * 128, hb, n, R_w)
                    proj_fm(pa, wv, j * 128, hb, n, R_w)
                    sb_ = j % 2
                    act(sg32[sb_][:, 0:n], ps[pb_][:, 0:n], AF.Sigmoid, [R_ps[pb_]], [R_sg[sb_]])
                    tt("dve", Ab[:, j, 15 + a0:15 + a0 + n], ps[pa][:, 0:n], sg32[sb_][:, 0:n], ALU.mult,
                       [R_ps[pa], R_sg[sb_]], [R_A])
            v32 = W32.take(2048).rearrange("p (j n) -> p j n", j=4)
            mean32 = W32.take(512)
            var32 = W32.take(512)
            rstd32 = W32.take(512)
            t1 = [W32.take(512) for _ in range(2)]
            t2 = [W32.take(512) for _ in range(2)]
            sga = sg32
            v16 = W16.take(2048).rearrange("p (j n) -> p j n", j=4)
            sq16 = W16.take(2048).rearrange("p (j n) -> p j n", j=4)
            z16 = W16.take(2048).rearrange("p (j n) -> p j n", j=4)
            ya16 = WB.take(4096).rearrange("p (k n) -> p k n", k=8)
            R_v32, R_v16, R_sq16, R_z16, R_ya = (Region("v32"), Region("v16"), Region("sq16"),
                                                 Region("z16"), Region("ya"))
            R_mean, R_var, R_rstd = Region("mean"), Region("var"), Region("rstd")
            R_v32j = [Region("v32_%d" % j) for j in range(4)]
            NPE = 31
            R_t1 = [Region("t1a"), Region("t1b")]
            R_t2 = [Region("t2a"), Region("t2b")]
            load_hT(0, 0)
            for bi in range(nblk):
                t0, n = BLOCKS[bi]
                hb = bi % 2
                if bi + 1 < nblk:
                    load_hT(bi + 1, 1 - hb)
                Ab, a0 = (A_lat, t0) if bi < 8 else (A_ctx, 0)
                for j in range(4):
                    pi = j % 2
                    for k in range(NPE):
                        mm(ps[pi][:, 0:n], dgs[:, j, k, :], Ab[:, j, a0 + k:a0 + k + n], k == 0, k == NPE - 1,
                           [R_dgs, R_A], [R_ps[pi]])
                    act(v32[:, j, 0:n], ps[pi][:, 0:n], AF.Identity, [R_ps[pi], R_small], [R_v32j[j]],
                        bias=cvec[:, j:j + 1])
                    for k in range(NPE, 31):
                        stt("dve", v32[:, j, 0:n], Ab[:, j, a0 + k:a0 + k + n], convw[:, j * 31 + k:j * 31 + k + 1],
                            v32[:, j, 0:n], ALU.mult, ALU.add, [R_A, R_small, R_v32j[j]], [R_v32j[j]])
                    act(v16[:, j, 0:n], ps[pi][:, 0:n], AF.Identity, [R_ps[pi], R_small], [R_v16],
                        bias=cvec[:, j:j + 1])
                    act(sq16[:, j, 0:n], v32[:, j, 0:n], AF.Square, [R_v32j[j]], [R_sq16])
                for j in range(4):
                    mm(ps[2][:, 0:n], o512_16, v16[:, j, 0:n], j == 0, j == 3, [R_c, R_v16], [R_ps[2]])
                for j in range(4):
                    mm(ps[3][:, 0:n], o512_16, sq16[:, j, 0:n], j == 0, j == 3, [R_c, R_sq16], [R_ps[3]])
                cp("dve", mean32[:, 0:n], ps[2][:, 0:n], [R_ps[2]], [R_mean])
                tt("dve", var32[:, 0:n], mean32[:, 0:n], mean32[:, 0:n], ALU.mult, [R_mean], [R_var])
                tt("dve", var32[:, 0:n], ps[3][:, 0:n], var32[:, 0:n], ALU.subtract, [R_ps[3], R_var], [R_var])
                act(rstd32[:, 0:n], var32[:, 0:n], AF.Sqrt, [R_var], [R_rstd], bias=EPS)
                recip(rstd32[:, 0:n], rstd32[:, 0:n], [R_rstd], [R_rstd])
                for j in range(4):
                    b2 = j % 2
                    pi = 4 + b2
                    proj_fm(pi, wv, 1024 + j * 128, hb, n, R_w)
                    act(sga[b2][:, 0:n], ps[pi][:, 0:n], AF.Silu, [R_ps[pi]], [R_sg[b2]])
                    tt("dve", t1[b2][:, 0:n], v32[:, j, 0:n], mean32[:, 0:n], ALU.subtract,
                       [R_v32j[j], R_mean], [R_t1[b2]])
                    tt("dve", t1[b2][:, 0:n], t1[b2][:, 0:n], rstd32[:, 0:n], ALU.mult,
                       [R_t1[b2], R_rstd], [R_t1[b2]])
                    act(t2[b2][:, 0:n], t1[b2][:, 0:n], AF.Silu, [R_t1[b2], R_small], [R_t2[b2]],
                        scale=cvec[:, 4 + j:5 + j], bias=cvec[:, 8 + j:9 + j])
                    tt("pool" if j % 2 == 0 else "dve", z16[:, j, 0:n], t2[b2][:, 0:n], sga[b2][:, 0:n], ALU.mult,
                       [R_t2[b2], R_sg[b2]], [R_z16])
                for i in range(8):
                    pi = i % 2
                    for j in range(4):
                        mm(ps[pi][:, 0:n], wpa[:, j, i * 128:(i + 1) * 128], z16[:, j, 0:n], j == 0, j == 3,
                           [R_wp(i * 128), R_z16], [R_ps[pi]])
                    if i % 2 == 1:
                        cp("act", ya16[:, i, 0:n], ps[pi][:, 0:n], [R_ps[pi]], [R_ya])
                    else:
                        cp("dve", ya16[:, i, 0:n], ps[pi][:, 0:n], [R_ps[pi]], [R_ya])
                dma("sp", Y_d[0][:, :, t0:t0 + n], ya16[:, :, 0:n], [R_ya], [R_Y[0][bi]])

            for kind in (0, 1):
                for a in (BIG, WB, W32, W16):
                    a.reset()
                S.barrier()
                is_wa = kind == 1
                R_w = WReg()
                R_wp = WReg()
                if not is_wa:
                    ncol = 2048
                    wv = WB.take(8 * ncol).rearrange("p (k n) -> p k n", k=8)
                    load_w(wv, win_in[l][:, 1536:3584], R_w, order=[1, 2, 0, 3], split_first=True)
                    cq, ck, cv_, cg = 0, 512, 1024, 1536
                    nkc, nkv = 4, 8
                    gq, gk = qkn[:, 0:1], qkn[:, 1:2]
                    wp_src = wpb_in[l]
                else:
                    ncol = 1408
                    wv = WB.take(8 * ncol).rearrange("p (k n) -> p k n", k=8)
                    for g in range(2):
                        rg = R_w.add(512 + g * 128, 512 + (g + 1) * 128)
                        for dup in range(2):
                            c0 = 512 + g * 128 + dup * 64
                            load_w(wv[:, :, c0:c0 + 64], win_in[l][:, 4096 + g * 64:4096 + (g + 1) * 64], R_w, region=rg)
                    load_w(wv[:, :, 768:896], win_in[l][:, 4224:4352], R_w, base=768)
                    load_w(wv[:, :, 0:512], win_in[l][:, 3584:4096], R_w, base=0)
                    load_w(wv[:, :, 896:1408], win_in[l][:, 4352:4864], R_w, base=896)
                    cq, ck, cv_, cg = 0, 512, 768, 896
                    nkc, nkv = 2, 2
                    gq, gk = qkn[:, 2:3], qkn[:, 3:4]
                    wp_src = wpc_in[l]
                wpp = WB.take(4 * 1024).rearrange("p (k n) -> p k n", k=4)
                load_w(wpp, wp_src, R_wp)
                KT = BIG.take(nkc * S_ALL).rearrange("p (c n) -> p c n", c=nkc)
                VV_f = BIG.take(34 * nkv * 65)
                VV = VV_f.rearrange("p (t h d) -> p t h d", t=34, h=nkv)
                R_KT, R_VV = Region("KT"), Region("VV")
                mset("pool", VV_f, 1.0, [R_VV])
                R_tab = Region("tab")
                R_etb = Region("etb")
                prep_thunks = []
                sgt = [W32.take(512) for _ in range(4)]
                R_sgt = [Region("sgt%d" % i) for i in range(4)]
                if not is_wa:
                    tab16_f = W16.take(5120)
                    tab16 = tab16_f.rearrange("p (h n) -> p h n", h=8)
                    def mk_piece(ty, pc):
                        def f():
                            sb_i = pc % 4
                            dma("sp", sgt[sb_i], natab_in[l, ty][:, pc * 512:(pc + 1) * 512], [], [R_sgt[sb_i]])
                            act(tab16_f[:, pc * 512:(pc + 1) * 512], sgt[sb_i], AF.Exp, [R_sgt[sb_i]], [R_tab])
                            if pc == 9:
                                dma("sp", etab_d[ty], tab16_f, [R_tab], [R_etab[ty]])
                        return f
                    prep_thunks = [mk_piece(ty, pc) for ty in range(5) for pc in range(10)]
                sq16 = [W16.take(512) for _ in range(2)]
                R_sq = [Region("sq0"), Region("sq1")]
                sd32 = [W32.take(512) for _ in range(2)]
                R_sd = [Region("sd0"), Region("sd1")]
                rc32 = W32.take(512)
                rs32 = W32.take(512)
                R_rope = Region("rope")
                if is_wa:
                    kn16 = [W16.take(512) for _ in range(2)]
                    R_kn = [Region("kn0"), Region("kn1")]
                    ra32 = [W32.take(512) for _ in range(2)]
                    rb32 = [W32.take(512) for _ in range(2)]
                    R_ra = [Region("ra0"), Region("ra1")]
                    R_rb = [Region("rb0"), Region("rb1")]

                def skew(items):
                    ns = max(len(it) for it in items)
                    for step in range(len(items) + ns - 1):
                        for s_ in range(ns):
                            i_ = step - s_
                            if 0 <= i_ < len(items) and s_ < len(items[i_]):
                                items[i_][s_]()

                def normed_stages(idx, psi, wcol, hb, n, gain, out16, R_out, rope, post=None):
                    b = idx % 2
                    pst = 6

                    def s1():
                        proj_fm(psi, wv, wcol, hb, n, R_w)
                        act(sq16[b][:, 0:n], ps[psi][:, 0:n], AF.Square, [R_ps[psi]], [R_sq[b]])

                    def s2():
                        mm(ps[pst][:, 0:n], bones16, sq16[b][:, 0:n], True, True, [R_c, R_sq[b]], [R_ps[pst]])
                        act(sd32[b][:, 0:n], ps[pst][:, 0:n], AF.Sqrt, [R_ps[pst]], [R_sd[b]], bias=EPS)
                        recip(sd32[b][:, 0:n], sd32[b][:, 0:n], [R_sd[b]], [R_sd[b]])
                        if not rope:
                            stt("dve", out16, ps[psi][:, 0:n], gain, sd32[b][:, 0:n], ALU.mult, ALU.mult,
                                [R_ps[psi], R_small, R_sd[b]], [R_out])
                        else:
                            stt("dve", kn16[b][:, 0:n], ps[psi][:, 0:n], gain, sd32[b][:, 0:n], ALU.mult, ALU.mult,
                                [R_ps[psi], R_small, R_sd[b]], [R_kn[b]])

                    def s3():
                        mm(ps[pst][:, 0:n], pmat16, kn16[b][:, 0:n], True, True, [R_c, R_kn[b]], [R_ps[pst]])
                        tt("pool", ra32[b][:, 0:n], kn16[b][:, 0:n], rc32[:, 0:n], ALU.mult, [R_kn[b], R_rope], [R_ra[b]])
                        tt("dve", rb32[b][:, 0:n], ps[pst][:, 0:n], rs32[:, 0:n], ALU.mult, [R_ps[pst], R_rope], [R_rb[b]])
                        tt("pool", out16, ra32[b][:, 0:n], rb32[b][:, 0:n], ALU.add, [R_ra[b], R_rb[b]], [R_out])
                    st_ = [s1, s2, s3] if rope else [s1, s2]
                    if post is not None:
                        st_.append(post)
                    return st_

                load_hT(0, 0)
                for bi in range(9):
                    t0, n = BLOCKS[bi]
                    hb = bi % 2
                    if bi + 1 < 9:
                        load_hT(bi + 1, 1 - hb)
                    rope = is_wa and bi < 8
                    if rope:
                        dma("sp", rc32[:, 0:n], ropec_in[:, t0:t0 + n], [], [R_rope])
                        dma("sp", rs32[:, 0:n], ropes_in[:, t0:t0 + n], [], [R_rope])
                    items = []
                    for c in range(nkc):
                        items.append(normed_stages(c, c, ck + c * 128, hb, n, gk, KT[:, c, t0:t0 + n], R_KT, rope))
                    for t_ in range(n // 128):
                        def v1(t_=t_):
                            pi = 4 + t_ % 2
                            proj_tm(pi, wv, cv_, nkv * 64, hb, t_, R_w)

                        def v2(t_=t_):
                            pi = 4 + t_ % 2
                            gt = t0 // 128 + t_
                            src = ps[pi][:, 0:nkv * 64].rearrange("p (h d) -> p h d", h=nkv)
                            cp("act" if t_ % 2 == 0 else "dve", VV[:, gt, :, 0:64], src, [R_ps[pi]], [R_VV])
                        items.append([v1, v2])
                    skew(items)
                    for _ in range(7):
                        if prep_thunks:
                            prep_thunks.pop(0)()
                while prep_thunks:
                    prep_thunks.pop(0)()

                QT = W16.take(2048).rearrange("p (c n) -> p c n", c=4)
                R_QT = Region("QT")
                QTz = W16.take(4096).rearrange("p (c e n) -> p c e n", c=4, e=2)
                R_QTz = [Region("QTz%d" % c) for c in range(4)]
                R_QTc = [Region("QTc%d" % c) for c in range(4)]
                ogT = W16.take(2048).rearrange("p (c n) -> p c n", c=4)
                R_ogT = Region("ogT")
                og16 = W16.take(512)
                R_og16 = Region("og16")
                Pw = [W16.take(640) for _ in range(2)]
                R_Pw = [Region("Pw0"), Region("Pw1")]
                R_Pw5 = [Region("Pw5_0"), Region("Pw5_1")]
                PcE = [BIG.take(384) for _ in range(2)]
                R_Pc = [Region("Pc0"), Region("Pc1")]
                Ew = [BIG.take(512) for _ in range(2)]
                R_Ew = [Region("Ew0"), Region("Ew1")]
                og32 = W32.take(512)
                R_og32 = Region("og32")
                yb16 = WB.take(4096).rearrange("p (k n) -> p k n", k=8)
                R_yb = Region("yb")
                R_rec = Region("rec")
                cur_edge = [-1]
                load_hT(0, 0)
                for bi in range(nblk):
                    t0, n = BLOCKS[bi]
                    hb = bi % 2
                    if bi + 1 < nblk:
                        load_hT(bi + 1, 1 - hb)
                    rope = is_wa and bi < 8
                    if rope:
                        dma("sp", rc32[:, 0:n], ropec_in[:, t0:t0 + n], [], [R_rope])
                        dma("sp", rs32[:, 0:n], ropes_in[:, t0:t0 + n], [], [R_rope])
                    ntile = n // 128
                    items = []
                    for c in range(4):
                        def qz(c=c):
                            for e_ in range(2):
                                act(QTz[:, c, e_, 0:n], QT[:, c, 0:n], AF.Copy, [R_QTc[c], R_hm], [R_QTz[c]],
                                    scale=hmask[:, e_:e_ + 1])
                        items.append(normed_stages(c, c, cq + c * 128, hb, n, gq, QT[:, c, 0:n], R_QTc[c], rope, post=qz))
                    for t_ in range(ntile):
                        def g1(t_=t_):
                            proj_tm(4 + t_ % 2, wv, cg, 512, hb, t_, R_w)

                        def g2(t_=t_):
                            pi = 4 + t_ % 2
                            act(sgt[t_], ps[pi][:, 0:512], AF.Silu, [R_ps[pi]], [R_sgt[t_]])
                        items.append([g1, g2])
                    skew(items)

                    def scores(t_, h):
                        qt = t0 // 128 + t_
                        if bi == 8:
                            wch, tabv, R_tb = [], None, None
                        elif not is_wa:
                            ty, wch = na_chunks(qt)
                            if cur_edge[0] != ty:
                                dma("sp", tab16_f, etab_d[ty], [R_etab[ty]], [R_tab])
                                cur_edge[0] = ty
                            tabv, R_tb = tab16, R_tab
                        else:
                            wch = [c_ for c_ in (qt - 1, qt, qt + 1) if 0 <= c_ < 32]
                            moff = 128 if qt == 0 else 0
                            tabv, R_tb = None, R_c
                        cch = [32, 33]
                        nw = len(wch)
                        c2 = h // 2
                        pb = 64 * (h % 2)
                        kc = (h // 4) if is_wa else c2
                        hp = h % 2
                        pw_i, pc_i = (2, 4) if hp == 0 else (3, 0)
                        qv = QTz[:, c2, h % 2, t_ * 128:(t_ + 1) * 128]
                        for i_, kch in enumerate(cch):
                            mm(ps[pc_i][:, i_ * 128:(i_ + 1) * 128],
                               KT[:, kc, kch * 128:(kch + 1) * 128], qv, True, True,
                               [R_KT, R_QTz[c2]], [R_ps[pc_i]])
                        if nw == 5:
                            kch = wch[4]
                            mm(ps[pc_i][:, 256:384],
                               KT[:, kc, kch * 128:(kch + 1) * 128], qv, True, True,
                               [R_KT, R_QTz[c2]], [R_ps[pc_i]])
                        nce = 384 if nw == 5 else 256
                        act(PcE[hp][:, 0:nce], ps[pc_i][:, 0:nce], AF.Exp, [R_ps[pc_i]], [R_Pc[hp]])
                        for i_, kch in enumerate(wch[:4]):
                            mm(ps[pw_i][:, i_ * 128:(i_ + 1) * 128],
                               KT[:, kc, kch * 128:(kch + 1) * 128], qv, True, True,
                               [R_KT, R_QTz[c2]], [R_ps[pw_i]])
                        if nw:
                            n4 = min(nw, 4) * 128
                            if is_wa:
                                mk = wamask16[:, moff:moff + nw * 128]
                            else:
                                mk = tabv[:, h, 0:nw * 128]
                            if nw == 5:
                                tt("dve", Pw[hp][:, 512:640], PcE[hp][:, 256:384], mk[:, 512:640], ALU.mult,
                                   [R_Pc[hp], R_tb], [R_Pw5[hp]])
                            act(Ew[hp][:, 0:n4], ps[pw_i][:, 0:n4], AF.Exp, [R_ps[pw_i]], [R_Ew[hp]])
                            tt("dve", Pw[hp][:, 0:n4], Ew[hp][:, 0:n4], mk[:, 0:n4], ALU.mult,
                               [R_Ew[hp], R_tb], [R_Pw[hp]])
                        allch = [(PcE[hp][:, i_ * 128:(i_ + 1) * 128], kch, R_Pc[hp]) for i_, kch in enumerate(cch)]
                        if nw == 5:
                            allch.append((Pw[hp][:, 512:640], wch[4], R_Pw5[hp]))
                        allch += [(Pw[hp][:, i_ * 128:(i_ + 1) * 128], kch, R_Pw[hp]) for i_, kch in enumerate(wch[:4])]
                        return allch

                    def pv(t_, h, allch):
                        vh = (h // 4) if is_wa else h
                        opi = 5 if h < 4 else 1
                        ov = ps[opi][:, 0:260].rearrange("p (h d) -> p h d", h=4)[:, h % 4, :]
                        for i_, (pap, kch, rg) in enumerate(allch):
                            mm(ov, pap, VV[:, kch, vh, :], i_ == 0, i_ == len(allch) - 1, [rg, R_VV], [R_ps[opi]])

                    def epi_half(t_, g4):
                        opi = 5 if g4 == 0 else 1
                        o3 = ps[opi][:, 0:260].rearrange("p (h d) -> p h d", h=4)
                        rec = stat[:, 0:4]
                        if is_wa:
                            tt("dve", rec, o3[:, :, 64], esink[:, g4 * 4:(g4 + 1) * 4], ALU.add,
                               [R_ps[opi], R_small], [R_rec])
                            recip(rec, rec, [R_rec], [R_rec])
                        else:
                            recip(rec, o3[:, :, 64], [R_ps[opi]], [R_rec])
                        o32 = og32[:, 0:256].rearrange("p (h d) -> p h d", h=4)
                        tt("dve", o32, o3[:, :, 0:64], rec.unsqueeze(2).to_broadcast([128, 4, 64]), ALU.mult,
                           [R_ps[opi], R_rec], [R_og32])
                        tt("pool", og16[:, g4 * 256:(g4 + 1) * 256], og32[:, 0:256],
                           sgt[t_][:, g4 * 256:(g4 + 1) * 256], ALU.mult, [R_og32, R_sgt[t_]], [R_og16])

                    def transposes(t_):
                        pT4 = psT[:, 0:512].rearrange("p (c n) -> p c n", c=4)
                        for c in range(4):
                            tr(pT4[:, c, :], og16[:, c * 128:(c + 1) * 128], [R_og16], [R_psT])
                        cp("act", ogT[:, :, t_ * 128:(t_ + 1) * 128], pT4, [R_psT], [R_ogT])

                    units = [(t_, h) for t_ in range(ntile) for h in range(8)]
                    prev = None
                    pending_tr = []
                    for ui, (t_, h) in enumerate(units):
                        allch = scores(t_, h)
                        if prev is not None:
                            pt, ph, pch = prev
                            pv(pt, ph, pch)
                            if ph == 3:
                                epi_half(pt, 0)
                            if ph == 7:
                                epi_half(pt, 1)
                                pending_tr.append((ui + 2, pt))
                        while pending_tr and pending_tr[0][0] <= ui:
                            transposes(pending_tr.pop(0)[1])
                        prev = (t_, h, allch)
                    pt, ph, pch = prev
                    pv(pt, ph, pch)
                    epi_half(pt, 1)
                    pending_tr.append((0, pt))
                    while pending_tr:
                        transposes(pending_tr.pop(0)[1])
                    for i in range(8):
                        pi = 2 + i % 2
                        for c in range(4):
                            mm(ps[pi][:, 0:n], wpp[:, c, i * 128:(i + 1) * 128], ogT[:, c, 0:n], c == 0, c == 3,
                               [R_wp(i * 128), R_ogT], [R_ps[pi]])
                        cp("act" if i % 2 == 1 else "dve", yb16[:, i, 0:n], ps[pi][:, 0:n], [R_ps[pi]], [R_yb])
                    dma("sp", Y_d[1 + kind][:, :, t0:t0 + n], yb16[:, :, 0:n], [R_yb], [R_Y[1 + kind][bi]])

            for a in (BIG, WB, W32, W16):
                a.reset()
            S.barrier()
            gate_bc = W16.take(4096).bitcast(F32).rearrange("p (r n) -> p r n", r=2)
            dg32 = W32.take(128)
            R_dg = Region("dg")
            for r in range(2):
                for k in range(8):
                    ts("dve", dg32, ident32, modsb[:, 16 + k, r:r + 1], None, ALU.mult, None,
                       [R_mod2, R_c], [R_dg])
                    pi = 1 + (k // 4)
                    mm(ps[pi][:, (k % 4) * 128:(k % 4 + 1) * 128], ones32, dg32, True, True,
                       [R_c, R_dg], [R_ps[pi]])
                    if k % 4 == 3:
                        cp("act", gate_bc[:, r, (k // 4) * 512:(k // 4 + 1) * 512], ps[pi][:],
                           [R_ps[pi]], [R_gate])

            R_w = WReg()
            R_wo = WReg()
            wv = WB.take(8 * 3072).rearrange("p (k n) -> p k n", k=8)
            load_w(wv, win_in[l][:, 4864:7936], R_w, order=[0, 2, 4, 1, 3, 5], split_first=True)
            wo = BIG.take(8 * 1024).rearrange("p (k n) -> p k n", k=8)
            load_w(wo, wo_in[l], R_wo)
            Yb = [[BIG.take(4096).rearrange("p (k n) -> p k n", k=8) for _ in range(3)] for _ in range(2)]
            R_Yb = [[Region("Yb%d%d" % (i, j)) for j in range(3)] for i in range(2)]
            yT2 = [W16.take(4096).rearrange("p (k n) -> p k n", k=8) for _ in range(2)]
            R_yT2 = [Region("yT0"), Region("yT1")]
            sgm = [W32.take(512) for _ in range(2)]
            R_sgm = [Region("sgm0"), Region("sgm1")]
            acc = W32.take(512)
            R_acc = Region("acc")
            xt = [W32.take(1024) for _ in range(2)]
            R_xt = [Region("xt0"), Region("xt1")]
            xo = [W32.take(1024) for _ in range(2)]
            R_xo = [Region("xo0"), Region("xo1")]
            tmp2 = [W32.take(512) for _ in range(2)]
            R_tmp2 = [Region("tmp0"), Region("tmp1")]

            def ld_blk(bi):
                load_hT(bi, bi % 2)
                t0, n = BLOCKS[bi]
                for br in range(3):
                    dma("sp", Yb[bi % 2][br][:, :, 0:n], Y_d[br][:, :, t0:t0 + n], [R_Y[br][bi]], [R_Yb[bi % 2][br]])
            ld_blk(0)
            xcnt = 0
            for bi in range(nblk):
                t0, n = BLOCKS[bi]
                hb = bi % 2
                if bi + 1 < nblk:
                    ld_blk(bi + 1)
                r = 0 if bi < 8 else 1
                yT, R_yT = yT2[bi % 2], R_yT2[bi % 2]
                for i in range(8):
                    for br in range(3):
                        pi = (i * 3 + br) % 2
                        proj_fm(pi, wv, br * 1024 + i * 128, hb, n, R_w)
                        act(sgm[pi][:, 0:n], ps[pi][:, 0:n], AF.Sigmoid, [R_ps[pi]], [R_sgm[pi]])
                        if br == 0:
                            tt("dve", acc[:, 0:n], sgm[pi][:, 0:n], Yb[hb][br][:, i, 0:n], ALU.mult,
                               [R_sgm[pi], R_Yb[hb][br]], [R_acc])
                        else:
                            tt("pool", sgm[pi][:, 0:n], sgm[pi][:, 0:n], Yb[hb][br][:, i, 0:n], ALU.mult,
                               [R_sgm[pi], R_Yb[hb][br]], [R_sgm[pi]])
                            if br == 1:
                                tt("dve", acc[:, 0:n], acc[:, 0:n], sgm[pi][:, 0:n], ALU.add,
                                   [R_acc, R_sgm[pi]], [R_acc])
                            else:
                                tt("dve", yT[:, i, 0:n], acc[:, 0:n], sgm[pi][:, 0:n], ALU.add,
                                   [R_acc, R_sgm[pi]], [R_yT])
                for t_ in range(n // 128):
                    gt = t0 // 128 + t_
                    xb = xcnt % 2
                    xcnt += 1
                    src = xin[gt * 128:(gt + 1) * 128, :] if gt < 32 else cin[(gt - 32) * 128:(gt - 31) * 128, :]
                    dst = xout[gt * 128:(gt + 1) * 128, :] if gt < 32 else ctx1_d[(gt - 32) * 128:(gt - 31) * 128, :]
                    dma("sp", xt[xb], src, [R_x1[gt]] if l > 0 else [], [R_xt[xb]])
                    for hf in range(2):
                        pi = 2 + hf + 2 * (xb % 2)
                        tmp, R_tmp = tmp2[hf], R_tmp2[hf]
                        for i in range(8):
                            mm(ps[pi][:, :], yT[:, i, t_ * 128:(t_ + 1) * 128], wo[:, i, hf * 512:(hf + 1) * 512],
                               i == 0, i == 7, [R_yT, R_wo(hf * 512)], [R_ps[pi]])
                        tt("dve", tmp, ps[pi][:, :], gate_bc[:, r, hf * 512:(hf + 1) * 512], ALU.mult,
                           [R_ps[pi], R_gate], [R_tmp])
                        tt("pool", xo[xb][:, hf * 512:(hf + 1) * 512], tmp, xt[xb][:, hf * 512:(hf + 1) * 512],
                           ALU.add, [R_tmp, R_xt[xb]], [R_xo[xb]])
                    if last:
                        dma("sp", dst, xo[xb], [R_xo[xb]], [], is_output=True)
                    else:
                        dma("sp", dst, xo[xb], [R_xo[xb]], [R_x1[gt]])

        S.finish()
        S.emit_all()
    return nc


_CACHE = {}


def _fm(v, nchunk):
    sh = v.shape[:-1]
    return np.ascontiguousarray(np.swapaxes(v.reshape(*sh, nchunk, 128), -1, -2))


def prep_inputs(inp):
    global _NA_IDX
    if _NA_IDX is None:
        _NA_IDX = _na_index()
    if "cst" not in _CACHE:
        _CACHE["cst"] = _consts()
    cst, rc, rs = _CACHE["cst"]
    f = lambda a: np.ascontiguousarray(np.asarray(a, dtype=np.float32))
    x, c, ctx, c_ctx = f(inp["x"]), f(inp["c"]), f(inp["ctx"]), f(inp["c_ctx"])
    common = {
        "norm_gT": _fm(f(inp["norm_g"]), 8),
        "b_adaT": _fm(f(inp["b_ada"]), 24),
        "w_ada": f(inp["w_ada"]), "w_in": f(inp["w_in"]),
        "w_proj_a": f(inp["w_proj_a"]), "w_proj_b": f(inp["w_proj_b"]), "w_proj_c": f(inp["w_proj_c"]),
        "w_o": f(inp["w_o"]),
        "cst": cst, "rope_c": rc, "rope_s": rs,
    }
    cw = f(inp["conv_w"])
    cwT = np.transpose(cw.reshape(L, 31, 4, 128), (0, 3, 2, 1))
    common["conv_wT"] = np.ascontiguousarray(cwT.reshape(L, 128, 124))
    common["cvecT"] = np.ascontiguousarray(np.concatenate(
        [_fm(f(inp["conv_b"]), 4), _fm(f(inp["cln_g"]), 4), _fm(f(inp["cln_b"]), 4)], axis=-1))
    qk = np.stack([f(inp["na_q_norm"]), f(inp["na_k_norm"]), f(inp["wa_q_norm"]), f(inp["wa_k_norm"])], -1)
    common["qk_norm"] = np.ascontiguousarray(np.concatenate([qk, qk], axis=1))
    common["sink_bc"] = np.ascontiguousarray(np.broadcast_to(f(inp["wa_sink"])[:, None, :], (L, 128, 8)))
    rpb = f(inp["na_rpb"]).reshape(L, 8, 15 * 31)
    rpb_pad = np.concatenate([rpb, np.full((L, 8, 1), NEG, np.float32)], axis=-1)
    tab = rpb_pad[:, :, _NA_IDX]
    tab = np.transpose(tab, (0, 2, 3, 1, 4, 5))
    common["na_tab"] = np.ascontiguousarray(tab.reshape(L, 5, 128, 8 * 640))
    maps = []
    for b in range(8):
        m = dict(common)
        m["x"] = x[b]
        m["ctx"] = ctx[b]
        cc = np.stack([c[b], c_ctx], axis=-1)
        m["ccT"] = np.ascontiguousarray(np.transpose(cc.reshape(8, 128, 2), (1, 0, 2)))
        maps.append(m)
    return maps


def kernel(**inputs):
    if "nc" not in _CACHE:
        _CACHE["nc"] = build()
    maps = prep_inputs(inputs)
    res = run_bass_kernel_spmd(_CACHE["nc"], maps, core_ids=list(range(8)))
    return np.stack([np.asarray(r["out"], dtype=np.float32) for r in res.results], axis=0)
```

```python
import numpy as np
from contextlib import ExitStack
import concourse.bass as bass
import concourse.mybir as mybir
from concourse.bass_utils import run_bass_kernel_spmd

F32 = mybir.dt.float32
BF16 = mybir.dt.bfloat16
AF = mybir.ActivationFunctionType
ALU = mybir.AluOpType

L = 2
S_LAT = 4096
N_CTX = 256
S_ALL = S_LAT + N_CTX
D = 1024
D_IN = 7936
EPS = 1e-6
NEG = -30000.0
ENGS = ("pe", "act", "dve", "pool", "sp")
BLOCKS = [(b * 512, 512) for b in range(8)] + [(S_LAT, N_CTX)]


class Region:
    __slots__ = ("name", "lw", "reads")

    def __init__(self, name=""):
        self.name = name
        self.lw = None
        self.reads = {}


class Sched:
    NDS = 12

    def __init__(self, nc, stack):
        self.nc = nc
        self.ops = {e: [] for e in ENGS}
        self.cnt = {e: 0 for e in ENGS}
        self.semobj = {}
        for e in ENGS:
            self.semobj[("e", e)] = stack.enter_context(nc.semaphore("s_" + e))
        self.dq = {}
        for q in ("sp", "act", "pool"):
            lst = []
            for i in range(self.NDS):
                key = ("d", q, i)
                self.semobj[key] = stack.enter_context(nc.semaphore("d_%s%d" % (q, i)))
                lst.append([key, 0])
            self.dq[q] = [lst, 0]
        self.known = {e: {} for e in ENGS}
        self.out_tokens = []

    def _need(self, eng, tok, waits):
        if tok is None:
            return
        key, val = tok
        if eng == "pe" and key == ("e", "pe"):
            return
        if self.known[eng].get(key, 0) >= val:
            return
        if waits.get(key, 0) < val:
            waits[key] = val

    def _collect(self, eng, reads, writes, waits):
        for r in reads:
            self._need(eng, r.lw, waits)
        for w in writes:
            self._need(eng, w.lw, waits)
            for k, v in w.reads.items():
                self._need(eng, (k, v), waits)
        for k, v in waits.items():
            self.known[eng][k] = v
        return [(self.semobj[k], v) for k, v in waits.items()]

    def op(self, eng, fn, reads=(), writes=()):
        wl = self._collect(eng, reads, writes, {})
        self.cnt[eng] += 1
        seq = self.cnt[eng]
        key = ("e", eng)
        sem = self.semobj[key]
        for r in reads:
            r.reads[key] = seq
        for w in writes:
            w.lw = (key, seq)
            w.reads = {}

        def emit(e):
            for s, v in wl:
                e.wait_ge(s, v)
            fn(e).then_inc(sem, 1)
        self.ops[eng].append(emit)

    def dma(self, q, fn, reads=(), writes=(), is_output=False):
        lst, idx = self.dq[q]
        ent = lst[idx % self.NDS]
        self.dq[q][1] = idx + 1
        key = ent[0]
        waits = {}
        if ent[1] > 0:
            self._need(q, (key, ent[1]), waits)
        wl = self._collect(q, reads, writes, waits)
        ent[1] += 16
        val = ent[1]
        sem = self.semobj[key]
        for r in reads:
            if r.reads.get(key, 0) < val:
                r.reads[key] = val
        for w in writes:
            w.lw = (key, val)
            w.reads = {}
        if is_output:
            self.out_tokens.append((key, val))

        def emit(e):
            for s, v in wl:
                e.wait_ge(s, v)
            fn(e).then_inc(sem, 16)
        self.ops[q].append(emit)

    def barrier(self):
        toks = [(("e", x), self.cnt[x]) for x in ENGS if self.cnt[x] > 0]
        for q in self.dq:
            for key, val in self.dq[q][0]:
                if val > 0:
                    toks.append((key, val))
        for e in ENGS:
            waits = {}
            for key, val in toks:
                if key == ("e", e):
                    continue
                if self.known[e].get(key, 0) < val:
                    waits[key] = val
                    self.known[e][key] = val
            wl = [(self.semobj[k], v) for k, v in waits.items()]
            if wl:
                def emit(eo, wl=wl):
                    for s, v in wl:
                        eo.wait_ge(s, v)
                self.ops[e].append(emit)

    def finish(self):
        final = {}
        for k, v in self.out_tokens:
            final[k] = max(final.get(k, 0), v)
        wl = [(self.semobj[k], v) for k, v in final.items()]

        def emit(e):
            for s, v in wl:
                e.wait_ge(s, v)
        self.ops["sp"].append(emit)

    def emit_all(self):
        with self.nc.Block() as block:
            @block.tensor
            def _(e):
                for f in self.ops["pe"]:
                    f(e)

            @block.scalar
            def _(e):
                for f in self.ops["act"]:
                    f(e)

            @block.vector
            def _(e):
                for f in self.ops["dve"]:
                    f(e)

            @block.gpsimd
            def _(e):
                for f in self.ops["pool"]:
                    f(e)

            @block.sync
            def _(e):
                for f in self.ops["sp"]:
                    f(e)


class Arena:
    def __init__(self, t, size, name):
        self.t, self.size, self.name, self.off = t, size, name, 0

    def reset(self):
        self.off = 0

    def take(self, n, name=""):
        n16 = (n + 15) // 16 * 16
        assert self.off + n16 <= self.size, (self.name, name, self.off, n, self.size)
        ap = self.t[:, self.off:self.off + n]
        self.off += n16
        return ap


NA_TYPES = [None, 0, 2, 60, 62]


def na_chunks(qt):
    i0 = 2 * qt
    if i0 <= 2:
        return 1 + i0 // 2, [0, 1, 2, 3]
    if i0 >= 60:
        return 3 + (i0 - 60) // 2, [28, 29, 30, 31]
    return 0, [qt - 2, qt - 1, qt, qt + 1, qt + 2]


def _na_index():
    idx = np.full((5, 128, 5, 128), 15 * 31, dtype=np.int64)
    for ty in range(5):
        qt = 8 if ty == 0 else NA_TYPES[ty] // 2
        _, chunks = na_chunks(qt)
        for ci, kc in enumerate(chunks):
            for p in range(128):
                r = 2 * kc + p // 64
                c = p % 64
                for n in range(128):
                    i = 2 * qt + n // 64
                    j = n % 64
                    rs = min(max(i - 4, 0), 56)
                    cs = min(max(j - 8, 0), 48)
                    if rs <= r < rs + 8 and cs <= c < cs + 16:
                        idx[ty, p, ci, n] = (r - i + 7) * 31 + (c - j + 15)
    return idx


_NA_IDX = None


def _consts():
    cst = np.zeros((128, 5 * 128 + 384), np.float32)
    cst[:, 0:128] = np.eye(128)
    pm = np.zeros((128, 128), np.float32)
    for m in range(128):
        sub = m % 32
        if sub < 16:
            pm[m + 16, m] = 1.0
        else:
            pm[m - 16, m] = 1.0
    cst[:, 128:256] = pm
    bo = np.zeros((128, 128), np.float32)
    bo[0:64, 0:64] = 1.0 / 64
    bo[64:128, 64:128] = 1.0 / 64
    cst[:, 256:384] = bo
    cst[:, 384:512] = 1.0 / 512
    cst[:, 512:640] = 1.0
    p = np.arange(128)[:, None]
    n = np.arange(128)[None, :]
    cst[:, 640:768] = (n <= p)
    cst[:, 768:896] = 1.0
    cst[:, 896:1024] = (p <= n)
    t = np.arange(S_LAT)
    rowp = (t // 64).astype(np.float32)
    colp = (t % 64).astype(np.float32)
    inv = (10000.0 ** (-np.arange(16, dtype=np.float32) / 16)).astype(np.float32)
    rc = np.zeros((128, S_LAT), np.float32)
    rs = np.zeros((128, S_LAT), np.float32)
    for pp in range(128):
        d = pp % 64
        sub = d % 32
        pos = rowp if d < 32 else colp
        ang = (pos * inv[sub % 16]).astype(np.float32)
        rc[pp] = np.cos(ang)
        rs[pp] = -np.sin(ang) if sub < 16 else np.sin(ang)
    return cst, rc, rs


def build(debug=False):
    nc = bass.Bass("TRN2", target_bir_lowering=False)
    EI = "ExternalInput"
    dbgk = "ExternalOutput" if debug else "Internal"

    def din(name, shape):
        return nc.dram_tensor(name, list(shape), F32, kind=EI).ap()

    x_in = din("x", [S_LAT, D])
    ctx_in = din("ctx", [N_CTX, D])
    ccT_in = din("ccT", [128, 8, 2])
    normg_in = din("norm_gT", [L, 128, 8])
    bada_in = din("b_adaT", [L, 128, 24])
    wada_in = din("w_ada", [L, D, 3 * D])
    win_in = din("w_in", [L, D, D_IN])
    wpa_in = din("w_proj_a", [L, 512, D])
    wpb_in = din("w_proj_b", [L, 512, D])
    wpc_in = din("w_proj_c", [L, 512, D])
    wo_in = din("w_o", [L, D, D])
    convw_in = din("conv_wT", [L, 128, 4 * 31])
    cvec_in = din("cvecT", [L, 128, 12])
    qkn_in = din("qk_norm", [L, 128, 4])
    sink_in = din("sink_bc", [L, 128, 8])
    natab_in = din("na_tab", [L, 5, 128, 8 * 640])
    cst_in = din("cst", [128, 1024])
    ropec_in = din("rope_c", [128, S_LAT])
    ropes_in = din("rope_s", [128, S_LAT])
    out_x = nc.dram_tensor("out", [S_LAT, D], F32, kind="ExternalOutput").ap()

    hT_d = nc.dram_tensor("hT_d", [128, 8, S_ALL], BF16, kind=dbgk).ap()
    Y_d = [nc.dram_tensor("Y%d_d" % i, [128, 8, S_ALL], BF16, kind=dbgk).ap() for i in range(3)]
    x1_d = nc.dram_tensor("x1_d", [S_LAT, D], F32, kind=dbgk).ap()
    ctx1_d = nc.dram_tensor("ctx1_d", [N_CTX, D], F32, kind=dbgk).ap()
    etab_d = nc.dram_tensor("etab_d", [5, 128, 5120], BF16, kind="Internal").ap()

    R_hT = [Region("hT%d" % i) for i in range(34)]
    R_Y = [[Region("Y%d_%d" % (br, b)) for b in range(9)] for br in range(3)]
    R_x1 = [Region("x1_%d" % i) for i in range(34)]
    R_etab = [Region("etab%d" % i) for i in range(5)]

    with ExitStack() as st:
        S = Sched(nc, st)

        def sb(name, shape, dt):
            return st.enter_context(nc.sbuf_tensor(name, list(shape), dt))

        BIG = Arena(sb("BIG", [128, 37120], BF16), 37120, "BIG")
        WB = Arena(sb("WB", [128, 24576], BF16), 24576, "WB")
        W32 = Arena(sb("W32", [128, 7168], F32), 7168, "W32")
        W16 = Arena(sb("W16", [128, 16384], BF16), 16384, "W16")
        HTB = sb("HTB", [128, 2, 8, 512], BF16)
        cst16 = sb("cst16", [128, 1024], BF16)
        cst32 = sb("cst32", [128, 256], F32)
        silc = sb("silc", [128, 8, 2], BF16)
        cc32 = sb("cc32", [128, 8, 2], F32)
        modsb = sb("modsb", [128, 24, 2], F32)
        gsb = sb("gsb", [128, 8, 2], F32)
        normg = sb("normg", [128, 8], F32)
        bada = sb("bada", [128, 24], F32)
        convw = sb("convw", [128, 124], F32)
        cvec = sb("cvec", [128, 12], F32)
        qkn = sb("qkn", [128, 4], F32)
        esink = sb("esink", [128, 8], F32)
        stat = sb("stat", [128, 8], F32)
        hmask = sb("hmask", [128, 2], F32)
        ps = [st.enter_context(nc.psum_tensor("ps%d" % i, [128, 512], F32)) for i in range(7)]
        psT = st.enter_context(nc.psum_tensor("psT", [128, 1024], BF16))
        R_ps = [Region("ps%d" % i) for i in range(7)]
        R_psT = Region("psT")
        R_HTB = [Region("HTB0"), Region("HTB1")]
        R_c = Region("consts")
        R_small = Region("small")
        R_gate = Region("gate_bc")

        ident16 = cst16[:, 0:128]
        pmat16 = cst16[:, 128:256]
        bones16 = cst16[:, 256:384]
        o512_16 = cst16[:, 384:512]
        wamask16 = cst16[:, 640:1024]
        ident32 = cst32[:, 0:128]
        ones32 = cst32[:, 128:256]

        def mm(out, lhsT, rhs, start, stop, rd, wr):
            S.op("pe", lambda e: e.matmul(out, lhsT=lhsT, rhs=rhs, start=start, stop=stop), rd, wr)

        def tr(out, in_, rd, wr):
            S.op("pe", lambda e: e.transpose(out=out, in_=in_, identity=ident16), rd + [R_c], wr)

        def act(out, in_, func, rd, wr, **kw):
            S.op("act", lambda e: e.activation(out=out, in_=in_, func=func, **kw), rd, wr)

        def tt(eng, out, in0, in1, op, rd, wr):
            S.op(eng, lambda e: e.tensor_tensor(out=out, in0=in0, in1=in1, op=op), rd, wr)

        def ts(eng, out, in0, s1, s2, op0, op1, rd, wr):
            if op1 is None:
                S.op(eng, lambda e: e.tensor_scalar(out=out, in0=in0, scalar1=s1, scalar2=None, op0=op0), rd, wr)
            else:
                S.op(eng, lambda e: e.tensor_scalar(out=out, in0=in0, scalar1=s1, scalar2=s2, op0=op0, op1=op1), rd, wr)

        def stt(eng, out, in0, scalar, in1, op0, op1, rd, wr):
            S.op(eng, lambda e: e.scalar_tensor_tensor(out=out, in0=in0, scalar=scalar, in1=in1, op0=op0, op1=op1), rd, wr)

        def cp(eng, out, in_, rd, wr):
            if eng == "act":
                S.op(eng, lambda e: e.activation(out=out, in_=in_, func=AF.Copy), rd, wr)
            else:
                S.op(eng, lambda e: e.tensor_copy(out=out, in_=in_), rd, wr)

        def recip(out, in_, rd, wr):
            S.op("dve", lambda e: e.reciprocal(out=out, in_=in_), rd, wr)

        def mset(eng, ap, val, wr):
            S.op(eng, lambda e: e.memset(ap, val), [], wr)

        def dma(q, out, in_, rd, wr, is_output=False):
            S.dma(q, lambda e: e.dma_start(out=out, in_=in_), rd, wr, is_output=is_output)

        class WReg:
            def __init__(self):
                self.rs = []

            def add(self, a, b):
                r = Region("w%d" % a)
                self.rs.append((a, b, r))
                return r

            def __call__(self, c):
                for a, b, r in self.rs:
                    if a <= c < b:
                        return r
                raise KeyError(c)

        def load_w(dst3, src2, wreg, base=0, order=None, region=None, split_first=False):
            n = src2.shape[1]
            starts = list(range(0, n, 512))
            if order is not None:
                starts = [starts[i] for i in order]
            pieces = []
            for i_, c0 in enumerate(starts):
                cw = min(512, n - c0)
                if split_first and i_ == 0 and cw == 512:
                    pieces += [(c0 + q_ * 128, 128) for q_ in range(4)]
                else:
                    pieces.append((c0, cw))
            for c0, cw in pieces:
                dma("pool", dst3[:, :, c0:c0 + cw],
                    src2[:, c0:c0 + cw].rearrange("(k p) n -> p k n", p=128), [],
                    [region if region is not None else wreg.add(base + c0, base + c0 + cw)])

        def load_hT(bi, buf):
            t0, n = BLOCKS[bi]
            rr = R_hT[t0 // 128:(t0 + n) // 128]
            dma("sp", HTB[:, buf, :, 0:n], hT_d[:, :, t0:t0 + n], rr, [R_HTB[buf]])

        def proj_fm(psi, wview, c0, hbuf, n, wreg):
            for k in range(8):
                mm(ps[psi][:, 0:n], wview[:, k, c0:c0 + 128], HTB[:, hbuf, k, 0:n],
                   k == 0, k == 7, [wreg(c0), R_HTB[hbuf]], [R_ps[psi]])

        def proj_tm(psi, wview, c0, ncols, hbuf, tt_, wreg):
            for k in range(8):
                mm(ps[psi][:, 0:ncols], HTB[:, hbuf, k, tt_ * 128:(tt_ + 1) * 128],
                   wview[:, k, c0:c0 + ncols], k == 0, k == 7, [wreg(c0), R_HTB[hbuf]], [R_ps[psi]])

        R_hm = Region("hmask")
        mset("pool", hmask[:, :], 0.0, [R_hm])
        mset("pool", hmask[0:64, 0:1], 1.0, [R_hm])
        mset("pool", hmask[64:128, 1:2], 1.0, [R_hm])
        dma("pool", cst16[:], cst_in, [], [R_c])
        dma("sp", cst32[:, 0:128], cst_in[:, 0:128], [], [R_c])
        dma("sp", cst32[:, 128:256], cst_in[:, 512:640], [], [R_c])
        dma("sp", cc32[:], ccT_in, [], [R_small])
        act(silc[:], cc32[:], AF.Silu, [R_small], [R_small])

        for l in range(L):
            last = (l == L - 1)
            nblk = 8 if last else 9
            xin = x_in if l == 0 else x1_d
            cin = ctx_in if l == 0 else ctx1_d
            xout = out_x if last else x1_d
            for a in (BIG, WB, W32, W16):
                a.reset()
            S.barrier()

            R_w = WReg()
            wad = WB.take(8 * 3072).rearrange("p (k n) -> p k n", k=8)
            load_w(wad, wada_in[l], R_w, split_first=True)
            dma("sp", normg[:], normg_in[l], [], [R_small])
            dma("sp", bada[:], bada_in[l], [], [R_small])
            dma("sp", convw[:], convw_in[l], [], [R_small])
            dma("sp", cvec[:], cvec_in[l], [], [R_small])
            dma("sp", qkn[:], qkn_in[l], [], [R_small])
            dma("sp", esink[:], sink_in[l], [], [R_small])
            modps = ps[0][:, 0:48].rearrange("p (f r) -> p f r", f=24)
            for f in range(16):
                for k in range(8):
                    mm(modps[:, f, :], wad[:, k, f * 128:(f + 1) * 128], silc[:, k, :],
                       k == 0, k == 7, [R_w(f * 128), R_small], [R_ps[0]])
            tt("dve", modsb[:, 0:16, :], modps[:, 0:16, :], bada[:, 0:16].unsqueeze(2).to_broadcast([128, 16, 2]), ALU.add,
               [R_ps[0], R_small], [R_small])
            ts("dve", gsb[:], modsb[:, 8:16, :], 1.0, None, ALU.add, None, [R_small], [R_small])
            tt("dve", gsb[:], gsb[:], normg[:].unsqueeze(2).to_broadcast([128, 8, 2]), ALU.mult,
               [R_small], [R_small])
            ts("dve", qkn[:, 0:1], qkn[:, 0:1], 0.125, None, ALU.mult, None, [R_small], [R_small])
            ts("dve", qkn[:, 2:3], qkn[:, 2:3], 0.125, None, ALU.mult, None, [R_small], [R_small])
            act(esink[:], esink[:], AF.Exp, [R_small], [R_small])

            W16.reset()
            dgs = BIG.take(124 * 128).rearrange("p (j k n) -> p j k n", j=4, k=31)
            R_dgs = Region("dgs")
            dg_todo = [(j, k) for j in range(4) for k in range(31)]

            def build_dg(cnt):
                for i_ in range(cnt):
                    if dg_todo:
                        j, k = dg_todo.pop(0)
                        if False:
                            ts("dve", dgs[:, j, k, :], ident32, convw[:, j * 31 + k:j * 31 + k + 1], None,
                               ALU.mult, None, [R_c, R_small], [R_dgs])
                        else:
                            act(dgs[:, j, k, :], ident32, AF.Copy, [R_c, R_small], [R_dgs],
                                scale=convw[:, j * 31 + k:j * 31 + k + 1])
            xt = [W32.take(1024) for _ in range(3)]
            t32 = [W32.take(1024) for _ in range(2)]
            junk16 = W16.take(1024)
            xn16 = [W16.take(1024) for _ in range(2)]
            hts16 = [W16.take(1024) for _ in range(2)]
            R_xt = [Region("xt%d" % i) for i in range(3)]
            R_t32, R_junk = [Region("t32a"), Region("t32b")], Region("junk")
            R_xn = [Region("xn0"), Region("xn1")]
            R_hts = [Region("hts0"), Region("hts1")]
            R_st = [Region("st0"), Region("st1")]
            psT3 = psT[:, :].rearrange("p (k n) -> p k n", k=8)

            def a_stages(ti):
                b = ti % 2
                b3 = ti % 3
                r = 0 if ti < 32 else 1
                t32v = t32[b].rearrange("p (k n) -> p k n", k=8)
                htv = hts16[b].rearrange("p (k n) -> p k n", k=8)

                def s0():
                    src = xin[ti * 128:(ti + 1) * 128, :] if ti < 32 else cin[(ti - 32) * 128:(ti - 31) * 128, :]
                    rd = [R_x1[ti]] if l > 0 else []
                    dma("sp", xt[b3], src, rd, [R_xt[b3]])

                def s1():
                    mset("dve", stat[:, b:b + 1], 0.0, [R_st[b]])
                    act(junk16, xt[b3], AF.Square, [R_xt[b3]], [R_junk, R_st[b]], accum_out=stat[:, b:b + 1])
                    act(stat[:, 2 + b:3 + b], stat[:, b:b + 1], AF.Sqrt, [R_st[b]], [R_st[b]],
                        scale=1.0 / D, bias=EPS)
                    recip(stat[:, 4 + b:5 + b], stat[:, 2 + b:3 + b], [R_st[b]], [R_st[b]])
                    ts("dve", xn16[b], xt[b3], stat[:, 4 + b:5 + b], None, ALU.mult, None,
                       [R_xt[b3], R_st[b]], [R_xn[b]])

                def s2():
                    for k in range(8):
                        tr(psT3[:, k, :], xn16[b][:, k * 128:(k + 1) * 128], [R_xn[b]], [R_psT])
                    tt("dve", t32v, psT3, gsb[:, :, r:r + 1].to_broadcast([128, 8, 128]), ALU.mult,
                       [R_psT, R_small], [R_t32[b]])

                def s3():
                    tt("pool", htv, t32v, modsb[:, 0:8, r:r + 1].to_broadcast([128, 8, 128]), ALU.add,
                       [R_t32[b], R_small], [R_hts[b]])
                    dma("sp", hT_d[:, :, ti * 128:(ti + 1) * 128], htv, [R_hts[b]], [R_hT[ti]])
                return [s0, s1, s2, s3]

            a_items = [a_stages(ti) for ti in range(34)]
            for step in range(34 + 3):
                for s_ in range(4):
                    i_ = step - s_
                    if 0 <= i_ < 34:
                        a_items[i_][s_]()
                build_dg(4)
            build_dg(124)

            R_mod2 = Region("mod2")
            modps2 = ps[1][:, 0:16].rearrange("p (f r) -> p f r", f=8)
            for f in range(16, 24):
                for k in range(8):
                    mm(modps2[:, f - 16, :], wad[:, k, f * 128:(f + 1) * 128], silc[:, k, :],
                       k == 0, k == 7, [R_w(f * 128), R_small], [R_ps[1]])
            tt("dve", modsb[:, 16:24, :], modps2, bada[:, 16:24].unsqueeze(2).to_broadcast([128, 8, 2]), ALU.add,
               [R_ps[1], R_small], [R_mod2])
            for a in (WB, W32, W16):
                a.reset()
            S.barrier()
            R_w = WReg()
            R_wp = WReg()
            wv = WB.take(8 * 1536).rearrange("p (k n) -> p k n", k=8)
            wpa = WB.take(4 * 1024).rearrange("p (k n) -> p k n", k=4)
            load_w(wv, win_in[l][:, 0:1536], R_w, split_first=True)
            load_w(wpa, wpa_in[l], R_wp)
            A_lat_f = BIG.take(4 * (S_LAT + 32))
            A_ctx_f = BIG.take(4 * (N_CTX + 32))
            A_lat = A_lat_f.rearrange("p (j n) -> p j n", j=4)
            A_ctx = A_ctx_f.rearrange("p (j n) -> p j n", j=4)
            R_A = Region("A")
            mset("pool", A_lat_f, 0.0, [R_A])
            mset("pool", A_ctx_f, 0.0, [R_A])
            sg32 = [W32.take(512) for _ in range(2)]
            R_sg = [Region("sg0"), Region("sg1")]
            load_hT(0, 0)
            for bi in range(nblk):
                t0, n = BLOCKS[bi]
                hb = bi % 2
                if bi + 1 < nblk:
                    load_hT(bi + 1, 1 - hb)
                Ab, a0 = (A_lat, t0) if bi < 8 else (A_ctx, 0)
                for j in range(4):
                    pa, pb_ = (0, 1) if j % 2 == 0 else (2, 3)
                    proj_fm(pb_, wv, 512 + j * 128, hb, n, R_w)
                    proj_fm(pa, wv, j * 128, hb, n, R_w)
                    sb_ = j % 2
                    act(sg32[sb_][:, 0:n], ps[pb_][:, 0:n], AF.Sigmoid, [R_ps[pb_]], [R_sg[sb_]])
                    tt("dve", Ab[:, j, 15 + a0:15 + a0 + n], ps[pa][:, 0:n], sg32[sb_][:, 0:n], ALU.mult,
                       [R_ps[pa], R_sg[sb_]], [R_A])
            v32 = W32.take(2048).rearrange("p (j n) -> p j n", j=4)
            mean32 = W32.take(512)
            var32 = W32.take(512)
            rstd32 = W32.take(512)
            t1 = [W32.take(512) for _ in range(2)]
            t2 = [W32.take(512) for _ in range(2)]
            sga = sg32
            v16 = W16.take(2048).rearrange("p (j n) -> p j n", j=4)
            sq16 = W16.take(2048).rearrange("p (j n) -> p j n", j=4)
            z16 = W16.take(2048).rearrange("p (j n) -> p j n", j=4)
            ya16 = WB.take(4096).rearrange("p (k n) -> p k n", k=8)
            R_v32, R_v16, R_sq16, R_z16, R_ya = (Region("v32"), Region("v16"), Region("sq16"),
                                                 Region("z16"), Region("ya"))
            R_mean, R_var, R_rstd = Region("mean"), Region("var"), Region("rstd")
            R_v32j = [Region("v32_%d" % j) for j in range(4)]
            NPE = 31
            R_t1 = [Region("t1a"), Region("t1b")]
            R_t2 = [Region("t2a"), Region("t2b")]
            load_hT(0, 0)
            for bi in range(nblk):
                t0, n = BLOCKS[bi]
                hb = bi % 2
                if bi + 1 < nblk:
                    load_hT(bi + 1, 1 - hb)
                Ab, a0 = (A_lat, t0) if bi < 8 else (A_ctx, 0)
                for j in range(4):
                    pi = j % 2
                    for k in range(NPE):
                        mm(ps[pi][:, 0:n], dgs[:, j, k, :], Ab[:, j, a0 + k:a0 + k + n], k == 0, k == NPE - 1,
                           [R_dgs, R_A], [R_ps[pi]])
                    act(v32[:, j, 0:n], ps[pi][:, 0:n], AF.Identity, [R_ps[pi], R_small], [R_v32j[j]],
                        bias=cvec[:, j:j + 1])
                    for k in range(NPE, 31):
                        stt("dve", v32[:, j, 0:n], Ab[:, j, a0 + k:a0 + k + n], convw[:, j * 31 + k:j * 31 + k + 1],
                            v32[:, j, 0:n], ALU.mult, ALU.add, [R_A, R_small, R_v32j[j]], [R_v32j[j]])
                    act(v16[:, j, 0:n], ps[pi][:, 0:n], AF.Identity, [R_ps[pi], R_small], [R_v16],
                        bias=cvec[:, j:j + 1])
                    act(sq16[:, j, 0:n], v32[:, j, 0:n], AF.Square, [R_v32j[j]], [R_sq16])
                for j in range(4):
                    mm(ps[2][:, 0:n], o512_16, v16[:, j, 0:n], j == 0, j == 3, [R_c, R_v16], [R_ps[2]])
                for j in range(4):
                    mm(ps[3][:, 0:n], o512_16, sq16[:, j, 0:n], j == 0, j == 3, [R_c, R_sq16], [R_ps[3]])
                cp("dve", mean32[:, 0:n], ps[2][:, 0:n], [R_ps[2]], [R_mean])
                tt("dve", var32[:, 0:n], mean32[:, 0:n], mean32[:, 0:n], ALU.mult, [R_mean], [R_var])
                tt("dve", var32[:, 0:n], ps[3][:, 0:n], var32[:, 0:n], ALU.subtract, [R_ps[3], R_var], [R_var])
                act(rstd32[:, 0:n], var32[:, 0:n], AF.Sqrt, [R_var], [R_rstd], bias=EPS)
                recip(rstd32[:, 0:n], rstd32[:, 0:n], [R_rstd], [R_rstd])
                for j in range(4):
                    b2 = j % 2
                    pi = 4 + b2
                    proj_fm(pi, wv, 1024 + j * 128, hb, n, R_w)
                    act(sga[b2][:, 0:n], ps[pi][:, 0:n], AF.Silu, [R_ps[pi]], [R_sg[b2]])
                    tt("dve", t1[b2][:, 0:n], v32[:, j, 0:n], mean32[:, 0:n], ALU.subtract,
                       [R_v32j[j], R_mean], [R_t1[b2]])
                    tt("dve", t1[b2][:, 0:n], t1[b2][:, 0:n], rstd32[:, 0:n], ALU.mult,
                       [R_t1[b2], R_rstd], [R_t1[b2]])
                    act(t2[b2][:, 0:n], t1[b2][:, 0:n], AF.Silu, [R_t1[b2], R_small], [R_t2[b2]],
                        scale=cvec[:, 4 + j:5 + j], bias=cvec[:, 8 + j:9 + j])
                    tt("pool" if j % 2 == 0 else "dve", z16[:, j, 0:n], t2[b2][:, 0:n], sga[b2][:, 0:n], ALU.mult,
                       [R_t2[b2], R_sg[b2]], [R_z16])
                for i in range(8):
                    pi = i % 2
                    for j in range(4):
                        mm(ps[pi][:, 0:n], wpa[:, j, i * 128:(i + 1) * 128], z16[:, j, 0:n], j == 0, j == 3,
                           [R_wp(i * 128), R_z16], [R_ps[pi]])
                    if i % 2 == 1:
                        cp("act", ya16[:, i, 0:n], ps[pi][:, 0:n], [R_ps[pi]], [R_ya])
                    else:
                        cp("dve", ya16[:, i, 0:n], ps[pi][:, 0:n], [R_ps[pi]], [R_ya])
                dma("sp", Y_d[0][:, :, t0:t0 + n], ya16[:, :, 0:n], [R_ya], [R_Y[0][bi]])

            for kind in (0, 1):
                for a in (BIG, WB, W32, W16):
                    a.reset()
                S.barrier()
                is_wa = kind == 1
                R_w = WReg()
                R_wp = WReg()
                if not is_wa:
                    ncol = 2048
                    wv = WB.take(8 * ncol).rearrange("p (k n) -> p k n", k=8)
                    load_w(wv, win_in[l][:, 1536:3584], R_w, order=[1, 2, 0, 3], split_first=True)
                    cq, ck, cv_, cg = 0, 512, 1024, 1536
                    nkc, nkv = 4, 8
                    gq, gk = qkn[:, 0:1], qkn[:, 1:2]
                    wp_src = wpb_in[l]
                else:
                    ncol = 1408
                    wv = WB.take(8 * ncol).rearrange("p (k n) -> p k n", k=8)
                    for g in range(2):
                        rg = R_w.add(512 + g * 128, 512 + (g + 1) * 128)
                        for dup in range(2):
                            c0 = 512 + g * 128 + dup * 64
                            load_w(wv[:, :, c0:c0 + 64], win_in[l][:, 4096 + g * 64:4096 + (g + 1) * 64], R_w, region=rg)
                    load_w(wv[:, :, 768:896], win_in[l][:, 4224:4352], R_w, base=768)
                    load_w(wv[:, :, 0:512], win_in[l][:, 3584:4096], R_w, base=0)
                    load_w(wv[:, :, 896:1408], win_in[l][:, 4352:4864], R_w, base=896)
                    cq, ck, cv_, cg = 0, 512, 768, 896
                    nkc, nkv = 2, 2
                    gq, gk = qkn[:, 2:3], qkn[:, 3:4]
                    wp_src = wpc_in[l]
                wpp = WB.take(4 * 1024).rearrange("p (k n) -> p k n", k=4)
                load_w(wpp, wp_src, R_wp)
                KT = BIG.take(nkc * S_ALL).rearrange("p (c n) -> p c n", c=nkc)
                VV_f = BIG.take(34 * nkv * 65)
                VV = VV_f.rearrange("p (t h d) -> p t h d", t=34, h=nkv)
                R_KT, R_VV = Region("KT"), Region("VV")
                mset("pool", VV_f, 1.0, [R_VV])
                R_tab = Region("tab")
                R_etb = Region("etb")
                prep_thunks = []
                sgt = [W32.take(512) for _ in range(4)]
                R_sgt = [Region("sgt%d" % i) for i in range(4)]
                if not is_wa:
                    tab16_f = W16.take(5120)
                    tab16 = tab16_f.rearrange("p (h n) -> p h n", h=8)
                    def mk_piece(ty, pc):
                        def f():
                            sb_i = pc % 4
                            dma("sp", sgt[sb_i], natab_in[l, ty][:, pc * 512:(pc + 1) * 512], [], [R_sgt[sb_i]])
                            act(tab16_f[:, pc * 512:(pc + 1) * 512], sgt[sb_i], AF.Exp, [R_sgt[sb_i]], [R_tab])
                            if pc == 9:
                                dma("sp", etab_d[ty], tab16_f, [R_tab], [R_etab[ty]])
                        return f
                    prep_thunks = [mk_piece(ty, pc) for ty in range(5) for pc in range(10)]
                sq16 = [W16.take(512) for _ in range(2)]
                R_sq = [Region("sq0"), Region("sq1")]
                sd32 = [W32.take(512) for _ in range(2)]
                R_sd = [Region("sd0"), Region("sd1")]
                rc32 = W32.take(512)
                rs32 = W32.take(512)
                R_rope = Region("rope")
                if is_wa:
                    kn16 = [W16.take(512) for _ in range(2)]
                    R_kn = [Region("kn0"), Region("kn1")]
                    ra32 = [W32.take(512) for _ in range(2)]
                    rb32 = [W32.take(512) for _ in range(2)]
                    R_ra = [Region("ra0"), Region("ra1")]
                    R_rb = [Region("rb0"), Region("rb1")]

                def skew(items):
                    ns = max(len(it) for it in items)
                    for step in range(len(items) + ns - 1):
                        for s_ in range(ns):
                            i_ = step - s_
                            if 0 <= i_ < len(items) and s_ < len(items[i_]):
                                items[i_][s_]()

                def normed_stages(idx, psi, wcol, hb, n, gain, out16, R_out, rope, post=None):
                    b = idx % 2
                    pst = 6

                    def s1():
                        proj_fm(psi, wv, wcol, hb, n, R_w)
                        act(sq16[b][:, 0:n], ps[psi][:, 0:n], AF.Square, [R_ps[psi]], [R_sq[b]])

                    def s2():
                        mm(ps[pst][:, 0:n], bones16, sq16[b][:, 0:n], True, True, [R_c, R_sq[b]], [R_ps[pst]])
                        act(sd32[b][:, 0:n], ps[pst][:, 0:n], AF.Sqrt, [R_ps[pst]], [R_sd[b]], bias=EPS)
                        recip(sd32[b][:, 0:n], sd32[b][:, 0:n], [R_sd[b]], [R_sd[b]])
                        if not rope:
                            stt("dve", out16, ps[psi][:, 0:n], gain, sd32[b][:, 0:n], ALU.mult, ALU.mult,
                                [R_ps[psi], R_small, R_sd[b]], [R_out])
                        else:
                            stt("dve", kn16[b][:, 0:n], ps[psi][:, 0:n], gain, sd32[b][:, 0:n], ALU.mult, ALU.mult,
                                [R_ps[psi], R_small, R_sd[b]], [R_kn[b]])

                    def s3():
                        mm(ps[pst][:, 0:n], pmat16, kn16[b][:, 0:n], True, True, [R_c, R_kn[b]], [R_ps[pst]])
                        tt("pool", ra32[b][:, 0:n], kn16[b][:, 0:n], rc32[:, 0:n], ALU.mult, [R_kn[b], R_rope], [R_ra[b]])
                        tt("dve", rb32[b][:, 0:n], ps[pst][:, 0:n], rs32[:, 0:n], ALU.mult, [R_ps[pst], R_rope], [R_rb[b]])
                        tt("pool", out16, ra32[b][:, 0:n], rb32[b][:, 0:n], ALU.add, [R_ra[b], R_rb[b]], [R_out])
                    st_ = [s1, s2, s3] if rope else [s1, s2]
                    if post is not None:
                        st_.append(post)
                    return st_

                load_hT(0, 0)
                all_items = []
                for bi in range(9):
                    t0, n = BLOCKS[bi]
                    hb = bi % 2
                    rope = is_wa and bi < 8

                    def ld(bi=bi, hb=hb, rope=rope, t0=t0, n=n):
                        if bi + 1 < 9:
                            load_hT(bi + 1, 1 - hb)
                        if rope:
                            dma("sp", rc32[:, 0:n], ropec_in[:, t0:t0 + n], [], [R_rope])
                            dma("sp", rs32[:, 0:n], ropes_in[:, t0:t0 + n], [], [R_rope])
                    all_items.append([ld])
                    for c in range(nkc):
                        all_items.append(normed_stages(c, c, ck + c * 128, hb, n, gk, KT[:, c, t0:t0 + n], R_KT, rope))
                    for t_ in range(n // 128):
                        def v1(t_=t_, hb=hb):
                            pi = 4 + t_ % 2
                            proj_tm(pi, wv, cv_, nkv * 64, hb, t_, R_w)

                        def v2(t_=t_, t0=t0):
                            pi = 4 + t_ % 2
                            gt = t0 // 128 + t_
                            src = ps[pi][:, 0:nkv * 64].rearrange("p (h d) -> p h d", h=nkv)
                            cp("act", VV[:, gt, :, 0:64], src, [R_ps[pi]], [R_VV])
                        all_items.append([v1, v2])
                    for _ in range(7):
                        if prep_thunks:
                            all_items.append([prep_thunks.pop(0)])
                while prep_thunks:
                    all_items.append([prep_thunks.pop(0)])
                skew(all_items)

                QT = W16.take(2048).rearrange("p (c n) -> p c n", c=4)
                R_QT = Region("QT")
                QTz = W16.take(4096).rearrange("p (c e n) -> p c e n", c=4, e=2)
                R_QTz = [Region("QTz%d" % c) for c in range(4)]
                R_QTc = [Region("QTc%d" % c) for c in range(4)]
                ogT = W16.take(2048).rearrange("p (c n) -> p c n", c=4)
                R_ogT = Region("ogT")
                og16 = W16.take(512)
                R_og16 = Region("og16")
                Pw = [W16.take(640) for _ in range(2)]
                R_Pw = [Region("Pw0"), Region("Pw1")]
                R_Pw5 = [Region("Pw5_0"), Region("Pw5_1")]
                PcE = [BIG.take(384) for _ in range(2)]
                R_Pc = [Region("Pc0"), Region("Pc1")]
                Ew = [BIG.take(512) for _ in range(2)]
                R_Ew = [Region("Ew0"), Region("Ew1")]
                og32 = W32.take(512)
                R_og32 = Region("og32")
                yb16 = WB.take(4096).rearrange("p (k n) -> p k n", k=8)
                R_yb = Region("yb")
                R_rec = Region("rec")
                cur_edge = [-1]
                load_hT(0, 0)
                for bi in range(nblk):
                    t0, n = BLOCKS[bi]
                    hb = bi % 2
                    if bi + 1 < nblk:
                        load_hT(bi + 1, 1 - hb)
                    rope = is_wa and bi < 8
                    if rope:
                        dma("sp", rc32[:, 0:n], ropec_in[:, t0:t0 + n], [], [R_rope])
                        dma("sp", rs32[:, 0:n], ropes_in[:, t0:t0 + n], [], [R_rope])
                    ntile = n // 128
                    items = []
                    for c in range(4):
                        def qz(c=c):
                            for e_ in range(2):
                                act(QTz[:, c, e_, 0:n], QT[:, c, 0:n], AF.Copy, [R_QTc[c], R_hm], [R_QTz[c]],
                                    scale=hmask[:, e_:e_ + 1])
                        items.append(normed_stages(c, c, cq + c * 128, hb, n, gq, QT[:, c, 0:n], R_QTc[c], rope, post=qz))
                    for t_ in range(ntile):
                        def g1(t_=t_):
                            proj_tm(4 + t_ % 2, wv, cg, 512, hb, t_, R_w)

                        def g2(t_=t_):
                            pi = 4 + t_ % 2
                            act(sgt[t_], ps[pi][:, 0:512], AF.Silu, [R_ps[pi]], [R_sgt[t_]])
                        items.append([g1, g2])
                    skew(items)

                    def scores(t_, h):
                        qt = t0 // 128 + t_
                        if bi == 8:
                            wch, tabv, R_tb = [], None, None
                        elif not is_wa:
                            ty, wch = na_chunks(qt)
                            if cur_edge[0] != ty:
                                dma("sp", tab16_f, etab_d[ty], [R_etab[ty]], [R_tab])
                                cur_edge[0] = ty
                            tabv, R_tb = tab16, R_tab
                        else:
                            wch = [c_ for c_ in (qt - 1, qt, qt + 1) if 0 <= c_ < 32]
                            moff = 128 if qt == 0 else 0
                            tabv, R_tb = None, R_c
                        cch = [32, 33]
                        nw = len(wch)
                        c2 = h // 2
                        pb = 64 * (h % 2)
                        kc = (h // 4) if is_wa else c2
                        hp = h % 2
                        pw_i, pc_i = (2, 4) if hp == 0 else (3, 0)
                        qv = QTz[:, c2, h % 2, t_ * 128:(t_ + 1) * 128]
                        for i_, kch in enumerate(cch):
                            mm(ps[pc_i][:, i_ * 128:(i_ + 1) * 128],
                               KT[:, kc, kch * 128:(kch + 1) * 128], qv, True, True,
                               [R_KT, R_QTz[c2]], [R_ps[pc_i]])
                        if nw == 5:
                            kch = wch[4]
                            mm(ps[pc_i][:, 256:384],
                               KT[:, kc, kch * 128:(kch + 1) * 128], qv, True, True,
                               [R_KT, R_QTz[c2]], [R_ps[pc_i]])
                        nce = 384 if nw == 5 else 256
                        act(PcE[hp][:, 0:nce], ps[pc_i][:, 0:nce], AF.Exp, [R_ps[pc_i]], [R_Pc[hp]])
                        for i_, kch in enumerate(wch[:4]):
                            mm(ps[pw_i][:, i_ * 128:(i_ + 1) * 128],
                               KT[:, kc, kch * 128:(kch + 1) * 128], qv, True, True,
                               [R_KT, R_QTz[c2]], [R_ps[pw_i]])
                        if nw:
                            n4 = min(nw, 4) * 128
                            if is_wa:
                                mk = wamask16[:, moff:moff + nw * 128]
                            else:
                                mk = tabv[:, h, 0:nw * 128]
                            if nw == 5:
                                tt("dve", Pw[hp][:, 512:640], PcE[hp][:, 256:384], mk[:, 512:640], ALU.mult,
                                   [R_Pc[hp], R_tb], [R_Pw5[hp]])
                            act(Ew[hp][:, 0:n4], ps[pw_i][:, 0:n4], AF.Exp, [R_ps[pw_i]], [R_Ew[hp]])
                            tt("dve", Pw[hp][:, 0:n4], Ew[hp][:, 0:n4], mk[:, 0:n4], ALU.mult,
                               [R_Ew[hp], R_tb], [R_Pw[hp]])
                        allch = [(PcE[hp][:, i_ * 128:(i_ + 1) * 128], kch, R_Pc[hp]) for i_, kch in enumerate(cch)]
                        if nw == 5:
                            allch.append((Pw[hp][:, 512:640], wch[4], R_Pw5[hp]))
                        allch += [(Pw[hp][:, i_ * 128:(i_ + 1) * 128], kch, R_Pw[hp]) for i_, kch in enumerate(wch[:4])]
                        return allch

                    def pv(t_, h, allch):
                        vh = (h // 4) if is_wa else h
                        opi = 5 if h < 4 else 1
                        ov = ps[opi][:, 0:260].rearrange("p (h d) -> p h d", h=4)[:, h % 4, :]
                        for i_, (pap, kch, rg) in enumerate(allch):
                            mm(ov, pap, VV[:, kch, vh, :], i_ == 0, i_ == len(allch) - 1, [rg, R_VV], [R_ps[opi]])

                    def epi_half(t_, g4):
                        opi = 5 if g4 == 0 else 1
                        o3 = ps[opi][:, 0:260].rearrange("p (h d) -> p h d", h=4)
                        rec = stat[:, 0:4]
                        if is_wa:
                            tt("dve", rec, o3[:, :, 64], esink[:, g4 * 4:(g4 + 1) * 4], ALU.add,
                               [R_ps[opi], R_small], [R_rec])
                            recip(rec, rec, [R_rec], [R_rec])
                        else:
                            recip(rec, o3[:, :, 64], [R_ps[opi]], [R_rec])
                        o32 = og32[:, 0:256].rearrange("p (h d) -> p h d", h=4)
                        tt("dve", o32, o3[:, :, 0:64], rec.unsqueeze(2).to_broadcast([128, 4, 64]), ALU.mult,
                           [R_ps[opi], R_rec], [R_og32])
                        tt("pool", og16[:, g4 * 256:(g4 + 1) * 256], og32[:, 0:256],
                           sgt[t_][:, g4 * 256:(g4 + 1) * 256], ALU.mult, [R_og32, R_sgt[t_]], [R_og16])

                    def transposes(t_):
                        pT4 = psT[:, 0:512].rearrange("p (c n) -> p c n", c=4)
                        for c in range(4):
                            tr(pT4[:, c, :], og16[:, c * 128:(c + 1) * 128], [R_og16], [R_psT])
                        cp("act", ogT[:, :, t_ * 128:(t_ + 1) * 128], pT4, [R_psT], [R_ogT])

                    units = [(t_, h) for t_ in range(ntile) for h in range(8)]
                    prev = None
                    pending_tr = []
                    for ui, (t_, h) in enumerate(units):
                        allch = scores(t_, h)
                        if prev is not None:
                            pt, ph, pch = prev
                            pv(pt, ph, pch)
                            if ph == 3:
                                epi_half(pt, 0)
                            if ph == 7:
                                epi_half(pt, 1)
                                pending_tr.append((ui + 2, pt))
                        while pending_tr and pending_tr[0][0] <= ui:
                            transposes(pending_tr.pop(0)[1])
                        prev = (t_, h, allch)
                    pt, ph, pch = prev
                    pv(pt, ph, pch)
                    epi_half(pt, 1)
                    pending_tr.append((0, pt))
                    while pending_tr:
                        transposes(pending_tr.pop(0)[1])
                    for i in range(8):
                        pi = 2 + i % 2
                        for c in range(4):
                            mm(ps[pi][:, 0:n], wpp[:, c, i * 128:(i + 1) * 128], ogT[:, c, 0:n], c == 0, c == 3,
                               [R_wp(i * 128), R_ogT], [R_ps[pi]])
                        cp("act" if i % 2 == 1 else "dve", yb16[:, i, 0:n], ps[pi][:, 0:n], [R_ps[pi]], [R_yb])
                    dma("sp", Y_d[1 + kind][:, :, t0:t0 + n], yb16[:, :, 0:n], [R_yb], [R_Y[1 + kind][bi]])

            for a in (BIG, WB, W32, W16):
                a.reset()
            S.barrier()
            gate_bc = W16.take(4096).bitcast(F32).rearrange("p (r n) -> p r n", r=2)
            dg32 = W32.take(128)
            R_dg = Region("dg")
            for r in range(2):
                for k in range(8):
                    ts("dve", dg32, ident32, modsb[:, 16 + k, r:r + 1], None, ALU.mult, None,
                       [R_mod2, R_c], [R_dg])
                    pi = 1 + (k // 4)
                    mm(ps[pi][:, (k % 4) * 128:(k % 4 + 1) * 128], ones32, dg32, True, True,
                       [R_c, R_dg], [R_ps[pi]])
                    if k % 4 == 3:
                        cp("act", gate_bc[:, r, (k // 4) * 512:(k // 4 + 1) * 512], ps[pi][:],
                           [R_ps[pi]], [R_gate])

            R_w = WReg()
            R_wo = WReg()
            wv = WB.take(8 * 3072).rearrange("p (k n) -> p k n", k=8)
            load_w(wv, win_in[l][:, 4864:7936], R_w, order=[0, 2, 4, 1, 3, 5], split_first=True)
            wo = BIG.take(8 * 1024).rearrange("p (k n) -> p k n", k=8)
            load_w(wo, wo_in[l], R_wo)
            Yb = [[BIG.take(4096).rearrange("p (k n) -> p k n", k=8) for _ in range(3)] for _ in range(2)]
            R_Yb = [[Region("Yb%d%d" % (i, j)) for j in range(3)] for i in range(2)]
            yT2 = [W16.take(4096).rearrange("p (k n) -> p k n", k=8) for _ in range(2)]
            R_yT2 = [Region("yT0"), Region("yT1")]
            sgm = [W32.take(512) for _ in range(2)]
            R_sgm = [Region("sgm0"), Region("sgm1")]
            acc = W32.take(512)
            R_acc = Region("acc")
            xt = [W32.take(1024) for _ in range(2)]
            R_xt = [Region("xt0"), Region("xt1")]
            xo = [W32.take(1024) for _ in range(2)]
            R_xo = [Region("xo0"), Region("xo1")]
            tmp2 = [W32.take(512) for _ in range(2)]
            R_tmp2 = [Region("tmp0"), Region("tmp1")]

            def ld_blk(bi):
                load_hT(bi, bi % 2)
                t0, n = BLOCKS[bi]
                for br in range(3):
                    dma("sp", Yb[bi % 2][br][:, :, 0:n], Y_d[br][:, :, t0:t0 + n], [R_Y[br][bi]], [R_Yb[bi % 2][br]])
            ld_blk(0)
            xcnt = 0
            for bi in range(nblk):
                t0, n = BLOCKS[bi]
                hb = bi % 2
                if bi + 1 < nblk:
                    ld_blk(bi + 1)
                r = 0 if bi < 8 else 1
                yT, R_yT = yT2[bi % 2], R_yT2[bi % 2]
                for i in range(8):
                    for br in range(3):
                        pi = (i * 3 + br) % 2
                        proj_fm(pi, wv, br * 1024 + i * 128, hb, n, R_w)
                        act(sgm[pi][:, 0:n], ps[pi][:, 0:n], AF.Sigmoid, [R_ps[pi]], [R_sgm[pi]])
                        if br == 0:
                            tt("dve", acc[:, 0:n], sgm[pi][:, 0:n], Yb[hb][br][:, i, 0:n], ALU.mult,
                               [R_sgm[pi], R_Yb[hb][br]], [R_acc])
                        else:
                            tt("pool", sgm[pi][:, 0:n], sgm[pi][:, 0:n], Yb[hb][br][:, i, 0:n], ALU.mult,
                               [R_sgm[pi], R_Yb[hb][br]], [R_sgm[pi]])
                            if br == 1:
                                tt("dve", acc[:, 0:n], acc[:, 0:n], sgm[pi][:, 0:n], ALU.add,
                                   [R_acc, R_sgm[pi]], [R_acc])
                            else:
                                tt("dve", yT[:, i, 0:n], acc[:, 0:n], sgm[pi][:, 0:n], ALU.add,
                                   [R_acc, R_sgm[pi]], [R_yT])
                for t_ in range(n // 128):
                    gt = t0 // 128 + t_
                    xb = xcnt % 2
                    xcnt += 1
                    src = xin[gt * 128:(gt + 1) * 128, :] if gt < 32 else cin[(gt - 32) * 128:(gt - 31) * 128, :]
                    dst = xout[gt * 128:(gt + 1) * 128, :] if gt < 32 else ctx1_d[(gt - 32) * 128:(gt - 31) * 128, :]
                    dma("sp", xt[xb], src, [R_x1[gt]] if l > 0 else [], [R_xt[xb]])
                    for hf in range(2):
                        pi = 2 + hf + 2 * (xb % 2)
                        tmp, R_tmp = tmp2[hf], R_tmp2[hf]
                        for i in range(8):
                            mm(ps[pi][:, :], yT[:, i, t_ * 128:(t_ + 1) * 128], wo[:, i, hf * 512:(hf + 1) * 512],
                               i == 0, i == 7, [R_yT, R_wo(hf * 512)], [R_ps[pi]])
                        tt("dve", tmp, ps[pi][:, :], gate_bc[:, r, hf * 512:(hf + 1) * 512], ALU.mult,
                           [R_ps[pi], R_gate], [R_tmp])
                        tt("pool", xo[xb][:, hf * 512:(hf + 1) * 512], tmp, xt[xb][:, hf * 512:(hf + 1) * 512],
                           ALU.add, [R_tmp, R_xt[xb]], [R_xo[xb]])
                    if last:
                        dma("sp", dst, xo[xb], [R_xo[xb]], [], is_output=True)
                    else:
                        dma("sp", dst, xo[xb], [R_xo[xb]], [R_x1[gt]])

        S.finish()
        S.emit_all()
    return nc


_CACHE = {}


def _fm(v, nchunk):
    sh = v.shape[:-1]
    return np.ascontiguousarray(np.swapaxes(v.reshape(*sh, nchunk, 128), -1, -2))


def prep_inputs(inp):
    global _NA_IDX
    if _NA_IDX is None:
        _NA_IDX = _na_index()
    if "cst" not in _CACHE:
        _CACHE["cst"] = _consts()
    cst, rc, rs = _CACHE["cst"]
    f = lambda a: np.ascontiguousarray(np.asarray(a, dtype=np.float32))
    x, c, ctx, c_ctx = f(inp["x"]), f(inp["c"]), f(inp["ctx"]), f(inp["c_ctx"])
    common = {
        "norm_gT": _fm(f(inp["norm_g"]), 8),
        "b_adaT": _fm(f(inp["b_ada"]), 24),
        "w_ada": f(inp["w_ada"]), "w_in": f(inp["w_in"]),
        "w_proj_a": f(inp["w_proj_a"]), "w_proj_b": f(inp["w_proj_b"]), "w_proj_c": f(inp["w_proj_c"]),
        "w_o": f(inp["w_o"]),
        "cst": cst, "rope_c": rc, "rope_s": rs,
    }
    cw = f(inp["conv_w"])
    cwT = np.transpose(cw.reshape(L, 31, 4, 128), (0, 3, 2, 1))
    common["conv_wT"] = np.ascontiguousarray(cwT.reshape(L, 128, 124))
    common["cvecT"] = np.ascontiguousarray(np.concatenate(
        [_fm(f(inp["conv_b"]), 4), _fm(f(inp["cln_g"]), 4), _fm(f(inp["cln_b"]), 4)], axis=-1))
    qk = np.stack([f(inp["na_q_norm"]), f(inp["na_k_norm"]), f(inp["wa_q_norm"]), f(inp["wa_k_norm"])], -1)
    common["qk_norm"] = np.ascontiguousarray(np.concatenate([qk, qk], axis=1))
    common["sink_bc"] = np.ascontiguousarray(np.broadcast_to(f(inp["wa_sink"])[:, None, :], (L, 128, 8)))
    rpb = f(inp["na_rpb"]).reshape(L, 8, 15 * 31)
    rpb_pad = np.concatenate([rpb, np.full((L, 8, 1), NEG, np.float32)], axis=-1)
    tab = rpb_pad[:, :, _NA_IDX]
    tab = np.transpose(tab, (0, 2, 3, 1, 4, 5))
    common["na_tab"] = np.ascontiguousarray(tab.reshape(L, 5, 128, 8 * 640))
    maps = []
    for b in range(8):
        m = dict(common)
        m["x"] = x[b]
        m["ctx"] = ctx[b]
        cc = np.stack([c[b], c_ctx], axis=-1)
        m["ccT"] = np.ascontiguousarray(np.transpose(cc.reshape(8, 128, 2), (1, 0, 2)))
        maps.append(m)
    return maps


def kernel(**inputs):
    if "nc" not in _CACHE:
        _CACHE["nc"] = build()
    maps = prep_inputs(inputs)
    res = run_bass_kernel_spmd(_CACHE["nc"], maps, core_ids=list(range(8)))
    return np.stack([np.asarray(r["out"], dtype=np.float32) for r in res.results], axis=0)
```

```python
import numpy as np
from contextlib import ExitStack
import concourse.bass as bass
import concourse.mybir as mybir
from concourse.bass_utils import run_bass_kernel_spmd

F32 = mybir.dt.float32
BF16 = mybir.dt.bfloat16
AF = mybir.ActivationFunctionType
ALU = mybir.AluOpType

L = 2
S_LAT = 4096
N_CTX = 256
S_ALL = S_LAT + N_CTX
D = 1024
D_IN = 7936
EPS = 1e-6
NEG = -30000.0
ENGS = ("pe", "act", "dve", "pool", "sp")
BLOCKS = [(b * 512, 512) for b in range(8)] + [(S_LAT, N_CTX)]


class Region:
    __slots__ = ("name", "lw", "reads")

    def __init__(self, name=""):
        self.name = name
        self.lw = None
        self.reads = {}


class Sched:
    NDS = 12

    def __init__(self, nc, stack):
        self.nc = nc
        self.ops = {e: [] for e in ENGS}
        self.cnt = {e: 0 for e in ENGS}
        self.semobj = {}
        for e in ENGS:
            self.semobj[("e", e)] = stack.enter_context(nc.semaphore("s_" + e))
        self.dq = {}
        for q in ("sp", "act", "pool"):
            lst = []
            for i in range(self.NDS):
                key = ("d", q, i)
                self.semobj[key] = stack.enter_context(nc.semaphore("d_%s%d" % (q, i)))
                lst.append([key, 0])
            self.dq[q] = [lst, 0]
        self.known = {e: {} for e in ENGS}
        self.out_tokens = []

    def _need(self, eng, tok, waits):
        if tok is None:
            return
        key, val = tok
        if eng == "pe" and key == ("e", "pe"):
            return
        if self.known[eng].get(key, 0) >= val:
            return
        if waits.get(key, 0) < val:
            waits[key] = val

    def _collect(self, eng, reads, writes, waits):
        for r in reads:
            self._need(eng, r.lw, waits)
        for w in writes:
            self._need(eng, w.lw, waits)
            for k, v in w.reads.items():
                self._need(eng, (k, v), waits)
        for k, v in waits.items():
            self.known[eng][k] = v
        return [(self.semobj[k], v) for k, v in waits.items()]

    def op(self, eng, fn, reads=(), writes=()):
        wl = self._collect(eng, reads, writes, {})
        self.cnt[eng] += 1
        seq = self.cnt[eng]
        key = ("e", eng)
        sem = self.semobj[key]
        for r in reads:
            r.reads[key] = seq
        for w in writes:
            w.lw = (key, seq)
            w.reads = {}

        def emit(e):
            for s, v in wl:
                e.wait_ge(s, v)
            fn(e).then_inc(sem, 1)
        self.ops[eng].append(emit)

    def dma(self, q, fn, reads=(), writes=(), is_output=False):
        lst, idx = self.dq[q]
        ent = lst[idx % self.NDS]
        self.dq[q][1] = idx + 1
        key = ent[0]
        waits = {}
        if ent[1] > 0:
            self._need(q, (key, ent[1]), waits)
        wl = self._collect(q, reads, writes, waits)
        ent[1] += 16
        val = ent[1]
        sem = self.semobj[key]
        for r in reads:
            if r.reads.get(key, 0) < val:
                r.reads[key] = val
        for w in writes:
            w.lw = (key, val)
            w.reads = {}
        if is_output:
            self.out_tokens.append((key, val))

        def emit(e):
            for s, v in wl:
                e.wait_ge(s, v)
            fn(e).then_inc(sem, 16)
        self.ops[q].append(emit)

    def barrier(self):
        toks = [(("e", x), self.cnt[x]) for x in ENGS if self.cnt[x] > 0]
        for q in self.dq:
            for key, val in self.dq[q][0]:
                if val > 0:
                    toks.append((key, val))
        for e in ENGS:
            waits = {}
            for key, val in toks:
                if key == ("e", e):
                    continue
                if self.known[e].get(key, 0) < val:
                    waits[key] = val
                    self.known[e][key] = val
            wl = [(self.semobj[k], v) for k, v in waits.items()]
            if wl:
                def emit(eo, wl=wl):
                    for s, v in wl:
                        eo.wait_ge(s, v)
                self.ops[e].append(emit)

    def finish(self):
        final = {}
        for k, v in self.out_tokens:
            final[k] = max(final.get(k, 0), v)
        wl = [(self.semobj[k], v) for k, v in final.items()]

        def emit(e):
            for s, v in wl:
                e.wait_ge(s, v)
        self.ops["sp"].append(emit)

    def emit_all(self):
        with self.nc.Block() as block:
            @block.tensor
            def _(e):
                for f in self.ops["pe"]:
                    f(e)

            @block.scalar
            def _(e):
                for f in self.ops["act"]:
                    f(e)

            @block.vector
            def _(e):
                for f in self.ops["dve"]:
                    f(e)

            @block.gpsimd
            def _(e):
                for f in self.ops["pool"]:
                    f(e)

            @block.sync
            def _(e):
                for f in self.ops["sp"]:
                    f(e)


class Arena:
    def __init__(self, t, size, name):
        self.t, self.size, self.name, self.off = t, size, name, 0

    def reset(self):
        self.off = 0

    def take(self, n, name=""):
        n16 = (n + 15) // 16 * 16
        assert self.off + n16 <= self.size, (self.name, name, self.off, n, self.size)
        ap = self.t[:, self.off:self.off + n]
        self.off += n16
        return ap


NA_TYPES = [None, 0, 2, 60, 62]


def na_chunks(qt):
    i0 = 2 * qt
    if i0 <= 2:
        return 1 + i0 // 2, [0, 1, 2, 3]
    if i0 >= 60:
        return 3 + (i0 - 60) // 2, [28, 29, 30, 31]
    return 0, [qt - 2, qt - 1, qt, qt + 1, qt + 2]


def _na_index():
    idx = np.full((5, 128, 5, 128), 15 * 31, dtype=np.int64)
    for ty in range(5):
        qt = 8 if ty == 0 else NA_TYPES[ty] // 2
        _, chunks = na_chunks(qt)
        for ci, kc in enumerate(chunks):
            for p in range(128):
                r = 2 * kc + p // 64
                c = p % 64
                for n in range(128):
                    i = 2 * qt + n // 64
                    j = n % 64
                    rs = min(max(i - 4, 0), 56)
                    cs = min(max(j - 8, 0), 48)
                    if rs <= r < rs + 8 and cs <= c < cs + 16:
                        idx[ty, p, ci, n] = (r - i + 7) * 31 + (c - j + 15)
    return idx


_NA_IDX = None


def _consts():
    cst = np.zeros((128, 5 * 128 + 384), np.float32)
    cst[:, 0:128] = np.eye(128)
    pm = np.zeros((128, 128), np.float32)
    for m in range(128):
        sub = m % 32
        if sub < 16:
            pm[m + 16, m] = 1.0
        else:
            pm[m - 16, m] = 1.0
    cst[:, 128:256] = pm
    bo = np.zeros((128, 128), np.float32)
    bo[0:64, 0:64] = 1.0 / 64
    bo[64:128, 64:128] = 1.0 / 64
    cst[:, 256:384] = bo
    cst[:, 384:512] = 1.0 / 512
    cst[:, 512:640] = 1.0
    p = np.arange(128)[:, None]
    n = np.arange(128)[None, :]
    cst[:, 640:768] = (n <= p)
    cst[:, 768:896] = 1.0
    cst[:, 896:1024] = (p <= n)
    t = np.arange(S_LAT)
    rowp = (t // 64).astype(np.float32)
    colp = (t % 64).astype(np.float32)
    inv = (10000.0 ** (-np.arange(16, dtype=np.float32) / 16)).astype(np.float32)
    rc = np.zeros((128, S_LAT), np.float32)
    rs = np.zeros((128, S_LAT), np.float32)
    for pp in range(128):
        d = pp % 64
        sub = d % 32
        pos = rowp if d < 32 else colp
        ang = (pos * inv[sub % 16]).astype(np.float32)
        rc[pp] = np.cos(ang)
        rs[pp] = -np.sin(ang) if sub < 16 else np.sin(ang)
    return cst, rc, rs


def build(debug=False):
    nc = bass.Bass("TRN2", target_bir_lowering=False)
    EI = "ExternalInput"
    dbgk = "ExternalOutput" if debug else "Internal"

    def din(name, shape):
        return nc.dram_tensor(name, list(shape), F32, kind=EI).ap()

    x_in = din("x", [S_LAT, D])
    ctx_in = din("ctx", [N_CTX, D])
    ccT_in = din("ccT", [128, 8, 2])
    normg_in = din("norm_gT", [L, 128, 8])
    bada_in = din("b_adaT", [L, 128, 24])
    wada_in = din("w_ada", [L, D, 3 * D])
    win_in = din("w_in", [L, D, D_IN])
    wpa_in = din("w_proj_a", [L, 512, D])
    wpb_in = din("w_proj_b", [L, 512, D])
    wpc_in = din("w_proj_c", [L, 512, D])
    wo_in = din("w_o", [L, D, D])
    convw_in = din("conv_wT", [L, 128, 4 * 31])
    cvec_in = din("cvecT", [L, 128, 12])
    qkn_in = din("qk_norm", [L, 128, 4])
    sink_in = din("sink_bc", [L, 128, 8])
    natab_in = din("na_tab", [L, 5, 128, 8 * 640])
    cst_in = din("cst", [128, 1024])
    ropec_in = din("rope_c", [128, S_LAT])
    ropes_in = din("rope_s", [128, S_LAT])
    out_x = nc.dram_tensor("out", [S_LAT, D], F32, kind="ExternalOutput").ap()

    hT_d = nc.dram_tensor("hT_d", [128, 8, S_ALL], BF16, kind=dbgk).ap()
    Y_d = [nc.dram_tensor("Y%d_d" % i, [128, 8, S_ALL], BF16, kind=dbgk).ap() for i in range(3)]
    x1_d = nc.dram_tensor("x1_d", [S_LAT, D], F32, kind=dbgk).ap()
    ctx1_d = nc.dram_tensor("ctx1_d", [N_CTX, D], F32, kind=dbgk).ap()
    etab_d = nc.dram_tensor("etab_d", [5, 128, 5120], BF16, kind="Internal").ap()

    R_hT = [Region("hT%d" % i) for i in range(34)]
    R_Y = [[Region("Y%d_%d" % (br, b)) for b in range(9)] for br in range(3)]
    R_x1 = [Region("x1_%d" % i) for i in range(34)]
    R_etab = [Region("etab%d" % i) for i in range(5)]

    with ExitStack() as st:
        S = Sched(nc, st)

        def sb(name, shape, dt):
            return st.enter_context(nc.sbuf_tensor(name, list(shape), dt))

        BIG = Arena(sb("BIG", [128, 37120], BF16), 37120, "BIG")
        WB = Arena(sb("WB", [128, 24576], BF16), 24576, "WB")
        W32 = Arena(sb("W32", [128, 7168], F32), 7168, "W32")
        W16 = Arena(sb("W16", [128, 16384], BF16), 16384, "W16")
        HTB = sb("HTB", [128, 2, 8, 512], BF16)
        cst16 = sb("cst16", [128, 1024], BF16)
        cst32 = sb("cst32", [128, 256], F32)
        silc = sb("silc", [128, 8, 2], BF16)
        cc32 = sb("cc32", [128, 8, 2], F32)
        modsb = sb("modsb", [128, 24, 2], F32)
        gsb = sb("gsb", [128, 8, 2], F32)
        normg = sb("normg", [128, 8], F32)
        bada = sb("bada", [128, 24], F32)
        convw = sb("convw", [128, 124], F32)
        cvec = sb("cvec", [128, 12], F32)
        qkn = sb("qkn", [128, 4], F32)
        esink = sb("esink", [128, 8], F32)
        stat = sb("stat", [128, 8], F32)
        hmask = sb("hmask", [128, 2], F32)
        ps = [st.enter_context(nc.psum_tensor("ps%d" % i, [128, 512], F32)) for i in range(7)]
        psT = st.enter_context(nc.psum_tensor("psT", [128, 1024], BF16))
        R_ps = [Region("ps%d" % i) for i in range(7)]
        R_psT = Region("psT")
        R_HTB = [Region("HTB0"), Region("HTB1")]
        R_c = Region("consts")
        R_small = Region("small")
        R_gate = Region("gate_bc")

        ident16 = cst16[:, 0:128]
        pmat16 = cst16[:, 128:256]
        bones16 = cst16[:, 256:384]
        o512_16 = cst16[:, 384:512]
        wamask16 = cst16[:, 640:1024]
        ident32 = cst32[:, 0:128]
        ones32 = cst32[:, 128:256]

        def mm(out, lhsT, rhs, start, stop, rd, wr):
            S.op("pe", lambda e: e.matmul(out, lhsT=lhsT, rhs=rhs, start=start, stop=stop), rd, wr)

        def tr(out, in_, rd, wr):
            S.op("pe", lambda e: e.transpose(out=out, in_=in_, identity=ident16), rd + [R_c], wr)

        def act(out, in_, func, rd, wr, **kw):
            S.op("act", lambda e: e.activation(out=out, in_=in_, func=func, **kw), rd, wr)

        def tt(eng, out, in0, in1, op, rd, wr):
            S.op(eng, lambda e: e.tensor_tensor(out=out, in0=in0, in1=in1, op=op), rd, wr)

        def ts(eng, out, in0, s1, s2, op0, op1, rd, wr):
            if op1 is None:
                S.op(eng, lambda e: e.tensor_scalar(out=out, in0=in0, scalar1=s1, scalar2=None, op0=op0), rd, wr)
            else:
                S.op(eng, lambda e: e.tensor_scalar(out=out, in0=in0, scalar1=s1, scalar2=s2, op0=op0, op1=op1), rd, wr)

        def stt(eng, out, in0, scalar, in1, op0, op1, rd, wr):
            S.op(eng, lambda e: e.scalar_tensor_tensor(out=out, in0=in0, scalar=scalar, in1=in1, op0=op0, op1=op1), rd, wr)

        def cp(eng, out, in_, rd, wr):
            if eng == "act":
                S.op(eng, lambda e: e.activation(out=out, in_=in_, func=AF.Copy), rd, wr)
            else:
                S.op(eng, lambda e: e.tensor_copy(out=out, in_=in_), rd, wr)

        def recip(out, in_, rd, wr):
            S.op("dve", lambda e: e.reciprocal(out=out, in_=in_), rd, wr)

        def mset(eng, ap, val, wr):
            S.op(eng, lambda e: e.memset(ap, val), [], wr)

        def dma(q, out, in_, rd, wr, is_output=False):
            S.dma(q, lambda e: e.dma_start(out=out, in_=in_), rd, wr, is_output=is_output)

        class WReg:
            def __init__(self):
                self.rs = []

            def add(self, a, b):
                r = Region("w%d" % a)
                self.rs.append((a, b, r))
                return r

            def __call__(self, c):
                for a, b, r in self.rs:
                    if a <= c < b:
                        return r
                raise KeyError(c)

        def load_w(dst3, src2, wreg, base=0, order=None, region=None, split_first=False):
            n = src2.shape[1]
            starts = list(range(0, n, 512))
            if order is not None:
                starts = [starts[i] for i in order]
            pieces = []
            for i_, c0 in enumerate(starts):
                cw = min(512, n - c0)
                if split_first and i_ == 0 and cw == 512:
                    pieces += [(c0 + q_ * 128, 128) for q_ in range(4)]
                else:
                    pieces.append((c0, cw))
            for c0, cw in pieces:
                dma("pool", dst3[:, :, c0:c0 + cw],
                    src2[:, c0:c0 + cw].rearrange("(k p) n -> p k n", p=128), [],
                    [region if region is not None else wreg.add(base + c0, base + c0 + cw)])

        def load_hT(bi, buf):
            t0, n = BLOCKS[bi]
            rr = R_hT[t0 // 128:(t0 + n) // 128]
            dma("sp", HTB[:, buf, :, 0:n], hT_d[:, :, t0:t0 + n], rr, [R_HTB[buf]])

        def proj_fm(psi, wview, c0, hbuf, n, wreg):
            for k in range(8):
                mm(ps[psi][:, 0:n], wview[:, k, c0:c0 + 128], HTB[:, hbuf, k, 0:n],
                   k == 0, k == 7, [wreg(c0), R_HTB[hbuf]], [R_ps[psi]])

        def proj_tm(psi, wview, c0, ncols, hbuf, tt_, wreg):
            for k in range(8):
                mm(ps[psi][:, 0:ncols], HTB[:, hbuf, k, tt_ * 128:(tt_ + 1) * 128],
                   wview[:, k, c0:c0 + ncols], k == 0, k == 7, [wreg(c0), R_HTB[hbuf]], [R_ps[psi]])

        R_hm = Region("hmask")
        mset("pool", hmask[:, :], 0.0, [R_hm])
        mset("pool", hmask[0:64, 0:1], 1.0, [R_hm])
        mset("pool", hmask[64:128, 1:2], 1.0, [R_hm])
        dma("pool", cst16[:], cst_in, [], [R_c])
        dma("sp", cst32[:, 0:128], cst_in[:, 0:128], [], [R_c])
        dma("sp", cst32[:, 128:256], cst_in[:, 512:640], [], [R_c])
        dma("sp", cc32[:], ccT_in, [], [R_small])
        act(silc[:], cc32[:], AF.Silu, [R_small], [R_small])

        for l in range(L):
            last = (l == L - 1)
            nblk = 8 if last else 9
            xin = x_in if l == 0 else x1_d
            cin = ctx_in if l == 0 else ctx1_d
            xout = out_x if last else x1_d
            for a in (BIG, WB, W32, W16):
                a.reset()
            S.barrier()

            R_w = WReg()
            wad = WB.take(8 * 3072).rearrange("p (k n) -> p k n", k=8)
            load_w(wad, wada_in[l], R_w, split_first=True)
            dma("sp", normg[:], normg_in[l], [], [R_small])
            dma("sp", bada[:], bada_in[l], [], [R_small])
            dma("sp", convw[:], convw_in[l], [], [R_small])
            dma("sp", cvec[:], cvec_in[l], [], [R_small])
            dma("sp", qkn[:], qkn_in[l], [], [R_small])
            dma("sp", esink[:], sink_in[l], [], [R_small])
            modps = ps[0][:, 0:48].rearrange("p (f r) -> p f r", f=24)
            for f in range(16):
                for k in range(8):
                    mm(modps[:, f, :], wad[:, k, f * 128:(f + 1) * 128], silc[:, k, :],
                       k == 0, k == 7, [R_w(f * 128), R_small], [R_ps[0]])
            tt("dve", modsb[:, 0:16, :], modps[:, 0:16, :], bada[:, 0:16].unsqueeze(2).to_broadcast([128, 16, 2]), ALU.add,
               [R_ps[0], R_small], [R_small])
            ts("dve", gsb[:], modsb[:, 8:16, :], 1.0, None, ALU.add, None, [R_small], [R_small])
            tt("dve", gsb[:], gsb[:], normg[:].unsqueeze(2).to_broadcast([128, 8, 2]), ALU.mult,
               [R_small], [R_small])
            ts("dve", qkn[:, 0:1], qkn[:, 0:1], 0.125, None, ALU.mult, None, [R_small], [R_small])
            ts("dve", qkn[:, 2:3], qkn[:, 2:3], 0.125, None, ALU.mult, None, [R_small], [R_small])
            act(esink[:], esink[:], AF.Exp, [R_small], [R_small])

            W16.reset()
            dgs = BIG.take(124 * 128).rearrange("p (j k n) -> p j k n", j=4, k=31)
            R_dgs = Region("dgs")
            dg_todo = [(j, k) for j in range(4) for k in range(31)]

            def build_dg(cnt):
                for i_ in range(cnt):
                    if dg_todo:
                        j, k = dg_todo.pop(0)
                        if False:
                            ts("dve", dgs[:, j, k, :], ident32, convw[:, j * 31 + k:j * 31 + k + 1], None,
                               ALU.mult, None, [R_c, R_small], [R_dgs])
                        else:
                            act(dgs[:, j, k, :], ident32, AF.Copy, [R_c, R_small], [R_dgs],
                                scale=convw[:, j * 31 + k:j * 31 + k + 1])
            xt = [W32.take(1024) for _ in range(3)]
            t32 = [W32.take(1024) for _ in range(2)]
            junk16 = W16.take(1024)
            xn16 = [W16.take(1024) for _ in range(2)]
            hts16 = [W16.take(1024) for _ in range(2)]
            R_xt = [Region("xt%d" % i) for i in range(3)]
            R_t32, R_junk = [Region("t32a"), Region("t32b")], Region("junk")
            R_xn = [Region("xn0"), Region("xn1")]
            R_hts = [Region("hts0"), Region("hts1")]
            R_st = [Region("st0"), Region("st1")]
            psT3 = psT[:, :].rearrange("p (k n) -> p k n", k=8)

            def a_stages(ti):
                b = ti % 2
                b3 = ti % 3
                r = 0 if ti < 32 else 1
                t32v = t32[b].rearrange("p (k n) -> p k n", k=8)
                htv = hts16[b].rearrange("p (k n) -> p k n", k=8)

                def s0():
                    src = xin[ti * 128:(ti + 1) * 128, :] if ti < 32 else cin[(ti - 32) * 128:(ti - 31) * 128, :]
                    rd = [R_x1[ti]] if l > 0 else []
                    dma("sp", xt[b3], src, rd, [R_xt[b3]])

                def s1():
                    mset("dve", stat[:, b:b + 1], 0.0, [R_st[b]])
                    act(junk16, xt[b3], AF.Square, [R_xt[b3]], [R_junk, R_st[b]], accum_out=stat[:, b:b + 1])
                    act(stat[:, 2 + b:3 + b], stat[:, b:b + 1], AF.Sqrt, [R_st[b]], [R_st[b]],
                        scale=1.0 / D, bias=EPS)
                    recip(stat[:, 4 + b:5 + b], stat[:, 2 + b:3 + b], [R_st[b]], [R_st[b]])
                    ts("dve", xn16[b], xt[b3], stat[:, 4 + b:5 + b], None, ALU.mult, None,
                       [R_xt[b3], R_st[b]], [R_xn[b]])

                def s2():
                    for k in range(8):
                        tr(psT3[:, k, :], xn16[b][:, k * 128:(k + 1) * 128], [R_xn[b]], [R_psT])
                    tt("dve", t32v, psT3, gsb[:, :, r:r + 1].to_broadcast([128, 8, 128]), ALU.mult,
                       [R_psT, R_small], [R_t32[b]])

                def s3():
                    tt("pool", htv, t32v, modsb[:, 0:8, r:r + 1].to_broadcast([128, 8, 128]), ALU.add,
                       [R_t32[b], R_small], [R_hts[b]])
                    dma("sp", hT_d[:, :, ti * 128:(ti + 1) * 128], htv, [R_hts[b]], [R_hT[ti]])
                return [s0, s1, s2, s3]

            a_items = [a_stages(ti) for ti in range(34)]
            for step in range(34 + 3):
                for s_ in range(4):
                    i_ = step - s_
                    if 0 <= i_ < 34:
                        a_items[i_][s_]()
                build_dg(4)
            build_dg(124)

            R_mod2 = Region("mod2")
            modps2 = ps[1][:, 0:16].rearrange("p (f r) -> p f r", f=8)
            for f in range(16, 24):
                for k in range(8):
                    mm(modps2[:, f - 16, :], wad[:, k, f * 128:(f + 1) * 128], silc[:, k, :],
                       k == 0, k == 7, [R_w(f * 128), R_small], [R_ps[1]])
            tt("dve", modsb[:, 16:24, :], modps2, bada[:, 16:24].unsqueeze(2).to_broadcast([128, 8, 2]), ALU.add,
               [R_ps[1], R_small], [R_mod2])
            for a in (WB, W32, W16):
                a.reset()
            S.barrier()
            R_w = WReg()
            R_wp = WReg()
            wv = WB.take(8 * 1536).rearrange("p (k n) -> p k n", k=8)
            wpa = WB.take(4 * 1024).rearrange("p (k n) -> p k n", k=4)
            load_w(wv, win_in[l][:, 0:1536], R_w, split_first=True)
            load_w(wpa, wpa_in[l], R_wp)
            A_lat_f = BIG.take(4 * (S_LAT + 32))
            A_ctx_f = BIG.take(4 * (N_CTX + 32))
            A_lat = A_lat_f.rearrange("p (j n) -> p j n", j=4)
            A_ctx = A_ctx_f.rearrange("p (j n) -> p j n", j=4)
            R_A = Region("A")
            mset("pool", A_lat_f, 0.0, [R_A])
            mset("pool", A_ctx_f, 0.0, [R_A])
            sg32 = [W32.take(512) for _ in range(2)]
            R_sg = [Region("sg0"), Region("sg1")]
            load_hT(0, 0)
            for bi in range(nblk):
                t0, n = BLOCKS[bi]
                hb = bi % 2
                if bi + 1 < nblk:
                    load_hT(bi + 1, 1 - hb)
                Ab, a0 = (A_lat, t0) if bi < 8 else (A_ctx, 0)
                for j in range(4):
                    pa, pb_ = (0, 1) if j % 2 == 0 else (2, 3)
                    proj_fm(pb_, wv, 512 + j * 128, hb, n, R_w)
                    proj_fm(pa, wv, j * 128, hb, n, R_w)
                    sb_ = j % 2
                    act(sg32[sb_][:, 0:n], ps[pb_][:, 0:n], AF.Sigmoid, [R_ps[pb_]], [R_sg[sb_]])
                    tt("dve", Ab[:, j, 15 + a0:15 + a0 + n], ps[pa][:, 0:n], sg32[sb_][:, 0:n], ALU.mult,
                       [R_ps[pa], R_sg[sb_]], [R_A])
            v32 = W32.take(2048).rearrange("p (j n) -> p j n", j=4)
            mean32 = W32.take(512)
            var32 = W32.take(512)
            rstd32 = W32.take(512)
            t1 = [W32.take(512) for _ in range(2)]
            t2 = [W32.take(512) for _ in range(2)]
            sga = sg32
            v16 = W16.take(2048).rearrange("p (j n) -> p j n", j=4)
            sq16 = W16.take(2048).rearrange("p (j n) -> p j n", j=4)
            z16 = W16.take(2048).rearrange("p (j n) -> p j n", j=4)
            ya16 = WB.take(4096).rearrange("p (k n) -> p k n", k=8)
            R_v32, R_v16, R_sq16, R_z16, R_ya = (Region("v32"), Region("v16"), Region("sq16"),
                                                 Region("z16"), Region("ya"))
            R_mean, R_var, R_rstd = Region("mean"), Region("var"), Region("rstd")
            R_v32j = [Region("v32_%d" % j) for j in range(4)]
            NPE = 31
            R_t1 = [Region("t1a"), Region("t1b")]
            R_t2 = [Region("t2a"), Region("t2b")]
            load_hT(0, 0)
            for bi in range(nblk):
                t0, n = BLOCKS[bi]
                hb = bi % 2
                if bi + 1 < nblk:
                    load_hT(bi + 1, 1 - hb)
                Ab, a0 = (A_lat, t0) if bi < 8 else (A_ctx, 0)
                for j in range(4):
                    pi = j % 2
                    for k in range(NPE):
                        mm(ps[pi][:, 0:n], dgs[:, j, k, :], Ab[:, j, a0 + k:a0 + k + n], k == 0, k == NPE - 1,
                           [R_dgs, R_A], [R_ps[pi]])
                    act(v32[:, j, 0:n], ps[pi][:, 0:n], AF.Identity, [R_ps[pi], R_small], [R_v32j[j]],
                        bias=cvec[:, j:j + 1])
                    for k in range(NPE, 31):
                        stt("dve", v32[:, j, 0:n], Ab[:, j, a0 + k:a0 + k + n], convw[:, j * 31 + k:j * 31 + k + 1],
                            v32[:, j, 0:n], ALU.mult, ALU.add, [R_A, R_small, R_v32j[j]], [R_v32j[j]])
                    act(v16[:, j, 0:n], ps[pi][:, 0:n], AF.Identity, [R_ps[pi], R_small], [R_v16],
                        bias=cvec[:, j:j + 1])
                    act(sq16[:, j, 0:n], v32[:, j, 0:n], AF.Square, [R_v32j[j]], [R_sq16])
                for j in range(4):
                    mm(ps[2][:, 0:n], o512_16, v16[:, j, 0:n], j == 0, j == 3, [R_c, R_v16], [R_ps[2]])
                for j in range(4):
                    mm(ps[3][:, 0:n], o512_16, sq16[:, j, 0:n], j == 0, j == 3, [R_c, R_sq16], [R_ps[3]])
                cp("dve", mean32[:, 0:n], ps[2][:, 0:n], [R_ps[2]], [R_mean])
                tt("dve", var32[:, 0:n], mean32[:, 0:n], mean32[:, 0:n], ALU.mult, [R_mean], [R_var])
                tt("dve", var32[:, 0:n], ps[3][:, 0:n], var32[:, 0:n], ALU.subtract, [R_ps[3], R_var], [R_var])
                act(rstd32[:, 0:n], var32[:, 0:n], AF.Sqrt, [R_var], [R_rstd], bias=EPS)
                recip(rstd32[:, 0:n], rstd32[:, 0:n], [R_rstd], [R_rstd])
                for j in range(4):
                    b2 = j % 2
                    pi = 4 + b2
                    proj_fm(pi, wv, 1024 + j * 128, hb, n, R_w)
                    act(sga[b2][:, 0:n], ps[pi][:, 0:n], AF.Silu, [R_ps[pi]], [R_sg[b2]])
                    tt("dve", t1[b2][:, 0:n], v32[:, j, 0:n], mean32[:, 0:n], ALU.subtract,
                       [R_v32j[j], R_mean], [R_t1[b2]])
                    tt("dve", t1[b2][:, 0:n], t1[b2][:, 0:n], rstd32[:, 0:n], ALU.mult,
                       [R_t1[b2], R_rstd], [R_t1[b2]])
                    act(t2[b2][:, 0:n], t1[b2][:, 0:n], AF.Silu, [R_t1[b2], R_small], [R_t2[b2]],
                        scale=cvec[:, 4 + j:5 + j], bias=cvec[:, 8 + j:9 + j])
                    tt("pool" if j % 2 == 0 else "dve", z16[:, j, 0:n], t2[b2][:, 0:n], sga[b2][:, 0:n], ALU.mult,
                       [R_t2[b2], R_sg[b2]], [R_z16])
                for i in range(8):
                    pi = i % 2
                    for j in range(4):
                        mm(ps[pi][:, 0:n], wpa[:, j, i * 128:(i + 1) * 128], z16[:, j, 0:n], j == 0, j == 3,
                           [R_wp(i * 128), R_z16], [R_ps[pi]])
                    if i % 2 == 1:
                        cp("act", ya16[:, i, 0:n], ps[pi][:, 0:n], [R_ps[pi]], [R_ya])
                    else:
                        cp("dve", ya16[:, i, 0:n], ps[pi][:, 0:n], [R_ps[pi]], [R_ya])
                dma("sp", Y_d[0][:, :, t0:t0 + n], ya16[:, :, 0:n], [R_ya], [R_Y[0][bi]])

            for kind in (0, 1):
                for a in (BIG, WB, W32, W16):
                    a.reset()
                S.barrier()
                is_wa = kind == 1
                R_w = WReg()
                R_wp = WReg()
                if not is_wa:
                    ncol = 2048
                    wv = WB.take(8 * ncol).rearrange("p (k n) -> p k n", k=8)
                    load_w(wv, win_in[l][:, 1536:3584], R_w, order=[1, 2, 0, 3], split_first=True)
                    cq, ck, cv_, cg = 0, 512, 1024, 1536
                    nkc, nkv = 4, 8
                    gq, gk = qkn[:, 0:1], qkn[:, 1:2]
                    wp_src = wpb_in[l]
                else:
                    ncol = 1408
                    wv = WB.take(8 * ncol).rearrange("p (k n) -> p k n", k=8)
                    for g in range(2):
                        rg = R_w.add(512 + g * 128, 512 + (g + 1) * 128)
                        for dup in range(2):
                            c0 = 512 + g * 128 + dup * 64
                            load_w(wv[:, :, c0:c0 + 64], win_in[l][:, 4096 + g * 64:4096 + (g + 1) * 64], R_w, region=rg)
                    load_w(wv[:, :, 768:896], win_in[l][:, 4224:4352], R_w, base=768)
                    load_w(wv[:, :, 0:512], win_in[l][:, 3584:4096], R_w, base=0)
                    load_w(wv[:, :, 896:1408], win_in[l][:, 4352:4864], R_w, base=896)
                    cq, ck, cv_, cg = 0, 512, 768, 896
                    nkc, nkv = 2, 2
                    gq, gk = qkn[:, 2:3], qkn[:, 3:4]
                    wp_src = wpc_in[l]
                wpp = WB.take(4 * 1024).rearrange("p (k n) -> p k n", k=4)
                load_w(wpp, wp_src, R_wp)
                KT = BIG.take(nkc * S_ALL).rearrange("p (c n) -> p c n", c=nkc)
                VV_f = BIG.take(34 * nkv * 65)
                VV = VV_f.rearrange("p (t h d) -> p t h d", t=34, h=nkv)
                R_KT, R_VV = Region("KT"), Region("VV")
                mset("pool", VV_f, 1.0, [R_VV])
                R_tab = Region("tab")
                R_etb = Region("etb")
                prep_thunks = []
                sgt = [W32.take(512) for _ in range(4)]
                R_sgt = [Region("sgt%d" % i) for i in range(4)]
                if not is_wa:
                    tab16_f = W16.take(5120)
                    tab16 = tab16_f.rearrange("p (h n) -> p h n", h=8)
                    def mk_piece(ty, pc):
                        def f():
                            sb_i = pc % 4
                            dma("sp", sgt[sb_i], natab_in[l, ty][:, pc * 512:(pc + 1) * 512], [], [R_sgt[sb_i]])
                            act(tab16_f[:, pc * 512:(pc + 1) * 512], sgt[sb_i], AF.Exp, [R_sgt[sb_i]], [R_tab])
                            if pc == 9:
                                dma("sp", etab_d[ty], tab16_f, [R_tab], [R_etab[ty]])
                        return f
                    prep_thunks = [mk_piece(ty, pc) for ty in range(5) for pc in range(10)]
                sq16 = [W16.take(512) for _ in range(2)]
                R_sq = [Region("sq0"), Region("sq1")]
                sd32 = [W32.take(512) for _ in range(2)]
                R_sd = [Region("sd0"), Region("sd1")]
                rc32 = W32.take(512)
                rs32 = W32.take(512)
                R_rope = Region("rope")
                if is_wa:
                    kn16 = [W16.take(512) for _ in range(2)]
                    R_kn = [Region("kn0"), Region("kn1")]
                    ra32 = [W32.take(512) for _ in range(2)]
                    rb32 = [W32.take(512) for _ in range(2)]
                    R_ra = [Region("ra0"), Region("ra1")]
                    R_rb = [Region("rb0"), Region("rb1")]

                def skew(items):
                    ns = max(len(it) for it in items)
                    for step in range(len(items) + ns - 1):
                        for s_ in range(ns):
                            i_ = step - s_
                            if 0 <= i_ < len(items) and s_ < len(items[i_]):
                                items[i_][s_]()

                def normed_stages(idx, psi, wcol, hb, n, gain, out16, R_out, rope, post=None):
                    b = idx % 2
                    pst = 6

                    def s1():
                        proj_fm(psi, wv, wcol, hb, n, R_w)
                        act(sq16[b][:, 0:n], ps[psi][:, 0:n], AF.Square, [R_ps[psi]], [R_sq[b]])

                    def s2():
                        mm(ps[pst][:, 0:n], bones16, sq16[b][:, 0:n], True, True, [R_c, R_sq[b]], [R_ps[pst]])
                        act(sd32[b][:, 0:n], ps[pst][:, 0:n], AF.Sqrt, [R_ps[pst]], [R_sd[b]], bias=EPS)
                        recip(sd32[b][:, 0:n], sd32[b][:, 0:n], [R_sd[b]], [R_sd[b]])
                        if not rope:
                            stt("dve", out16, ps[psi][:, 0:n], gain, sd32[b][:, 0:n], ALU.mult, ALU.mult,
                                [R_ps[psi], R_small, R_sd[b]], [R_out])
                        else:
                            stt("dve", kn16[b][:, 0:n], ps[psi][:, 0:n], gain, sd32[b][:, 0:n], ALU.mult, ALU.mult,
                                [R_ps[psi], R_small, R_sd[b]], [R_kn[b]])

                    def s3():
                        mm(ps[pst][:, 0:n], pmat16, kn16[b][:, 0:n], True, True, [R_c, R_kn[b]], [R_ps[pst]])
                        tt("pool", ra32[b][:, 0:n], kn16[b][:, 0:n], rc32[:, 0:n], ALU.mult, [R_kn[b], R_rope], [R_ra[b]])
                        tt("dve", rb32[b][:, 0:n], ps[pst][:, 0:n], rs32[:, 0:n], ALU.mult, [R_ps[pst], R_rope], [R_rb[b]])
                        tt("pool", out16, ra32[b][:, 0:n], rb32[b][:, 0:n], ALU.add, [R_ra[b], R_rb[b]], [R_out])
                    st_ = [s1, s2, s3] if rope else [s1, s2]
                    if post is not None:
                        st_.append(post)
                    return st_

                load_hT(0, 0)
                all_items = []
                for bi in range(9):
                    t0, n = BLOCKS[bi]
                    hb = bi % 2
                    rope = is_wa and bi < 8

                    def ld(bi=bi, hb=hb, rope=rope, t0=t0, n=n):
                        if bi + 1 < 9:
                            load_hT(bi + 1, 1 - hb)
                        if rope:
                            dma("sp", rc32[:, 0:n], ropec_in[:, t0:t0 + n], [], [R_rope])
                            dma("sp", rs32[:, 0:n], ropes_in[:, t0:t0 + n], [], [R_rope])
                    all_items.append([ld])
                    for c in range(nkc):
                        all_items.append(normed_stages(c, c, ck + c * 128, hb, n, gk, KT[:, c, t0:t0 + n], R_KT, rope))
                    for t_ in range(n // 128):
                        def v1(t_=t_, hb=hb):
                            pi = 4 + t_ % 2
                            proj_tm(pi, wv, cv_, nkv * 64, hb, t_, R_w)

                        def v2(t_=t_, t0=t0):
                            pi = 4 + t_ % 2
                            gt = t0 // 128 + t_
                            src = ps[pi][:, 0:nkv * 64].rearrange("p (h d) -> p h d", h=nkv)
                            cp("act", VV[:, gt, :, 0:64], src, [R_ps[pi]], [R_VV])
                        all_items.append([v1, v2])
                    for _ in range(7):
                        if prep_thunks:
                            all_items.append([prep_thunks.pop(0)])
                while prep_thunks:
                    all_items.append([prep_thunks.pop(0)])
                skew(all_items)

                QT = W16.take(2048).rearrange("p (c n) -> p c n", c=4)
                R_QT = Region("QT")
                QTz = W16.take(4096).rearrange("p (c e n) -> p c e n", c=4, e=2)
                R_QTz = [Region("QTz%d" % c) for c in range(4)]
                R_QTc = [Region("QTc%d" % c) for c in range(4)]
                ogT = W16.take(2048).rearrange("p (c n) -> p c n", c=4)
                R_ogT = Region("ogT")
                og16 = W16.take(512)
                R_og16 = Region("og16")
                Pw = [W16.take(640) for _ in range(2)]
                R_Pw = [Region("Pw0"), Region("Pw1")]
                R_Pw5 = [Region("Pw5_0"), Region("Pw5_1")]
                PcE = [BIG.take(384) for _ in range(2)]
                R_Pc = [Region("Pc0"), Region("Pc1")]
                Ew = [BIG.take(512) for _ in range(2)]
                R_Ew = [Region("Ew0"), Region("Ew1")]
                og32 = W32.take(512)
                R_og32 = Region("og32")
                yb16 = WB.take(4096).rearrange("p (k n) -> p k n", k=8)
                R_yb = Region("yb")
                R_rec = Region("rec")
                cur_edge = [-1]
                deferred_tail = [None]
                load_hT(0, 0)
                for bi in range(nblk):
                    t0, n = BLOCKS[bi]
                    hb = bi % 2
                    if bi + 1 < nblk:
                        load_hT(bi + 1, 1 - hb)
                    rope = is_wa and bi < 8
                    if rope:
                        dma("sp", rc32[:, 0:n], ropec_in[:, t0:t0 + n], [], [R_rope])
                        dma("sp", rs32[:, 0:n], ropes_in[:, t0:t0 + n], [], [R_rope])
                    ntile = n // 128
                    items = []
                    for c in range(4):
                        def qz(c=c):
                            for e_ in range(2):
                                act(QTz[:, c, e_, 0:n], QT[:, c, 0:n], AF.Copy, [R_QTc[c], R_hm], [R_QTz[c]],
                                    scale=hmask[:, e_:e_ + 1])
                        items.append(normed_stages(c, c, cq + c * 128, hb, n, gq, QT[:, c, 0:n], R_QTc[c], rope, post=qz))
                    for t_ in range(ntile):
                        def g1(t_=t_):
                            proj_tm(4 + t_ % 2, wv, cg, 512, hb, t_, R_w)

                        def g2(t_=t_):
                            pi = 4 + t_ % 2
                            act(sgt[t_], ps[pi][:, 0:512], AF.Silu, [R_ps[pi]], [R_sgt[t_]])
                        items.append([g1, g2])
                    skew(items)
                    if deferred_tail[0] is not None:
                        deferred_tail[0]()
                        deferred_tail[0] = None

                    def scores(t_, h):
                        qt = t0 // 128 + t_
                        if bi == 8:
                            wch, tabv, R_tb = [], None, None
                        elif not is_wa:
                            ty, wch = na_chunks(qt)
                            if cur_edge[0] != ty:
                                dma("sp", tab16_f, etab_d[ty], [R_etab[ty]], [R_tab])
                                cur_edge[0] = ty
                            tabv, R_tb = tab16, R_tab
                        else:
                            wch = [c_ for c_ in (qt - 1, qt, qt + 1) if 0 <= c_ < 32]
                            moff = 128 if qt == 0 else 0
                            tabv, R_tb = None, R_c
                        cch = [32, 33]
                        nw = len(wch)
                        c2 = h // 2
                        pb = 64 * (h % 2)
                        kc = (h // 4) if is_wa else c2
                        hp = h % 2
                        pw_i, pc_i = (2, 4) if hp == 0 else (3, 0)
                        qv = QTz[:, c2, h % 2, t_ * 128:(t_ + 1) * 128]
                        for i_, kch in enumerate(cch):
                            mm(ps[pc_i][:, i_ * 128:(i_ + 1) * 128],
                               KT[:, kc, kch * 128:(kch + 1) * 128], qv, True, True,
                               [R_KT, R_QTz[c2]], [R_ps[pc_i]])
                        if nw == 5:
                            kch = wch[4]
                            mm(ps[pc_i][:, 256:384],
                               KT[:, kc, kch * 128:(kch + 1) * 128], qv, True, True,
                               [R_KT, R_QTz[c2]], [R_ps[pc_i]])
                        nce = 384 if nw == 5 else 256
                        act(PcE[hp][:, 0:nce], ps[pc_i][:, 0:nce], AF.Exp, [R_ps[pc_i]], [R_Pc[hp]])
                        for i_, kch in enumerate(wch[:4]):
                            mm(ps[pw_i][:, i_ * 128:(i_ + 1) * 128],
                               KT[:, kc, kch * 128:(kch + 1) * 128], qv, True, True,
                               [R_KT, R_QTz[c2]], [R_ps[pw_i]])
                        if nw:
                            n4 = min(nw, 4) * 128
                            if is_wa:
                                mk = wamask16[:, moff:moff + nw * 128]
                            else:
                                mk = tabv[:, h, 0:nw * 128]
                            if nw == 5:
                                tt("dve", Pw[hp][:, 512:640], PcE[hp][:, 256:384], mk[:, 512:640], ALU.mult,
                                   [R_Pc[hp], R_tb], [R_Pw5[hp]])
                            act(Ew[hp][:, 0:n4], ps[pw_i][:, 0:n4], AF.Exp, [R_ps[pw_i]], [R_Ew[hp]])
                            tt("dve", Pw[hp][:, 0:n4], Ew[hp][:, 0:n4], mk[:, 0:n4], ALU.mult,
                               [R_Ew[hp], R_tb], [R_Pw[hp]])
                        allch = [(PcE[hp][:, i_ * 128:(i_ + 1) * 128], kch, R_Pc[hp]) for i_, kch in enumerate(cch)]
                        if nw == 5:
                            allch.append((Pw[hp][:, 512:640], wch[4], R_Pw5[hp]))
                        allch += [(Pw[hp][:, i_ * 128:(i_ + 1) * 128], kch, R_Pw[hp]) for i_, kch in enumerate(wch[:4])]
                        return allch

                    def pv(t_, h, allch):
                        vh = (h // 4) if is_wa else h
                        opi = 5 if h < 4 else 1
                        ov = ps[opi][:, 0:260].rearrange("p (h d) -> p h d", h=4)[:, h % 4, :]
                        for i_, (pap, kch, rg) in enumerate(allch):
                            mm(ov, pap, VV[:, kch, vh, :], i_ == 0, i_ == len(allch) - 1, [rg, R_VV], [R_ps[opi]])

                    def epi_half(t_, g4):
                        opi = 5 if g4 == 0 else 1
                        o3 = ps[opi][:, 0:260].rearrange("p (h d) -> p h d", h=4)
                        rec = stat[:, 0:4]
                        if is_wa:
                            tt("dve", rec, o3[:, :, 64], esink[:, g4 * 4:(g4 + 1) * 4], ALU.add,
                               [R_ps[opi], R_small], [R_rec])
                            recip(rec, rec, [R_rec], [R_rec])
                        else:
                            recip(rec, o3[:, :, 64], [R_ps[opi]], [R_rec])
                        o32 = og32[:, 0:256].rearrange("p (h d) -> p h d", h=4)
                        tt("dve", o32, o3[:, :, 0:64], rec.unsqueeze(2).to_broadcast([128, 4, 64]), ALU.mult,
                           [R_ps[opi], R_rec], [R_og32])
                        tt("pool", og16[:, g4 * 256:(g4 + 1) * 256], og32[:, 0:256],
                           sgt[t_][:, g4 * 256:(g4 + 1) * 256], ALU.mult, [R_og32, R_sgt[t_]], [R_og16])

                    def transposes(t_):
                        pT4 = psT[:, 0:512].rearrange("p (c n) -> p c n", c=4)
                        for c in range(4):
                            tr(pT4[:, c, :], og16[:, c * 128:(c + 1) * 128], [R_og16], [R_psT])
                        cp("act", ogT[:, :, t_ * 128:(t_ + 1) * 128], pT4, [R_psT], [R_ogT])

                    units = [(t_, h) for t_ in range(ntile) for h in range(8)]
                    prev = None
                    pending_tr = []
                    for ui, (t_, h) in enumerate(units):
                        allch = scores(t_, h)
                        if prev is not None:
                            pt, ph, pch = prev
                            pv(pt, ph, pch)
                            if ph == 3:
                                epi_half(pt, 0)
                            if ph == 7:
                                epi_half(pt, 1)
                                pending_tr.append((ui + 2, pt))
                        while pending_tr and pending_tr[0][0] <= ui:
                            transposes(pending_tr.pop(0)[1])
                        prev = (t_, h, allch)
                    pt, ph, pch = prev
                    pv(pt, ph, pch)
                    epi_half(pt, 1)
                    pending_tr.append((0, pt))

                    def tail(pending_tr=pending_tr, transposes=transposes, n=n, t0=t0, bi=bi):
                        while pending_tr:
                            transposes(pending_tr.pop(0)[1])
                        for i in range(8):
                            pi = 2 + i % 2
                            for c in range(4):
                                mm(ps[pi][:, 0:n], wpp[:, c, i * 128:(i + 1) * 128], ogT[:, c, 0:n], c == 0, c == 3,
                                   [R_wp(i * 128), R_ogT], [R_ps[pi]])
                            cp("act" if i % 2 == 1 else "dve", yb16[:, i, 0:n], ps[pi][:, 0:n], [R_ps[pi]], [R_yb])
                        dma("sp", Y_d[1 + kind][:, :, t0:t0 + n], yb16[:, :, 0:n], [R_yb], [R_Y[1 + kind][bi]])
                    deferred_tail[0] = tail
                if deferred_tail[0] is not None:
                    deferred_tail[0]()
                    deferred_tail[0] = None

            for a in (BIG, WB, W32, W16):
                a.reset()
            S.barrier()
            gate_bc = W16.take(4096).bitcast(F32).rearrange("p (r n) -> p r n", r=2)
            dg32 = W32.take(128)
            R_dg = Region("dg")
            for r in range(2):
                for k in range(8):
                    ts("dve", dg32, ident32, modsb[:, 16 + k, r:r + 1], None, ALU.mult, None,
                       [R_mod2, R_c], [R_dg])
                    pi = 1 + (k // 4)
                    mm(ps[pi][:, (k % 4) * 128:(k % 4 + 1) * 128], ones32, dg32, True, True,
                       [R_c, R_dg], [R_ps[pi]])
                    if k % 4 == 3:
                        cp("act", gate_bc[:, r, (k // 4) * 512:(k // 4 + 1) * 512], ps[pi][:],
                           [R_ps[pi]], [R_gate])

            R_w = WReg()
            R_wo = WReg()
            wv = WB.take(8 * 3072).rearrange("p (k n) -> p k n", k=8)
            load_w(wv, win_in[l][:, 4864:7936], R_w, order=[0, 2, 4, 1, 3, 5], split_first=True)
            wo = BIG.take(8 * 1024).rearrange("p (k n) -> p k n", k=8)
            load_w(wo, wo_in[l], R_wo)
            Yb = [[BIG.take(4096).rearrange("p (k n) -> p k n", k=8) for _ in range(3)] for _ in range(2)]
            R_Yb = [[Region("Yb%d%d" % (i, j)) for j in range(3)] for i in range(2)]
            yT2 = [W16.take(4096).rearrange("p (k n) -> p k n", k=8) for _ in range(2)]
            R_yT2 = [Region("yT0"), Region("yT1")]
            sgm = [W32.take(512) for _ in range(2)]
            R_sgm = [Region("sgm0"), Region("sgm1")]
            acc = W32.take(512)
            R_acc = Region("acc")
            xt = [W32.take(1024) for _ in range(2)]
            R_xt = [Region("xt0"), Region("xt1")]
            xo = [W32.take(1024) for _ in range(2)]
            R_xo = [Region("xo0"), Region("xo1")]
            tmp2 = [W32.take(512) for _ in range(2)]
            R_tmp2 = [Region("tmp0"), Region("tmp1")]

            def ld_blk(bi):
                load_hT(bi, bi % 2)
                t0, n = BLOCKS[bi]
                for br in range(3):
                    dma("sp", Yb[bi % 2][br][:, :, 0:n], Y_d[br][:, :, t0:t0 + n], [R_Y[br][bi]], [R_Yb[bi % 2][br]])
            ld_blk(0)
            xcnt = 0
            for bi in range(nblk):
                t0, n = BLOCKS[bi]
                hb = bi % 2
                if bi + 1 < nblk:
                    ld_blk(bi + 1)
                r = 0 if bi < 8 else 1
                yT, R_yT = yT2[bi % 2], R_yT2[bi % 2]
                for i in range(8):
                    for br in range(3):
                        pi = (i * 3 + br) % 2
                        proj_fm(pi, wv, br * 1024 + i * 128, hb, n, R_w)
                        act(sgm[pi][:, 0:n], ps[pi][:, 0:n], AF.Sigmoid, [R_ps[pi]], [R_sgm[pi]])
                        if br == 0:
                            tt("dve", acc[:, 0:n], sgm[pi][:, 0:n], Yb[hb][br][:, i, 0:n], ALU.mult,
                               [R_sgm[pi], R_Yb[hb][br]], [R_acc])
                        else:
                            tt("pool", sgm[pi][:, 0:n], sgm[pi][:, 0:n], Yb[hb][br][:, i, 0:n], ALU.mult,
                               [R_sgm[pi], R_Yb[hb][br]], [R_sgm[pi]])
                            if br == 1:
                                tt("dve", acc[:, 0:n], acc[:, 0:n], sgm[pi][:, 0:n], ALU.add,
                                   [R_acc, R_sgm[pi]], [R_acc])
                            else:
                                tt("dve", yT[:, i, 0:n], acc[:, 0:n], sgm[pi][:, 0:n], ALU.add,
                                   [R_acc, R_sgm[pi]], [R_yT])
                for t_ in range(n // 128):
                    gt = t0 // 128 + t_
                    xb = xcnt % 2
                    xcnt += 1
                    src = xin[gt * 128:(gt + 1) * 128, :] if gt < 32 else cin[(gt - 32) * 128:(gt - 31) * 128, :]
                    dst = xout[gt * 128:(gt + 1) * 128, :] if gt < 32 else ctx1_d[(gt - 32) * 128:(gt - 31) * 128, :]
                    dma("sp", xt[xb], src, [R_x1[gt]] if l > 0 else [], [R_xt[xb]])
                    for hf in range(2):
                        pi = 2 + hf + 2 * (xb % 2)
                        tmp, R_tmp = tmp2[hf], R_tmp2[hf]
                        for i in range(8):
                            mm(ps[pi][:, :], yT[:, i, t_ * 128:(t_ + 1) * 128], wo[:, i, hf * 512:(hf + 1) * 512],
                               i == 0, i == 7, [R_yT, R_wo(hf * 512)], [R_ps[pi]])
                        tt("dve", tmp, ps[pi][:, :], gate_bc[:, r, hf * 512:(hf + 1) * 512], ALU.mult,
                           [R_ps[pi], R_gate], [R_tmp])
                        tt("pool", xo[xb][:, hf * 512:(hf + 1) * 512], tmp, xt[xb][:, hf * 512:(hf + 1) * 512],
                           ALU.add, [R_tmp, R_xt[xb]], [R_xo[xb]])
                    if last:
                        dma("sp", dst, xo[xb], [R_xo[xb]], [], is_output=True)
                    else:
                        dma("sp", dst, xo[xb], [R_xo[xb]], [R_x1[gt]])

        S.finish()
        S.emit_all()
    return nc


_CACHE = {}


def _fm(v, nchunk):
    sh = v.shape[:-1]
    return np.ascontiguousarray(np.swapaxes(v.reshape(*sh, nchunk, 128), -1, -2))


def prep_inputs(inp):
    global _NA_IDX
    if _NA_IDX is None:
        _NA_IDX = _na_index()
    if "cst" not in _CACHE:
        _CACHE["cst"] = _consts()
    cst, rc, rs = _CACHE["cst"]
    f = lambda a: np.ascontiguousarray(np.asarray(a, dtype=np.float32))
    x, c, ctx, c_ctx = f(inp["x"]), f(inp["c"]), f(inp["ctx"]), f(inp["c_ctx"])
    common = {
        "norm_gT": _fm(f(inp["norm_g"]), 8),
        "b_adaT": _fm(f(inp["b_ada"]), 24),
        "w_ada": f(inp["w_ada"]), "w_in": f(inp["w_in"]),
        "w_proj_a": f(inp["w_proj_a"]), "w_proj_b": f(inp["w_proj_b"]), "w_proj_c": f(inp["w_proj_c"]),
        "w_o": f(inp["w_o"]),
        "cst": cst, "rope_c": rc, "rope_s": rs,
    }
    cw = f(inp["conv_w"])
    cwT = np.transpose(cw.reshape(L, 31, 4, 128), (0, 3, 2, 1))
    common["conv_wT"] = np.ascontiguousarray(cwT.reshape(L, 128, 124))
    common["cvecT"] = np.ascontiguousarray(np.concatenate(
        [_fm(f(inp["conv_b"]), 4), _fm(f(inp["cln_g"]), 4), _fm(f(inp["cln_b"]), 4)], axis=-1))
    qk = np.stack([f(inp["na_q_norm"]), f(inp["na_k_norm"]), f(inp["wa_q_norm"]), f(inp["wa_k_norm"])], -1)
    common["qk_norm"] = np.ascontiguousarray(np.concatenate([qk, qk], axis=1))
    common["sink_bc"] = np.ascontiguousarray(np.broadcast_to(f(inp["wa_sink"])[:, None, :], (L, 128, 8)))
    rpb = f(inp["na_rpb"]).reshape(L, 8, 15 * 31)
    rpb_pad = np.concatenate([rpb, np.full((L, 8, 1), NEG, np.float32)], axis=-1)
    tab = rpb_pad[:, :, _NA_IDX]
    tab = np.transpose(tab, (0, 2, 3, 1, 4, 5))
    common["na_tab"] = np.ascontiguousarray(tab.reshape(L, 5, 128, 8 * 640))
    maps = []
    for b in range(8):
        m = dict(common)
        m["x"] = x[b]
        m["ctx"] = ctx[b]
        cc = np.stack([c[b], c_ctx], axis=-1)
        m["ccT"] = np.ascontiguousarray(np.transpose(cc.reshape(8, 128, 2), (1, 0, 2)))
        maps.append(m)
    return maps


def kernel(**inputs):
    if "nc" not in _CACHE:
        _CACHE["nc"] = build()
    maps = prep_inputs(inputs)
    res = run_bass_kernel_spmd(_CACHE["nc"], maps, core_ids=list(range(8)))
    return np.stack([np.asarray(r["out"], dtype=np.float32) for r in res.results], axis=0)
```
